# Optimizing a Trainium2 kernel written in Bass

```python
import jax, jax.numpy as jnp
from jax import lax
import numpy as np

D_MODEL = 2048
BATCH = 4
SEQ = 2048
DEPTH = 2

GRID_W = 64
CTX_LEN = 256
RET_HEADS = 8
RET_HEAD_DIM = 128
RET_W = RET_HEADS * RET_HEAD_DIM
K_SCALE = RET_HEAD_DIM ** -0.5
CHUNK = 128
ROPE_BASE = 10000.0
CONV_GROUPS = 8
CONV_W = D_MODEL - RET_W
CONV_WIDTH = 3
IN_W = 4 * RET_W + 3 * CONV_W
POOL_WINDOWS = (2, 4, 8, 16)
POOL_GROUPS = len(POOL_WINDOWS)
POOL_GROUP_W = D_MODEL // POOL_GROUPS
D_FF = 5632
MACARON = 0.5
N_MOD = 9
N_EVEN = (DEPTH + 1) // 2
N_ODD = DEPTH // 2
EPS = 1e-6
GN_EPS = 1e-5

kernel_name = "hybrid_retention_shortconv_pool_macaron_dit"


def rmsnorm(x, g):
    xf = x.astype(jnp.float32)
    n = xf * lax.rsqrt(jnp.mean(xf * xf, axis=-1, keepdims=True) + EPS)
    return (n * g.astype(jnp.float32)).astype(x.dtype)


def modulate(h, shift, scale):
    return h * (1.0 + scale) + shift


def adaln(cond, w_mod_l, b_mod_l):
    m = jax.nn.silu(cond) @ w_mod_l + b_mod_l
    return jnp.split(m[..., None, :], N_MOD, axis=-1)


def ffn_sublayer(x, g_norm, shift, scale, gate, w_gate, w_up, w_down):
    h = modulate(rmsnorm(x, g_norm), shift, scale)
    y = (jax.nn.silu(h @ w_gate) * (h @ w_up)) @ w_down
    return x + MACARON * gate * y


def to_heads(t):
    b, l, _ = t.shape
    return t.reshape(b, l, RET_HEADS, RET_HEAD_DIM).transpose(0, 2, 1, 3).astype(jnp.float32)


def axial_rope_tables(rows, cols):
    quarter = RET_HEAD_DIM // 4
    inv = ROPE_BASE ** (-jnp.arange(quarter, dtype=jnp.float32) / quarter)
    ang = jnp.concatenate([rows.astype(jnp.float32)[:, None] * inv,
                           cols.astype(jnp.float32)[:, None] * inv], axis=-1)
    return jnp.cos(ang), jnp.sin(ang)


def apply_rope(t, cos, sin):
    half = RET_HEAD_DIM // 2
    t1, t2 = t[..., :half], t[..., half:]
    return jnp.concatenate([t1 * cos - t2 * sin, t1 * sin + t2 * cos], axis=-1)


def retention_chunked(q, k, v, log_gamma, s0):
    b, h, l, dk = q.shape
    dv = v.shape[-1]
    n = l // CHUNK
    lg = log_gamma.astype(jnp.float32)
    qc = q.reshape(b, h, n, CHUNK, dk)
    kc = k.reshape(b, h, n, CHUNK, dk)
    vc = v.reshape(b, h, n, CHUNK, dv)
    idx = jnp.arange(CHUNK, dtype=jnp.float32)
    diff = idx[:, None] - idx[None, :]
    decay_intra = jnp.where(diff >= 0, jnp.exp(lg[:, None, None] * jnp.maximum(diff, 0.0)), 0.0)
    q_decay = jnp.exp(lg[:, None] * (idx + 1.0))
    k_decay = jnp.exp(lg[:, None] * (CHUNK - 1.0 - idx))
    chunk_decay = jnp.exp(lg * CHUNK)
    scores = jnp.einsum('bhncd,bhnmd->bhncm', qc, kc) * decay_intra[None, :, None]
    intra = jnp.einsum('bhncm,bhnmv->bhncv', scores, vc)
    kv_chunk = jnp.einsum('bhncd,bhncv->bhndv', kc * k_decay[None, :, None, :, None], vc)

    def step(s, kv_n):
        return s * chunk_decay[None, :, None, None] + kv_n, s

    _, s_before = lax.scan(step, s0.astype(jnp.float32), jnp.moveaxis(kv_chunk, 2, 0))
    cross = jnp.einsum('bhncd,nbhdv->bhncv', qc * q_decay[None, :, None, :, None], s_before)
    return (intra + cross).reshape(b, h, l, dv)


def retention_bidir(q, k, v, lg_fwd, lg_bwd, s_fwd, s_bwd):
    out_f = retention_chunked(q, k, v, lg_fwd, s_fwd)
    out_b = retention_chunked(jnp.flip(q, 2), jnp.flip(k, 2), jnp.flip(v, 2), lg_bwd, s_bwd)
    return out_f + jnp.flip(out_b, 2)


def context_states(kc, vc, lg_fwd, lg_bwd):
    lc = kc.shape[2]
    pos = jnp.arange(lc, dtype=jnp.float32)
    w_f = jnp.exp(lg_fwd.astype(jnp.float32)[:, None] * (lc - 1.0 - pos))
    w_b = jnp.exp(lg_bwd.astype(jnp.float32)[:, None] * pos)
    s_f = jnp.einsum('hl,bhlk,bhlv->bhkv', w_f, kc, vc)
    s_b = jnp.einsum('hl,bhlk,bhlv->bhkv', w_b, kc, vc)
    return s_f, s_b


def group_norm_heads(o):
    mu = jnp.mean(o, axis=-1, keepdims=True)
    var = jnp.mean(jnp.square(o - mu), axis=-1, keepdims=True)
    return (o - mu) * lax.rsqrt(var + GN_EPS)


def conv3_centred(u, w):
    up = jnp.pad(u, ((0, 0), (1, 1), (0, 0)))
    return up[:, :-2] * w[0] + up[:, 1:-1] * w[1] + up[:, 2:] * w[2]


def even_mixer(p, w_conv, w_out, lg_fwd, lg_bwd, s_fwd, s_bwd, rope):
    b, l, _ = p.shape
    q, k, v, g, bg, cg, u = jnp.split(
        p, [RET_W, 2 * RET_W, 3 * RET_W, 4 * RET_W, 4 * RET_W + CONV_W, 4 * RET_W + 2 * CONV_W], axis=-1)
    qh, kh, vh = to_heads(q), to_heads(k) * K_SCALE, to_heads(v)
    if rope is not None:
        qh, kh = apply_rope(qh, *rope), apply_rope(kh, *rope)
    o = group_norm_heads(retention_bidir(qh, kh, vh, lg_fwd, lg_bwd, s_fwd, s_bwd))
    ret = o.transpose(0, 2, 1, 3).reshape(b, l, RET_W).astype(p.dtype) * jax.nn.silu(g)
    conv = bg * conv3_centred(cg * u, w_conv)
    return jnp.concatenate([ret, conv], axis=-1) @ w_out


def centred_window_mean(u, w):
    l = u.shape[1]
    cs = jnp.cumsum(u.astype(jnp.float32), axis=1)
    cs = jnp.concatenate([jnp.zeros_like(cs[:, :1]), cs], axis=1)
    t = jnp.arange(l)
    lo = jnp.clip(t - w // 2, 0, l)
    hi = jnp.clip(t + (w - w // 2), 0, l)
    s = jnp.take(cs, hi, axis=1) - jnp.take(cs, lo, axis=1)
    cnt = (hi - lo).astype(jnp.float32)[None, :, None]
    return (s / cnt).astype(u.dtype)


def pool_mixer(h, w_groups, scale):
    b, l, d = h.shape
    hg = h.reshape(b, l, POOL_GROUPS, POOL_GROUP_W)
    pooled = jnp.stack([centred_window_mean(hg[:, :, i], w) - hg[:, :, i]
                        for i, w in enumerate(POOL_WINDOWS)], axis=2)
    y = jnp.einsum('blgc,gcd->blgd', pooled, w_groups).reshape(b, l, d)
    return y * scale


def setup_inputs(seed: int = 0) -> dict:
    key = jax.random.key(seed)
    ks = jax.random.split(key, 24)
    f32 = jnp.float32

    def nrm(k, shape, fan_in, scale=1.0):
        return jax.random.normal(k, shape, f32) * (scale * fan_in ** -0.5)

    gamma0 = 1.0 - 2.0 ** (-5.0 - np.arange(RET_HEADS, dtype=np.float32))
    decay_logit0 = jnp.asarray(np.log(gamma0 / (1.0 - gamma0)).astype(np.float32))
    return {
        "x": jax.random.normal(ks[0], (BATCH, SEQ, D_MODEL), f32),
        "c": jax.random.normal(ks[1], (BATCH, D_MODEL), f32),
        "ctx": jax.random.normal(ks[2], (BATCH, CTX_LEN, D_MODEL), f32),
        "c_ctx": jax.random.normal(ks[3], (D_MODEL,), f32),
        "w_mod": nrm(ks[4], (DEPTH, D_MODEL, N_MOD * D_MODEL), D_MODEL, 0.5),
        "b_mod": 0.02 * jax.random.normal(ks[5], (DEPTH, N_MOD * D_MODEL), f32),
        "norm_ffn1": 1.0 + 0.05 * jax.random.normal(ks[6], (DEPTH, D_MODEL), f32),
        "norm_mix": 1.0 + 0.05 * jax.random.normal(ks[7], (DEPTH, D_MODEL), f32),
        "norm_ffn2": 1.0 + 0.05 * jax.random.normal(ks[8], (DEPTH, D_MODEL), f32),
        "ffn1_w_gate": nrm(ks[9], (DEPTH, D_MODEL, D_FF), D_MODEL),
        "ffn1_w_up": nrm(ks[10], (DEPTH, D_MODEL, D_FF), D_MODEL),
        "ffn1_w_down": nrm(ks[11], (DEPTH, D_FF, D_MODEL), D_FF),
        "ffn2_w_gate": nrm(ks[12], (DEPTH, D_MODEL, D_FF), D_MODEL),
        "ffn2_w_up": nrm(ks[13], (DEPTH, D_MODEL, D_FF), D_MODEL),
        "ffn2_w_down": nrm(ks[14], (DEPTH, D_FF, D_MODEL), D_FF),
        "mix_w_in": nrm(ks[15], (N_EVEN, D_MODEL, IN_W), D_MODEL),
        "mix_w_conv": nrm(ks[16], (N_EVEN, CONV_WIDTH, CONV_W), CONV_WIDTH),
        "mix_w_out": nrm(ks[17], (N_EVEN, D_MODEL, D_MODEL), D_MODEL),
        "ret_decay_fwd": decay_logit0 + 0.1 * jax.random.normal(ks[18], (N_EVEN, RET_HEADS), f32),
        "ret_decay_bwd": decay_logit0 + 0.1 * jax.random.normal(ks[19], (N_EVEN, RET_HEADS), f32),
        "pool_w": nrm(ks[20], (N_ODD, POOL_GROUPS, POOL_GROUP_W, POOL_GROUP_W), POOL_GROUP_W),
        "pool_scale": 1.0 + 0.1 * jax.random.normal(ks[21], (N_ODD, D_MODEL), f32),
        "final_norm": 1.0 + 0.05 * jax.random.normal(ks[22], (D_MODEL,), f32),
    }


def reference(x, c, ctx, c_ctx, w_mod, b_mod, norm_ffn1, norm_mix, norm_ffn2,
              ffn1_w_gate, ffn1_w_up, ffn1_w_down, ffn2_w_gate, ffn2_w_up, ffn2_w_down,
              mix_w_in, mix_w_conv, mix_w_out, ret_decay_fwd, ret_decay_bwd,
              pool_w, pool_scale, final_norm):
    b, l, _ = x.shape
    ROWS = l // GRID_W
    rows = jnp.repeat(jnp.arange(ROWS), GRID_W)
    cols = jnp.tile(jnp.arange(GRID_W), ROWS)
    rope = axial_rope_tables(rows, cols)
    last_even = ((DEPTH - 1) // 2) * 2
    xc = ctx
    for li in range(DEPTH):
        ctx_needed = li <= last_even
        ctx_full = li < last_even
        m = adaln(c, w_mod[li], b_mod[li])
        x = ffn_sublayer(x, norm_ffn1[li], m[0], m[1], m[2], ffn1_w_gate[li], ffn1_w_up[li], ffn1_w_down[li])
        if ctx_needed:
            mc = adaln(c_ctx, w_mod[li], b_mod[li])
            xc = ffn_sublayer(xc, norm_ffn1[li], mc[0], mc[1], mc[2],
                              ffn1_w_gate[li], ffn1_w_up[li], ffn1_w_down[li])
        h = modulate(rmsnorm(x, norm_mix[li]), m[3], m[4])
        if li % 2 == 0:
            e = li // 2
            lg_f = jax.nn.log_sigmoid(ret_decay_fwd[e])
            lg_b = jax.nn.log_sigmoid(ret_decay_bwd[e])
            hc = modulate(rmsnorm(xc, norm_mix[li]), mc[3], mc[4])
            if ctx_full:
                pc = hc @ mix_w_in[e]
                kc, vc = pc[..., RET_W:2 * RET_W], pc[..., 2 * RET_W:3 * RET_W]
                zeros = jnp.zeros((b, RET_HEADS, RET_HEAD_DIM, RET_HEAD_DIM), jnp.float32)
                yc = even_mixer(pc, mix_w_conv[e], mix_w_out[e], lg_f, lg_b, zeros, zeros, None)
            else:
                kc = hc @ mix_w_in[e][:, RET_W:2 * RET_W]
                vc = hc @ mix_w_in[e][:, 2 * RET_W:3 * RET_W]
            s_f, s_b = context_states(to_heads(kc) * K_SCALE, to_heads(vc), lg_f, lg_b)
            y = even_mixer(h @ mix_w_in[e], mix_w_conv[e], mix_w_out[e], lg_f, lg_b, s_f, s_b, rope)
        else:
            o = li // 2
            y = pool_mixer(h, pool_w[o], pool_scale[o])
            if ctx_full:
                hc = modulate(rmsnorm(xc, norm_mix[li]), mc[3], mc[4])
                yc = pool_mixer(hc, pool_w[o], pool_scale[o])
        x = x + m[5] * y.astype(x.dtype)
        x = ffn_sublayer(x, norm_ffn2[li], m[6], m[7], m[8], ffn2_w_gate[li], ffn2_w_up[li], ffn2_w_down[li])
        if ctx_full:
            xc = xc + mc[5] * yc.astype(xc.dtype)
            xc = ffn_sublayer(xc, norm_ffn2[li], mc[6], mc[7], mc[8],
                              ffn2_w_gate[li], ffn2_w_up[li], ffn2_w_down[li])
    return rmsnorm(x, final_norm)
```

```python
import numpy as np
import ml_dtypes
import concourse.bass as bass
import concourse.mybir as mybir
from concourse.bass_utils import run_bass_kernel_spmd

F32 = mybir.dt.float32
BF16 = mybir.dt.bfloat16
AF = mybir.ActivationFunctionType
ALU = mybir.AluOpType

D = 2048
KC = 16
FF = 5632
FC = 44
T = 1024
NEXT = 130
NH = 8
NSLOT = 6
NDMASEM = 12
EPS = 1e-6
GN_EPS = 1e-5
K_SCALE = 128 ** -0.5
SAME_ENGINE_SYNC = True


class Prog:
    ENG = ['pe', 'act', 'dve', 'pool', 'sp']

    def __init__(self):
        self.ops = {e: [] for e in self.ENG}
        self.count = {}
        self.waited = {e: {} for e in self.ENG}
        self.reg = {}
        self.dma_rr = 0
        self.final = []

    def _need(self, eng, tok):
        if tok is None:
            return
        k, v = tok
        if k == 'tl_' + eng and (eng == 'pe' or not SAME_ENGINE_SYNC):
            return
        if self.waited[eng].get(k, 0) < v:
            self.ops[eng].append(('wait', k, v))
            self.waited[eng][k] = v

    def _deps(self, eng, reads, writes):
        for k in reads:
            r = self.reg.get(k)
            if r is not None:
                self._need(eng, r[0])
        for k in writes:
            r = self.reg.get(k)
            if r is not None:
                self._need(eng, r[0])
                for t in r[1]:
                    self._need(eng, t)

    def _update(self, tok, reads, writes):
        for k in reads:
            r = self.reg.setdefault(k, [None, []])
            r[1].append(tok)
            if len(r[1]) > 64:
                best = {}
                for (kk, vv) in r[1]:
                    if best.get(kk, 0) < vv:
                        best[kk] = vv
                r[1] = list(best.items())
        for k in writes:
            self.reg[k] = [tok, []]

    @staticmethod
    def _is_psum(k):
        return k == 'psb' or (isinstance(k, tuple) and k[0] in ('ps', 'psb'))

    def op(self, eng, fn, reads=(), writes=()):
        ex = [k for k in reads if self._is_psum(k)]
        if ex:
            reads = [k for k in reads if not self._is_psum(k)]
            writes = list(writes) + [('psb', 0) if (k == 'psb' or k[0] == 'psb') else k for k in ex]
        writes = [('psb', 0) if (k == 'psb' or (isinstance(k, tuple) and k[0] == 'psb')) else k for k in writes]
        self._deps(eng, reads, writes)
        semk = 'tl_' + eng
        v = self.count.get(semk, 0) + 1
        self.count[semk] = v
        self.ops[eng].append(('op', fn, semk))
        tok = (semk, v)
        self._update(tok, reads, writes)
        return tok

    def dma(self, q, out, in_, reads=(), writes=()):
        self._deps(q, reads, writes)
        semk = 'dma%d' % self.dma_rr
        self.dma_rr = (self.dma_rr + 1) % NDMASEM
        prev = self.count.get(semk, 0)
        if prev:
            self._need(q, (semk, prev))
        v = prev + 16
        self.count[semk] = v
        self.ops[q].append(('dma', out, in_, semk))
        tok = (semk, v)
        self._update(tok, reads, writes)
        return tok

    def cc(self, ins, outs, groups, reads=(), writes=()):
        q = 'pool'
        self._deps(q, reads, writes)
        semk = 'ccsem'
        v = self.count.get(semk, 0) + 1
        self.count[semk] = v
        self.ops[q].append(('cc', ins, outs, groups, semk))
        tok = (semk, v)
        self._update(tok, reads, writes)
        return tok

    def finish(self, q, tok):
        self._need(q, tok)

    def emit(self, nc):
        import contextlib
        semnames = sorted(self.count.keys())
        with contextlib.ExitStack() as st:
            sems = {k: st.enter_context(nc.semaphore(k)) for k in semnames}
            block = st.enter_context(nc.Block())
            ops = self.ops

            def run(eng, e):
                for o in ops[eng]:
                    if o[0] == 'wait':
                        e.wait_ge(sems[o[1]], o[2])
                    elif o[0] == 'op':
                        f = o[1]
                        if isinstance(f, tuple):
                            ins = getattr(e, f[0])(**f[1])
                        else:
                            ins = f(e)
                        ins.then_inc(sems[o[2]], 1)
                    elif o[0] == 'dma':
                        src = o[2]() if callable(o[2]) else o[2]
                        e.dma_start(out=o[1], in_=src).then_inc(sems[o[3]], 16)
                    elif o[0] == 'cc':
                        e.collective_compute("AllGather", ALU.bypass, replica_groups=o[3],
                                             ins=o[1], outs=o[2]).then_inc(sems[o[4]])

            @block.tensor
            def _(e):
                run('pe', e)

            @block.scalar
            def _(e):
                run('act', e)

            @block.vector
            def _(e):
                run('dve', e)

            @block.gpsimd
            def _(e):
                run('pool', e)

            @block.sync
            def _(e):
                run('sp', e)


class Builder:
    def __init__(self, phases, fused):
        self.phases = phases
        self.fused = fused
        self.P = Prog()
        self.wplan = []
        self.nc = bass.Bass("TRN2", target_bir_lowering=False)
        self.ps_rr = 0
        self.rr = {}

    def rot(self, name, n):
        i = self.rr.get(name, 0)
        self.rr[name] = (i + 1) % n
        return i

    def psum(self):
        i = self.ps_rr
        self.ps_rr = (self.ps_rr + 1) % 6
        return i

    def wtile(self, desc):
        i = len(self.wplan)
        self.wplan.append(desc)
        s = i % NSLOT
        self.P.dma('pool', self.ring[:, s, :], (lambda i=i: self.wts[i, :, :]), reads=[], writes=[('ring', s)])
        return s

    def I(self, eng, name, reads=(), writes=(), **kw):
        return self.P.op(eng, (name, kw), reads=reads, writes=writes)

    def mm_group(self, ps_ap, pairs, reads, writes, transpose=False):
        n = len(pairs)

        def fn(e):
            ins = None
            for j, (l, r) in enumerate(pairs):
                ins = e.matmul(ps_ap, l, r, start=(j == 0), stop=(j == n - 1))
            return ins
        return self.P.op('pe', fn, reads=reads, writes=writes)

    def build(self):
        nc = self.nc
        P = self.P
        ph = self.phases
        import contextlib
        st = contextlib.ExitStack()
        with st:
            dt = nc.dram_tensor
            self.xin = dt("xin", [128, KC, T], F32, kind="ExternalInput").ap()
            self.xein = dt("xein", [128, KC, NEXT], F32, kind="ExternalInput").ap()
            self.vecs_d = dt("vecs", [128, NV], F32, kind="ExternalInput").ap()
            self.consts_d = dt("consts", [128, NCONST], F32, kind="ExternalInput").ap()
            self.rope_d = dt("rope", [128, 2, T], F32, kind="ExternalInput").ap()
            last = ph[-1]
            if last == 'C':
                self.out_d = dt("outT", [128, KC, T], F32, kind="ExternalOutput").ap()
            else:
                self.xout = dt("xout", [128, KC, T], F32, kind="ExternalOutput").ap()
                self.xeout = dt("xeout", [128, KC, NEXT], F32, kind="ExternalOutput").ap()
            if self.fused:
                self.pay1 = dt("pay1", [4 * NH * 128, 128], F32)
                self.pay1g = dt("pay1g", [2 * 4 * NH * 128, 128], F32)
                self.pay2 = dt("pay2", [128, KC * 16], F32)
                self.pay2g = dt("pay2g", [2 * 128, KC * 16], F32)
                self.pay1_w = self.pay1.ap() if hasattr(self.pay1, 'ap') else self.pay1
            else:
                if 'A' in ph:
                    self.pay1 = dt("pay1", [4 * NH * 128, 128], F32, kind="ExternalOutput")
                if 'B' in ph:
                    self.pay1g = dt("pay1g", [2 * 4 * NH * 128, 128], F32, kind="ExternalInput")
                    self.pay2 = dt("pay2", [128, KC * 16], F32, kind="ExternalOutput")
                if 'C' in ph:
                    self.pay2g = dt("pay2g", [2 * 128, KC * 16], F32, kind="ExternalInput")

            sb = lambda name, shape, dtype: st.enter_context(nc.sbuf_tensor(name, shape, dtype))
            self.x = sb("x", [128, KC, T], F32)
            self.xe = sb("xe", [128, KC, NEXT], F32)
            self.h = sb("h", [128, KC, T + NEXT], BF16)
            self.ring = sb("ring", [128, NSLOT, 2048], BF16)
            self.abuf = sb("abuf", [128, 2, 4, T + NEXT], BF16)
            self.vecs = sb("vecs_s", [128, NV], F32)
            self.consts = sb("consts_s", [128, NCONST], F32)
            self.wide = sb("wide", [128, 3, T + 16], F32)
            self.modraw = sb("modraw", [128, 2, 144], F32)
            self.tabA = sb("tabA", [128, 2, KC], F32)
            self.tabG = sb("tabG", [128, 2, KC], F32)
            self.sc = sb("sc", [128, KC, 2], BF16)
            self.scf = sb("scf", [128, KC, 2], F32)
            self.ones = sb("ones", [128, 128], BF16)
            self.onesf = sb("onesf", [128, 128], F32)
            self.sq = sb("sq", [128, 2, 512], BF16)
            self.f32t = sb("f32t", [128, 4, 514], F32)
            self.rstd = sb("rstd", [128, 512], F32)
            self.sg = sb("sg", [128, 2, 514], F32)
            ps = lambda name, shape, dtype: st.enter_context(nc.psum_tensor(name, shape, dtype))
            self.ps = [ps("ps%d" % i, [128, 512], F32) for i in range(7)]
            self.psb = ps("psb", [128, 1024], BF16)
            self.mix_alloc(sb)

            P.dma('sp', self.vecs[:, :], self.vecs_d[:, :], writes=['vecs'])
            P.dma('sp', self.consts[:, :], self.consts_d[:, :], writes=['consts'])
            if 'A' in ph or 'B' in ph:
                P.dma('sp', self.wide[:, 0:2, 0:T], self.rope_d[:, :, :], writes=[('wide', 0), ('wide', 1)])
            for k in range(KC):
                P.dma('sp', self.x[:, k, :], self.xin[:, k, :], writes=[('x', k, 0), ('x', k, 1)])
            P.dma('sp', self.xe[:, :, :], self.xein[:, :, :], writes=[('xe', k) for k in range(KC)])
            P.op('dve', lambda e: e.memset(self.ones[:, :], 1.0), writes=['ones'])
            P.op('dve', lambda e: e.memset(self.onesf[:, :], 1.0 / 128.0), writes=['onesf'])
            cc = self.vecs[:, V_CC:V_CC + 32]
            self.I('act', 'activation', reads=['vecs'], writes=['scf'],
                   out=self.scf[:, :, :].rearrange("p k e -> p (k e)"), in_=cc, func=AF.Silu)
            self.I('dve', 'tensor_copy', reads=['scf'], writes=['sc'], out=self.sc[:, :, :], in_=self.scf[:, :, :])

            self.main_tiles = [dict(kind='x', c0=0, n=512, half=0), dict(kind='x', c0=512, n=512, half=1)]
            self.ext_tile = dict(kind='xe', c0=0, n=NEXT)

            fused_all = (ph == ['A', 'B', 'C'])
            if 'A' in ph:
                self.adaln(0, 0, 3)
                self.adaln_begin(0, 3, 9 if fused_all else 6)
                self.ffn(0, 1, self.main_tiles + [self.ext_tile], ext_ctx=True, side=True)
                self.norm_mod(0, 'mix', self.main_tiles + [self.ext_tile], ext_ctx=True)
                self.mixer_states()
            if 'A' in ph and 'B' in ph:
                if self.fused:
                    P.cc([self.pay1[:, :]], [self.pay1g[:, :]], PAIRS, reads=['pay1'], writes=['pay1g'])
            if 'B' in ph:
                if 'A' not in ph:
                    self.adaln(0, 3, 6)
                    self.norm_mod(0, 'mix', self.main_tiles + [self.ext_tile], ext_ctx=True)
                self.mixer_main()
                if not fused_all:
                    self.adaln(0, 6, 9)
                self.adaln_begin(1, 0, 3)
                self.ffn(0, 2, self.main_tiles, side=True)
                self.adaln_begin(1, 3, 9 if fused_all else 3)
                if not fused_all:
                    self.aj = None
                self.ffn(1, 1, self.main_tiles, side=fused_all)
                self.halo_out()
            if 'B' in ph and 'C' in ph:
                if self.fused:
                    P.cc([self.pay2[:, :]], [self.pay2g[:, :]], PAIRS, reads=['pay2'], writes=['pay2g'])
            if 'C' in ph:
                if not fused_all:
                    self.adaln(1, 3, 6)
                self.pool_mixer()
                if not fused_all:
                    self.adaln(1, 6, 9)
                self.ffn(1, 2, self.main_tiles)
                self.final_norm()
            else:
                toks = []
                for k in range(KC):
                    toks.append(P.dma('sp', self.xout[:, k, :], self.x[:, k, :], reads=[('x', k, 0), ('x', k, 1)]))
                toks.append(P.dma('sp', self.xeout[:, :, :], self.xe[:, :, :], reads=[('xe', k) for k in range(KC)]))
                if 'A' in ph and not self.fused:
                    r = self.P.reg.get('pay1')
                    if r is not None and r[0] is not None:
                        toks.append(r[0])
                if 'B' in ph and not self.fused:
                    r = self.P.reg.get('pay2')
                    if r is not None and r[0] is not None:
                        toks.append(r[0])
                for t in toks:
                    P.finish('sp', t)
            self.wts = dt("wts", [len(self.wplan), 128, 2048], F32, kind="ExternalInput").ap()
            P.emit(nc)
        return nc

    def ntiles_placeholder(self):
        return self.ntiles

    def xap(self, tile, k, c0=None, c1=None):
        if c0 is None:
            c0, c1 = 0, tile['n']
        if tile['kind'] == 'x':
            return self.x[:, k, tile['c0'] + c0: tile['c0'] + c1]
        return self.xe[:, k, tile['c0'] + c0: tile['c0'] + c1]

    def xkey(self, tile, k):
        if tile['kind'] == 'x':
            return ('x', k, tile['half'])
        return ('xe', k)

    def hcol(self, tile):
        return tile['c0'] if tile['kind'] == 'x' else T + tile['c0']

    def hkey(self, tile, k):
        if tile['kind'] == 'x':
            return ('h', k, tile['half'])
        return ('he', k)

    def segs(self, tile, ext_ctx):
        if tile['kind'] == 'xe' and ext_ctx:
            return [(0, 2, 0), (2, NEXT, 1)]
        return [(0, tile['n'], 0)]

    def adaln(self, li, q0, q1):
        self.adaln_begin(li, q0, q1)
        self.adaln_work(10 ** 9)

    def adaln_begin(self, li, q0, q1):
        assert getattr(self, 'aj', None) is None
        self.aj = dict(li=li, q0=q0, q1=q1, jj=0, nj=(q1 - q0) * 16)

    def adaln_pending(self):
        aj = getattr(self, 'aj', None)
        return 0 if aj is None else aj['nj'] - aj['jj']

    def adaln_work(self, ntiles):
        aj = getattr(self, 'aj', None)
        if aj is None:
            return
        li, q0, q1 = aj['li'], aj['q0'], aj['q1']
        pb = 6
        psv = self.ps[pb]
        while ntiles > 0 and aj['jj'] < aj['nj']:
            jj = aj['jj']
            j = q0 * 16 + jj
            s = self.wtile(('colblk', 'w_mod', li, j))
            pairs = [(self.ring[:, s, kc * 128:(kc + 1) * 128], self.sc[:, kc, :]) for kc in range(KC)]
            self.mm_group(psv[:, 2 * jj:2 * jj + 2], pairs, reads=[('ring', s), 'sc'], writes=[('ps', pb)])
            aj['jj'] += 1
            ntiles -= 1
        if aj['jj'] >= aj['nj']:
            nj = aj['nj']
            pview = psv[:, 0:2 * nj].rearrange("p (j e) -> p j e", e=2)
            bm = self.vecs[:, V_BMOD + li * 144 + q0 * 16: V_BMOD + li * 144 + q1 * 16]
            for e_ in range(2):
                self.I('dve', 'tensor_tensor', reads=[('ps', pb), 'vecs'], writes=[('modraw', q) for q in range(q0, q1)],
                       out=self.modraw[:, e_, q0 * 16:q1 * 16], in0=pview[:, :, e_], in1=bm, op=ALU.add)
            self.aj = None

    def mod_tables(self, li, sub, gain_col, gate_mul, extra_gate_col=None):
        q0 = 3 * sub
        mk = [('modraw', q0), ('modraw', q0 + 1), ('modraw', q0 + 2)]
        for e_ in range(2):
            self.I('dve', 'scalar_tensor_tensor', reads=mk + ['vecs'], writes=['tabA'],
                   out=self.tabA[:, e_, :], in0=self.modraw[:, e_, (q0 + 1) * 16:(q0 + 2) * 16], scalar=1.0,
                   in1=self.vecs[:, gain_col:gain_col + 16], op0=ALU.add, op1=ALU.mult)
            if extra_gate_col is None:
                self.I('dve', 'tensor_scalar', reads=mk, writes=['tabG'],
                       out=self.tabG[:, e_, :], in0=self.modraw[:, e_, (q0 + 2) * 16:(q0 + 3) * 16],
                       scalar1=float(gate_mul), scalar2=None, op0=ALU.mult)
            else:
                self.I('dve', 'tensor_tensor', reads=mk + ['vecs'], writes=['tabG'],
                       out=self.tabG[:, e_, :], in0=self.modraw[:, e_, (q0 + 2) * 16:(q0 + 3) * 16],
                       in1=self.vecs[:, extra_gate_col:extra_gate_col + 16], op=ALU.mult)

    def norm_mod(self, li, which, tiles, ext_ctx=False):
        sub = {'ffn1': 0, 'mix': 1, 'ffn2': 2}[which]
        gain_col = {'ffn1': V_NF1, 'mix': V_NMIX, 'ffn2': V_NF2}[which] + li * 16
        gate_mul = 1.0 if which == 'mix' else 0.5
        extra = (V_PSCALE if (which == 'mix' and li == 1) else None)
        self.mod_tables(li, sub, gain_col, gate_mul, extra)
        q0 = 3 * sub
        for tile in tiles:
            n = tile['n']
            pb = self.psum()
            psv = self.ps[pb][:, 0:n]
            for k in range(KC):
                b = self.rot('sq', 2)
                self.I('act', 'activation', reads=[self.xkey(tile, k)], writes=[('sq', b)],
                       out=self.sq[:, b, 0:n], in_=self.xap(tile, k), func=AF.Square)
                self.I('pe', 'matmul', reads=[('sq', b), 'ones'], writes=[('ps', pb)],
                       out=psv, lhsT=self.ones[:, :], rhs=self.sq[:, b, 0:n], start=(k == 0), stop=(k == KC - 1))
            self.I('act', 'activation', reads=[('ps', pb), 'consts'], writes=['rstd'],
                   out=self.rstd[:, 0:n], in_=psv, func=AF.Sqrt, bias=self.consts[:, C_EPS:C_EPS + 1], scale=1.0 / D)
            self.I('dve', 'reciprocal', reads=['rstd'], writes=['rstd'], out=self.rstd[:, 0:n], in_=self.rstd[:, 0:n])
            hc0 = self.hcol(tile)
            for k in range(KC):
                b = self.rot('f32t', 4)
                self.I('dve', 'tensor_tensor', reads=[self.xkey(tile, k), 'rstd'], writes=[('f32t', b)],
                       out=self.f32t[:, b, 0:n], in0=self.xap(tile, k), in1=self.rstd[:, 0:n], op=ALU.mult)
                for (c0, c1, e_) in self.segs(tile, ext_ctx):
                    self.I('act', 'activation', reads=[('f32t', b), 'tabA', ('modraw', q0)], writes=[self.hkey(tile, k)],
                           out=self.h[:, k, hc0 + c0:hc0 + c1], in_=self.f32t[:, b, c0:c1], func=AF.Identity,
                           bias=self.modraw[:, e_, q0 * 16 + k:q0 * 16 + k + 1], scale=self.tabA[:, e_, k:k + 1])

    def akey(self, ab, c, tile):
        return ('a', ab, c, tile['kind'], tile.get('half', 0))

    def ffn(self, li, idx, tiles, ext_ctx=False, side=False):
        which = 'ffn1' if idx == 1 else 'ffn2'
        self.norm_mod(li, which, tiles, ext_ctx)
        wg, wu, wd = {1: ('ffn1_w_gate', 'ffn1_w_up', 'ffn1_w_down'), 2: ('ffn2_w_gate', 'ffn2_w_up', 'ffn2_w_down')}[idx]
        NG = FC // 4

        def gate_up(g):
            ab = g % 2
            for c in range(4):
                fcn = g * 4 + c
                sg_ = self.wtile(('colblk', wg, li, fcn))
                su_ = self.wtile(('colblk', wu, li, fcn))
                for tile in tiles:
                    n = tile['n']
                    hc0 = self.hcol(tile)
                    pg = self.psum()
                    pu = self.psum()
                    hk = [self.hkey(tile, k) for k in range(KC)]
                    self.mm_group(self.ps[pg][:, 0:n],
                                  [(self.ring[:, sg_, kc * 128:(kc + 1) * 128], self.h[:, kc, hc0:hc0 + n]) for kc in range(KC)],
                                  reads=[('ring', sg_)] + hk, writes=[('ps', pg)])
                    self.mm_group(self.ps[pu][:, 0:n],
                                  [(self.ring[:, su_, kc * 128:(kc + 1) * 128], self.h[:, kc, hc0:hc0 + n]) for kc in range(KC)],
                                  reads=[('ring', su_)] + hk, writes=[('ps', pu)])
                    b = self.rot('sg', 2)
                    self.I('act', 'activation', reads=[('ps', pg)], writes=[('sg', b)],
                           out=self.sg[:, b, 0:n], in_=self.ps[pg][:, 0:n], func=AF.Silu)
                    self.I('dve', 'tensor_tensor', reads=[('sg', b), ('ps', pu)], writes=[self.akey(ab, c, tile)],
                           out=self.abuf[:, ab, c, hc0:hc0 + n], in0=self.sg[:, b, 0:n], in1=self.ps[pu][:, 0:n], op=ALU.mult)

        def down(g):
            ab = g % 2
            slots = [self.wtile(('rowblk', wd, li, g * 4 + c)) for c in range(4)]
            for tile in tiles:
                n = tile['n']
                hc0 = self.hcol(tile)
                akeys = [self.akey(ab, c, tile) for c in range(4)]
                for dk in range(KC):
                    pd = self.psum()
                    self.mm_group(self.ps[pd][:, 0:n],
                                  [(self.ring[:, slots[c], dk * 128:(dk + 1) * 128], self.abuf[:, ab, c, hc0:hc0 + n]) for c in range(4)],
                                  reads=[('ring', s_) for s_ in slots] + akeys, writes=[('ps', pd)])
                    for (c0, c1, e_) in self.segs(tile, ext_ctx):
                        self.I('dve', 'scalar_tensor_tensor', reads=[('ps', pd), 'tabG', self.xkey(tile, dk)],
                               writes=[self.xkey(tile, dk)],
                               out=self.xap(tile, dk, c0, c1), in0=self.ps[pd][:, c0:c1], scalar=self.tabG[:, e_, dk:dk + 1],
                               in1=self.xap(tile, dk, c0, c1), op0=ALU.mult, op1=ALU.add)

        gate_up(0)
        for g in range(NG):
            if g + 1 < NG:
                gate_up(g + 1)
            down(g)
            if side and self.adaln_pending():
                left = NG - 1 - g
                self.adaln_work(self.adaln_pending() if left == 0 else -(-self.adaln_pending() // (left + 1)))

    def mix_alloc(self, sb):
        self.qk = sb("qk", [128, 2, T], BF16)
        self.qfb = sb("qfb", [128, 2, 2, 128], BF16)
        self.vh = sb("vh", [128, 9, 128], BF16)
        self.kd = sb("kd", [128, 2, 2, 128], BF16)
        self.Sst = sb("Sst", [128, 2, 128], F32)
        self.Sbf = sb("Sbf", [128, 2, 8, 128], BF16)
        self.PT = sb("PT", [128, 2, 128], BF16)
        self.pst = sb("pst", [128, 6, 128], F32)
        self.Dh = sb("Dh", [128, 4, 128], F32)
        self.dec = sb("dec", [128, 6, 16], F32)
        self.identb = sb("identb", [128, 128], BF16)
        self.dec_done = False

    def mt(self, i):
        if i < 4:
            return self.f32t[:, i, :], ('f32t', i)
        return self.sg[:, i - 4, :], ('sg', i - 4)

    def dec_setup(self):
        if self.dec_done:
            return
        self.dec_done = True
        I = self.I
        raw = self.vecs[:, V_DEC:V_DEC + 16]
        LG, KD, CDt, CD8, AL, TMP = [self.dec[:, i, :] for i in range(6)]
        cst = self.consts
        I('act', 'activation', reads=['vecs'], writes=['dec'], out=TMP, in_=raw, func=AF.Exp, scale=-1.0)
        I('act', 'activation', reads=['dec', 'consts'], writes=['dec'], out=TMP, in_=TMP, func=AF.Ln,
          bias=cst[:, C_ONE:C_ONE + 1], scale=1.0)
        I('dve', 'tensor_scalar', reads=['dec'], writes=['dec'], out=LG, in0=TMP, scalar1=-1.0, scalar2=None, op0=ALU.mult)
        I('dve', 'tensor_scalar', reads=['dec', 'consts'], writes=['dec'], out=TMP[:, 0:8], in0=LG[:, 0:8],
          scalar1=cst[:, C_127MP:C_127MP + 1], scalar2=None, op0=ALU.mult)
        I('dve', 'tensor_scalar', reads=['dec', 'consts'], writes=['dec'], out=TMP[:, 8:16], in0=LG[:, 8:16],
          scalar1=cst[:, C_P:C_P + 1], scalar2=None, op0=ALU.mult)
        I('act', 'activation', reads=['dec', 'consts'], writes=['dec'], out=KD, in_=TMP, func=AF.Exp,
          bias=cst[:, C_LNK:C_LNK + 1], scale=1.0)
        I('act', 'activation', reads=['dec'], writes=['dec'], out=CDt, in_=LG, func=AF.Exp, scale=128.0)
        I('act', 'activation', reads=['dec'], writes=['dec'], out=CD8, in_=LG, func=AF.Exp, scale=1024.0)
        sel0 = self.vecs[:, V_SEL:V_SEL + 1]
        sel1 = self.vecs[:, V_SEL + 1:V_SEL + 2]
        I('dve', 'tensor_scalar', reads=['dec', 'vecs'], writes=['dec'], out=AL[:, 0:8], in0=CD8[:, 0:8],
          scalar1=sel1, scalar2=sel0, op0=ALU.mult, op1=ALU.add)
        I('dve', 'tensor_scalar', reads=['dec', 'vecs'], writes=['dec'], out=AL[:, 8:16], in0=CD8[:, 8:16],
          scalar1=sel0, scalar2=sel1, op0=ALU.mult, op1=ALU.add)
        I('dve', 'tensor_copy', reads=['consts'], writes=['identb'], out=self.identb[:, :], in_=cst[:, C_ID:C_ID + 128])

    def LGc(self, d, hd):
        return self.dec[:, 0, 8 * d + hd:8 * d + hd + 1]

    def KDc(self, d, hd):
        return self.dec[:, 1, 8 * d + hd:8 * d + hd + 1]

    def CDc(self, d, hd):
        return self.dec[:, 2, 8 * d + hd:8 * d + hd + 1]

    def ALc(self, d, hd):
        return self.dec[:, 4, 8 * d + hd:8 * d + hd + 1]

    def proj_rope(self, slot, dst):
        I = self.I
        for half in range(2):
            c0 = half * 512
            pq = self.psum()
            self.mm_group(self.ps[pq][:, :],
                          [(self.ring[:, slot, kc * 128:(kc + 1) * 128], self.h[:, kc, c0:c0 + 512]) for kc in range(KC)],
                          reads=[('ring', slot)] + [('h', k, half) for k in range(KC)], writes=[('ps', pq)])
            m0, k0 = self.mt(0)
            m1, k1 = self.mt(1)
            m2, k2 = self.mt(2)
            I('act', 'activation', reads=[('ps', pq)], writes=[k0], out=m0[:, 0:512], in_=self.ps[pq][:, :], func=AF.Copy)
            pr = self.psum()
            I('pe', 'matmul', reads=[k0, 'consts'], writes=[('ps', pr)], out=self.ps[pr][:, :],
              lhsT=self.consts[:, C_PSW:C_PSW + 128], rhs=m0[:, 0:512], start=True, stop=True)
            I('dve', 'tensor_tensor', reads=[k0, ('wide', 0)], writes=[k1], out=m1[:, 0:512], in0=m0[:, 0:512],
              in1=self.wide[:, 0, c0:c0 + 512], op=ALU.mult)
            I('dve', 'tensor_tensor', reads=[('ps', pr), ('wide', 1)], writes=[k2], out=m2[:, 0:512], in0=self.ps[pr][:, :],
              in1=self.wide[:, 1, c0:c0 + 512], op=ALU.mult)
            I('dve', 'tensor_tensor', reads=[k1, k2], writes=[('qk', dst, half)], out=self.qk[:, dst, c0:c0 + 512],
              in0=m1[:, 0:512], in1=m2[:, 0:512], op=ALU.add)

    def v_proj(self, slot):
        I = self.I
        for grp in range(3):
            ns = [0, 1, 2, 3] if grp == 0 else ([4, 5, 6, 7] if grp == 1 else [8])
            pv = self.psum()
            for j, n in enumerate(ns):
                if n < 8:
                    cols = (n * 128, n * 128 + 128)
                    hk = [('h', k, n // 4) for k in range(KC)]
                else:
                    cols = (T + 2, T + 130)
                    hk = [('he', k) for k in range(KC)]
                self.mm_group(self.ps[pv][:, j * 128:(j + 1) * 128],
                              [(self.h[:, kc, cols[0]:cols[1]], self.ring[:, slot, kc * 128:(kc + 1) * 128]) for kc in range(KC)],
                              reads=[('ring', slot)] + hk, writes=[('ps', pv)])
            for j, n in enumerate(ns):
                I('act', 'activation', reads=[('ps', pv)], writes=[('vh', n)], out=self.vh[:, n, :],
                  in_=self.ps[pv][:, j * 128:(j + 1) * 128], func=AF.Copy)

    def kv_mm(self, d, r, n):
        pk = self.psum()
        self.I('pe', 'matmul', reads=[('kd', d, r), ('vh', n)], writes=[('ps', pk)], out=self.ps[pk][:, 0:128],
               lhsT=self.kd[:, d, r, :], rhs=self.vh[:, n, :], start=True, stop=True)
        return pk

    def ctx_kd(self, hd, slot):
        I = self.I
        pc = self.psum()
        self.mm_group(self.ps[pc][:, 0:128],
                      [(self.h[:, kc, T + 2:T + 130], self.ring[:, slot, kc * 128:(kc + 1) * 128]) for kc in range(KC)],
                      reads=[('ring', slot)] + [('he', k) for k in range(KC)], writes=[('ps', pc)])
        r = self.rot('kd', 2)
        I('act', 'activation', reads=[('ps', pc), 'dec'], writes=[('kd', 0, r)], out=self.kd[:, 0, r, :],
          in_=self.ps[pc][:, 0:128], func=AF.Identity, scale=self.KDc(0, hd))
        I('dve', 'tensor_scalar', reads=[('ps', pc), 'dec'], writes=[('kd', 1, r)], out=self.kd[:, 1, r, :],
          in0=self.ps[pc][:, 0:128], scalar1=self.KDc(1, hd), scalar2=None, op0=ALU.mult)
        return r

    def mixer_states(self):
        I = self.I
        self.dec_setup()
        pay = self.pay1.ap() if hasattr(self.pay1, 'ap') else self.pay1
        payv = pay.rearrange("(k h p) v -> h p k v", k=4, h=NH, p=128)
        for hd in range(NH):
            sk = self.wtile(('colblk', 'mix_w_in', 0, 8 + hd))
            sv = self.wtile(('colblk', 'mix_w_in', 0, 16 + hd))
            self.proj_rope(sk, 1)
            self.v_proj(sv)
            r = self.ctx_kd(hd, sk)
            for d in range(2):
                pk = self.kv_mm(d, r, 8)
                I('act', 'activation', reads=[('ps', pk)], writes=[('pst', d)], out=self.pst[:, d, :],
                  in_=self.ps[pk][:, 0:128], func=AF.Copy)
            rs = {}
            for n in range(8):
                rs[n] = None
            order_f = list(range(8))
            order_b = list(range(7, -1, -1))
            first = [True, True]
            for step in range(8):
                for d, n in ((0, order_f[step]), (1, order_b[step])):
                    r = self.k_tm_chunk_dir(hd, n, d)
                    pk = self.kv_mm(d, r, n)
                    if first[d]:
                        I('dve', 'tensor_copy', reads=[('ps', pk)], writes=[('pst', 2 + d)], out=self.pst[:, 2 + d, :],
                          in_=self.ps[pk][:, 0:128])
                        first[d] = False
                    else:
                        I('dve', 'scalar_tensor_tensor', reads=[('ps', pk), ('pst', 2 + d), 'dec'], writes=[('pst', 2 + d)],
                          out=self.pst[:, 2 + d, :], in0=self.pst[:, 2 + d, :], scalar=self.CDc(d, hd),
                          in1=self.ps[pk][:, 0:128], op0=ALU.mult, op1=ALU.add)
            self.P.dma('sp', payv[hd], self.pst[:, 0:4, :], reads=[('pst', i) for i in range(4)], writes=['pay1'])

    def k_tm_chunk_dir(self, hd, n, d):
        I = self.I
        r = self.rot('kd%d' % d, 2)
        I('pe', 'transpose', reads=[('qk', 1, n // 4), 'identb'], writes=[('psb', n)],
          out=self.psb[:, n * 128:(n + 1) * 128], in_=self.qk[:, 1, n * 128:(n + 1) * 128], identity=self.identb[:, :])
        if d == 0:
            I('act', 'activation', reads=[('psb', n), 'dec'], writes=[('kd', d, r)], out=self.kd[:, d, r, :],
              in_=self.psb[:, n * 128:(n + 1) * 128], func=AF.Identity, scale=self.KDc(d, hd))
        else:
            I('dve', 'tensor_scalar', reads=[('psb', n), 'dec'], writes=[('kd', d, r)], out=self.kd[:, d, r, :],
              in0=self.psb[:, n * 128:(n + 1) * 128], scalar1=self.KDc(d, hd), scalar2=None, op0=ALU.mult)
        return r

    def head_tables(self, hd):
        I = self.I
        cst = self.consts
        lnk = cst[:, C_LNK:C_LNK + 1]
        I('act', 'activation', reads=['consts', 'dec'], writes=[('Dh', 0)], out=self.Dh[:, 0, :], in_=cst[:, C_RDF:C_RDF + 128],
          func=AF.Exp, bias=lnk, scale=self.LGc(0, hd))
        I('act', 'activation', reads=['consts', 'dec'], writes=[('Dh', 1)], out=self.Dh[:, 1, :], in_=cst[:, C_RDB:C_RDB + 128],
          func=AF.Exp, bias=lnk, scale=self.LGc(1, hd))
        I('dve', 'tensor_tensor', reads=[('Dh', 0), ('Dh', 1)], writes=[('Dh', 0)], out=self.Dh[:, 0, :], in0=self.Dh[:, 0, :],
          in1=self.Dh[:, 1, :], op=ALU.add)
        I('act', 'activation', reads=['consts', 'dec'], writes=[('Dh', 2)], out=self.Dh[:, 2, :], in_=cst[:, C_ROWF:C_ROWF + 128],
          func=AF.Exp, scale=self.LGc(0, hd))
        I('act', 'activation', reads=['consts', 'dec'], writes=[('Dh', 3)], out=self.Dh[:, 3, :], in_=cst[:, C_ROWB:C_ROWB + 128],
          func=AF.Exp, scale=self.LGc(1, hd))

    def mix_down(self, g):
        ab = g % 2
        slots = [self.wtile(('rowblk', 'mix_w_out', 0, g * 4 + c)) for c in range(4)]
        for tile in self.main_tiles:
            c0 = tile['c0']
            akeys = [self.akey(ab, c, tile) for c in range(4)]
            for dk in range(KC):
                pd = self.psum()
                self.mm_group(self.ps[pd][:, :],
                              [(self.ring[:, slots[c], dk * 128:(dk + 1) * 128], self.abuf[:, ab, c, c0:c0 + 512]) for c in range(4)],
                              reads=[('ring', s_) for s_ in slots] + akeys, writes=[('ps', pd)])
                self.I('dve', 'scalar_tensor_tensor', reads=[('ps', pd), 'tabG', self.xkey(tile, dk)], writes=[self.xkey(tile, dk)],
                       out=self.xap(tile, dk), in0=self.ps[pd][:, :], scalar=self.tabG[:, 0, dk:dk + 1],
                       in1=self.xap(tile, dk), op0=ALU.mult, op1=ALU.add)

    def mixer_main(self):
        I = self.I
        self.dec_setup()
        payg = self.pay1g.ap() if hasattr(self.pay1g, 'ap') else self.pay1g
        pgv = payg.rearrange("(r k h p) v -> r k h p v", r=2, k=4, h=NH, p=128)
        sel0 = self.vecs[:, V_SEL:V_SEL + 1]
        sel1 = self.vecs[:, V_SEL + 1:V_SEL + 2]
        for hd in range(NH):
            g = hd // 4
            c = hd % 4
            ab = g % 2
            sq_ = self.wtile(('colblk', 'mix_w_in', 0, hd))
            sk = self.wtile(('colblk', 'mix_w_in', 0, 8 + hd))
            sv = self.wtile(('colblk', 'mix_w_in', 0, 16 + hd))
            sgt = self.wtile(('colblk', 'mix_w_in', 0, 24 + hd))
            srcs = [(0, 0), (1, 0), (0, 1), (1, 1), (0, 2), (1, 3)]
            for i, (rk, kind) in enumerate(srcs):
                self.P.dma('sp', self.pst[:, i, :], pgv[rk, kind, hd], reads=['pay1g'], writes=[('pst', i)])
            I('dve', 'scalar_tensor_tensor', reads=[('pst', 0), ('pst', 1), 'dec'], writes=[('Sst', 0)], out=self.Sst[:, 0, :],
              in0=self.pst[:, 0, :], scalar=self.CDc(0, hd), in1=self.pst[:, 1, :], op0=ALU.mult, op1=ALU.add)
            I('dve', 'tensor_scalar', reads=[('Sst', 0), 'dec'], writes=[('Sst', 0)], out=self.Sst[:, 0, :], in0=self.Sst[:, 0, :],
              scalar1=self.ALc(0, hd), scalar2=None, op0=ALU.mult)
            I('dve', 'scalar_tensor_tensor', reads=[('Sst', 0), ('pst', 4), 'vecs'], writes=[('Sst', 0)], out=self.Sst[:, 0, :],
              in0=self.pst[:, 4, :], scalar=sel1, in1=self.Sst[:, 0, :], op0=ALU.mult, op1=ALU.add)
            I('dve', 'scalar_tensor_tensor', reads=[('pst', 2), ('pst', 3), 'dec'], writes=[('Sst', 1)], out=self.Sst[:, 1, :],
              in0=self.pst[:, 3, :], scalar=self.CDc(1, hd), in1=self.pst[:, 2, :], op0=ALU.mult, op1=ALU.add)
            I('dve', 'tensor_scalar', reads=[('Sst', 1), 'dec'], writes=[('Sst', 1)], out=self.Sst[:, 1, :], in0=self.Sst[:, 1, :],
              scalar1=self.ALc(1, hd), scalar2=None, op0=ALU.mult)
            I('dve', 'scalar_tensor_tensor', reads=[('Sst', 1), ('pst', 5), 'vecs'], writes=[('Sst', 1)], out=self.Sst[:, 1, :],
              in0=self.pst[:, 5, :], scalar=sel0, in1=self.Sst[:, 1, :], op0=ALU.mult, op1=ALU.add)
            self.head_tables(hd)
            self.proj_rope(sq_, 0)
            self.proj_rope(sk, 1)
            self.v_proj_main(sv)
            I('act', 'activation', reads=[('Sst', 0)], writes=[('Sbf', 0, 0)], out=self.Sbf[:, 0, 0, :], in_=self.Sst[:, 0, :], func=AF.Copy)
            I('act', 'activation', reads=[('Sst', 1)], writes=[('Sbf', 1, 7)], out=self.Sbf[:, 1, 7, :], in_=self.Sst[:, 1, :], func=AF.Copy)
            for step in range(7):
                for d, n in ((0, step), (1, 7 - step)):
                    r = self.k_tm_chunk_dir(hd, n, d)
                    pk = self.kv_mm(d, r, n)
                    I('dve', 'scalar_tensor_tensor', reads=[('ps', pk), ('Sst', d), 'dec'], writes=[('Sst', d)],
                      out=self.Sst[:, d, :], in0=self.Sst[:, d, :], scalar=self.CDc(d, hd), in1=self.ps[pk][:, 0:128],
                      op0=ALU.mult, op1=ALU.add)
                    nn = n + 1 if d == 0 else n - 1
                    I('act', 'activation', reads=[('Sst', d)], writes=[('Sbf', d, nn)], out=self.Sbf[:, d, nn, :],
                      in_=self.Sst[:, d, :], func=AF.Copy)
            for half in range(2):
                po = self.psum()
                for j in range(4):
                    n = half * 4 + j
                    cs = slice(n * 128, (n + 1) * 128)
                    psc = self.psum()
                    I('pe', 'matmul', reads=[('qk', 0, half), ('qk', 1, half)], writes=[('ps', psc)], out=self.ps[psc][:, 0:128],
                      lhsT=self.qk[:, 1, cs], rhs=self.qk[:, 0, cs], start=True, stop=True)
                    r = self.rot('PT', 2)
                    I('dve', 'tensor_tensor', reads=[('ps', psc), ('Dh', 0)], writes=[('PT', r)], out=self.PT[:, r, :],
                      in0=self.ps[psc][:, 0:128], in1=self.Dh[:, 0, :], op=ALU.mult)
                    rq = self.rot('qfb', 2)
                    I('dve', 'tensor_tensor', reads=[('qk', 0, half), ('Dh', 2)], writes=[('qfb', 0, rq)], out=self.qfb[:, 0, rq, :],
                      in0=self.qk[:, 0, cs], in1=self.Dh[:, 2, :], op=ALU.mult)
                    I('dve', 'tensor_tensor', reads=[('qk', 0, half), ('Dh', 3)], writes=[('qfb', 1, rq)], out=self.qfb[:, 1, rq, :],
                      in0=self.qk[:, 0, cs], in1=self.Dh[:, 3, :], op=ALU.mult)
                    self.mm_group(self.ps[po][:, j * 128:(j + 1) * 128],
                                  [(self.vh[:, n, :], self.PT[:, r, :]),
                                   (self.Sbf[:, 0, n, :], self.qfb[:, 0, rq, :]),
                                   (self.Sbf[:, 1, n, :], self.qfb[:, 1, rq, :])],
                                  reads=[('vh', n), ('PT', r), ('Sbf', 0, n), ('Sbf', 1, n), ('qfb', 0, rq), ('qfb', 1, rq)],
                                  writes=[('ps', po)])
                m0, k0 = self.mt(0)
                m1, k1 = self.mt(1)
                m2, k2 = self.mt(2)
                m3, k3 = self.mt(3)
                I('act', 'activation', reads=[('ps', po)], writes=[k0], out=m0[:, 0:512], in_=self.ps[po][:, :], func=AF.Copy)
                pm = self.psum()
                I('pe', 'matmul', reads=[k0, 'onesf'], writes=[('ps', pm)], out=self.ps[pm][:, :], lhsT=self.onesf[:, :],
                  rhs=m0[:, 0:512], start=True, stop=True)
                I('dve', 'tensor_tensor', reads=[k0, ('ps', pm)], writes=[k1], out=m1[:, 0:512], in0=m0[:, 0:512],
                  in1=self.ps[pm][:, :], op=ALU.subtract)
                I('act', 'activation', reads=[k1], writes=[k2], out=m2[:, 0:512], in_=m1[:, 0:512], func=AF.Square)
                pvv = self.psum()
                I('pe', 'matmul', reads=[k2, 'onesf'], writes=[('ps', pvv)], out=self.ps[pvv][:, :], lhsT=self.onesf[:, :],
                  rhs=m2[:, 0:512], start=True, stop=True)
                I('act', 'activation', reads=[('ps', pvv), 'consts'], writes=[k2], out=m2[:, 0:512], in_=self.ps[pvv][:, :],
                  func=AF.Sqrt, bias=self.consts[:, C_GNEPS:C_GNEPS + 1], scale=1.0)
                I('dve', 'reciprocal', reads=[k2], writes=[k2], out=m2[:, 0:512], in_=m2[:, 0:512])
                I('dve', 'tensor_tensor', reads=[k1, k2], writes=[k1], out=m1[:, 0:512], in0=m1[:, 0:512], in1=m2[:, 0:512], op=ALU.mult)
                pg = self.psum()
                c0 = half * 512
                self.mm_group(self.ps[pg][:, :],
                              [(self.ring[:, sgt, kc * 128:(kc + 1) * 128], self.h[:, kc, c0:c0 + 512]) for kc in range(KC)],
                              reads=[('ring', sgt)] + [('h', k, half) for k in range(KC)], writes=[('ps', pg)])
                I('act', 'activation', reads=[('ps', pg)], writes=[k3], out=m3[:, 0:512], in_=self.ps[pg][:, :], func=AF.Silu)
                I('dve', 'tensor_tensor', reads=[k1, k3], writes=[self.akey(ab, c, self.main_tiles[half])],
                  out=self.abuf[:, ab, c, c0:c0 + 512], in0=m1[:, 0:512], in1=m3[:, 0:512], op=ALU.mult)
            if hd == 7:
                self.mix_down(0)
        wcv = self.vecs[:, V_WCONV:V_WCONV + 24].rearrange("p (t j) -> p t j", t=3)
        for j in range(8):
            g = 2 + j // 4
            c = j % 4
            ab = g % 2
            sB = self.wtile(('colblk', 'mix_w_in', 0, 32 + j))
            sC = self.wtile(('colblk', 'mix_w_in', 0, 40 + j))
            sU = self.wtile(('colblk', 'mix_w_in', 0, 48 + j))
            cu = [self.mt(0), self.mt(1)]
            mu, ku = self.mt(2)
            for half in range(2):
                c0 = half * 512
                hk = [('h', k, half) for k in range(KC)]
                pC = self.psum()
                pU = self.psum()
                self.mm_group(self.ps[pC][:, :], [(self.ring[:, sC, kc * 128:(kc + 1) * 128], self.h[:, kc, c0:c0 + 512]) for kc in range(KC)],
                              reads=[('ring', sC)] + hk, writes=[('ps', pC)])
                self.mm_group(self.ps[pU][:, :], [(self.ring[:, sU, kc * 128:(kc + 1) * 128], self.h[:, kc, c0:c0 + 512]) for kc in range(KC)],
                              reads=[('ring', sU)] + hk, writes=[('ps', pU)])
                I('act', 'activation', reads=[('ps', pU)], writes=[ku], out=mu[:, 0:512], in_=self.ps[pU][:, :], func=AF.Copy)
                I('dve', 'tensor_tensor', reads=[('ps', pC), ku], writes=[cu[half][1]], out=cu[half][0][:, 1:513],
                  in0=self.ps[pC][:, :], in1=mu[:, 0:512], op=ALU.mult)
            pH = self.psum()
            hk = [('he', k) for k in range(KC)]
            self.mm_group(self.ps[pH][:, 0:2], [(self.ring[:, sC, kc * 128:(kc + 1) * 128], self.h[:, kc, T:T + 2]) for kc in range(KC)],
                          reads=[('ring', sC)] + hk, writes=[('ps', pH)])
            pH2 = self.psum()
            self.mm_group(self.ps[pH2][:, 0:2], [(self.ring[:, sU, kc * 128:(kc + 1) * 128], self.h[:, kc, T:T + 2]) for kc in range(KC)],
                          reads=[('ring', sU)] + hk, writes=[('ps', pH2)])
            I('act', 'activation', reads=[('ps', pH2)], writes=[ku], out=mu[:, 0:2], in_=self.ps[pH2][:, 0:2], func=AF.Copy)
            I('dve', 'tensor_tensor', reads=[('ps', pH), ku], writes=[ku], out=mu[:, 2:4], in0=self.ps[pH][:, 0:2], in1=mu[:, 0:2], op=ALU.mult)
            I('dve', 'tensor_tensor', reads=[ku, 'vecs'], writes=[ku], out=mu[:, 4:6], in0=mu[:, 2:4],
              in1=self.vecs[:, V_SEL + 2:V_SEL + 4], op=ALU.mult)
            I('dve', 'tensor_copy', reads=[ku], writes=[cu[0][1]], out=cu[0][0][:, 0:1], in_=mu[:, 4:5])
            I('dve', 'tensor_copy', reads=[ku], writes=[cu[1][1]], out=cu[1][0][:, 513:514], in_=mu[:, 5:6])
            I('dve', 'tensor_copy', reads=[cu[1][1]], writes=[cu[0][1]], out=cu[0][0][:, 513:514], in_=cu[1][0][:, 1:2])
            I('dve', 'tensor_copy', reads=[cu[0][1]], writes=[cu[1][1]], out=cu[1][0][:, 0:1], in_=cu[0][0][:, 512:513])
            for half in range(2):
                c0 = half * 512
                mc_, kc_ = self.mt(3)
                src = cu[half][0]
                I('dve', 'tensor_scalar', reads=[cu[half][1], 'vecs'], writes=[kc_], out=mc_[:, 0:512], in0=src[:, 1:513],
                  scalar1=wcv[:, 1, j:j + 1], scalar2=None, op0=ALU.mult)
                I('dve', 'scalar_tensor_tensor', reads=[cu[half][1], kc_, 'vecs'], writes=[kc_], out=mc_[:, 0:512], in0=src[:, 0:512],
                  scalar=wcv[:, 0, j:j + 1], in1=mc_[:, 0:512], op0=ALU.mult, op1=ALU.add)
                I('dve', 'scalar_tensor_tensor', reads=[cu[half][1], kc_, 'vecs'], writes=[kc_], out=mc_[:, 0:512], in0=src[:, 2:514],
                  scalar=wcv[:, 2, j:j + 1], in1=mc_[:, 0:512], op0=ALU.mult, op1=ALU.add)
                pB = self.psum()
                self.mm_group(self.ps[pB][:, :], [(self.ring[:, sB, kc * 128:(kc + 1) * 128], self.h[:, kc, c0:c0 + 512]) for kc in range(KC)],
                              reads=[('ring', sB)] + [('h', k, half) for k in range(KC)], writes=[('ps', pB)])
                I('dve', 'tensor_tensor', reads=[('ps', pB), kc_], writes=[self.akey(ab, c, self.main_tiles[half])],
                  out=self.abuf[:, ab, c, c0:c0 + 512], in0=self.ps[pB][:, :], in1=mc_[:, 0:512], op=ALU.mult)
            if j == 3:
                self.mix_down(1)
            if j == 7:
                self.mix_down(2)
        self.mix_down(3)

    def v_proj_main(self, slot):
        I = self.I
        for grp in range(2):
            ns = [0, 1, 2, 3] if grp == 0 else [4, 5, 6, 7]
            pv = self.psum()
            for j, n in enumerate(ns):
                self.mm_group(self.ps[pv][:, j * 128:(j + 1) * 128],
                              [(self.h[:, kc, n * 128:n * 128 + 128], self.ring[:, slot, kc * 128:(kc + 1) * 128]) for kc in range(KC)],
                              reads=[('ring', slot)] + [('h', k, n // 4) for k in range(KC)], writes=[('ps', pv)])
            for j, n in enumerate(ns):
                I('act', 'activation', reads=[('ps', pv)], writes=[('vh', n)], out=self.vh[:, n, :],
                  in_=self.ps[pv][:, j * 128:(j + 1) * 128], func=AF.Copy)

    def halo_out(self):
        pay = self.pay2.ap() if hasattr(self.pay2, 'ap') else self.pay2
        pv = pay.rearrange("p (k t) -> p k t", k=KC)
        xk = [('x', k, 0) for k in range(KC)] + [('x', k, 1) for k in range(KC)]
        self.P.dma('sp', pv[:, :, 0:8], self.x[:, :, 0:8], reads=xk, writes=['pay2'])
        self.P.dma('sp', pv[:, :, 8:16], self.x[:, :, T - 8:T], reads=xk, writes=['pay2'])

    def pool_mixer(self):
        I = self.I
        payg = self.pay2g.ap() if hasattr(self.pay2g, 'ap') else self.pay2g
        pgv = payg.rearrange("(r p) (k t) -> r p k t", r=2, k=KC)
        xek = [('xe', k) for k in range(KC)]
        self.P.dma('sp', self.xe[:, :, 0:8], pgv[0][:, :, 8:16], reads=['pay2g'], writes=xek)
        self.P.dma('sp', self.xe[:, :, 8:16], pgv[1][:, :, 0:8], reads=['pay2g'], writes=xek)
        halo_tile = dict(kind='xe', c0=0, n=16)
        self.norm_mod(1, 'mix', self.main_tiles + [halo_tile])
        W = 8 + T + 8
        maskL = self.vecs[:, V_SEL + 2:V_SEL + 3]
        maskR = self.vecs[:, V_SEL + 3:V_SEL + 4]
        pf = self.vecs[:, V_PFAC:V_PFAC + 64].rearrange("p (w t) -> p w t", w=4)
        for g in range(4):
            ab = g % 2
            w = 2 << g
            for c in range(4):
                k = 4 * g + c
                hp = self.wide[:, 0, :]
                I('act', 'activation', reads=[('h', k, 0), ('h', k, 1)], writes=[('wide', 0)], out=hp[:, 8:8 + T], in_=self.h[:, k, 0:T], func=AF.Copy)
                I('dve', 'tensor_scalar', reads=[('he', k), 'vecs'], writes=[('wide', 0)], out=hp[:, 0:8], in0=self.h[:, k, T:T + 8],
                  scalar1=maskL, scalar2=None, op0=ALU.mult)
                I('dve', 'tensor_scalar', reads=[('he', k), 'vecs'], writes=[('wide', 0)], out=hp[:, 8 + T:W], in0=self.h[:, k, T + 8:T + 16],
                  scalar1=maskR, scalar2=None, op0=ALU.mult)
                a_, b_ = self.wide[:, 1, :], self.wide[:, 2, :]
                I('dve', 'tensor_tensor', reads=[('wide', 0)], writes=[('wide', 1)], out=a_[:, 1:W], in0=hp[:, 0:W - 1], in1=hp[:, 1:W], op=ALU.add)
                cur, ck, oth, ok = a_, ('wide', 1), b_, ('wide', 2)
                lo, hi, sh = 1, W, 1
                for lvl in range(g):
                    nlo, nhi = lo + sh, hi - sh
                    I('dve', 'tensor_tensor', reads=[ck], writes=[ok], out=oth[:, nlo:nhi], in0=cur[:, nlo - sh:nhi - sh],
                      in1=cur[:, nlo + sh:nhi + sh], op=ALU.add)
                    cur, ck, oth, ok = oth, ok, cur, ck
                    lo, hi, sh = nlo, nhi, sh * 2
                tile0 = self.main_tiles[0]
                I('dve', 'scalar_tensor_tensor', reads=[ck, ('wide', 0)], writes=[ok], out=oth[:, 8:8 + T], in0=cur[:, 8:8 + T],
                  scalar=1.0 / w, in1=hp[:, 8:8 + T], op0=ALU.mult, op1=ALU.subtract)
                for (a0, t0) in ((8, 0), (T, 8)):
                    I('dve', 'tensor_tensor', reads=[ck, 'vecs'], writes=[ck], out=cur[:, a0:a0 + 8], in0=cur[:, a0:a0 + 8],
                      in1=pf[:, g, t0:t0 + 8], op=ALU.mult)
                    I('dve', 'tensor_tensor', reads=[ck, ('wide', 0), ok], writes=[ok], out=oth[:, a0:a0 + 8], in0=cur[:, a0:a0 + 8],
                      in1=hp[:, a0:a0 + 8], op=ALU.subtract)
                I('act', 'activation', reads=[ok], writes=[self.akey(ab, c, self.main_tiles[0]), self.akey(ab, c, self.main_tiles[1])],
                  out=self.abuf[:, ab, c, 0:T], in_=oth[:, 8:8 + T], func=AF.Copy)
            sp_ = self.wtile(('pool', g))
            for tile in self.main_tiles:
                c0 = tile['c0']
                akeys = [self.akey(ab, c, tile) for c in range(4)]
                for do in range(4):
                    dk = 4 * g + do
                    pd = self.psum()
                    self.mm_group(self.ps[pd][:, :],
                                  [(self.ring[:, sp_, i * 512 + do * 128:i * 512 + do * 128 + 128], self.abuf[:, ab, i, c0:c0 + 512]) for i in range(4)],
                                  reads=[('ring', sp_)] + akeys, writes=[('ps', pd)])
                    I('dve', 'scalar_tensor_tensor', reads=[('ps', pd), 'tabG', self.xkey(tile, dk)], writes=[self.xkey(tile, dk)],
                      out=self.xap(tile, dk), in0=self.ps[pd][:, :], scalar=self.tabG[:, 0, dk:dk + 1],
                      in1=self.xap(tile, dk), op0=ALU.mult, op1=ALU.add)

    def final_norm(self):
        I = self.I
        toks = []
        for tile in self.main_tiles:
            n = tile['n']
            c0 = tile['c0']
            pb = self.psum()
            psv = self.ps[pb][:, 0:n]
            for k in range(KC):
                b = self.rot('sq', 2)
                I('act', 'activation', reads=[self.xkey(tile, k)], writes=[('sq', b)], out=self.sq[:, b, 0:n], in_=self.xap(tile, k), func=AF.Square)
                I('pe', 'matmul', reads=[('sq', b), 'ones'], writes=[('ps', pb)], out=psv, lhsT=self.ones[:, :], rhs=self.sq[:, b, 0:n],
                  start=(k == 0), stop=(k == KC - 1))
            I('act', 'activation', reads=[('ps', pb), 'consts'], writes=['rstd'], out=self.rstd[:, 0:n], in_=psv, func=AF.Sqrt,
              bias=self.consts[:, C_EPS:C_EPS + 1], scale=1.0 / D)
            I('dve', 'reciprocal', reads=['rstd'], writes=['rstd'], out=self.rstd[:, 0:n], in_=self.rstd[:, 0:n])
            for k in range(KC):
                b = self.rot('f32t', 4)
                I('dve', 'scalar_tensor_tensor', reads=[self.xkey(tile, k), 'rstd', 'vecs'], writes=[('f32t', b)],
                  out=self.f32t[:, b, 0:n], in0=self.xap(tile, k), scalar=self.vecs[:, V_FNORM + k:V_FNORM + k + 1],
                  in1=self.rstd[:, 0:n], op0=ALU.mult, op1=ALU.mult)
                toks.append(self.P.dma('sp', self.out_d[:, k, c0:c0 + n], self.f32t[:, b, 0:n], reads=[('f32t', b)], writes=[('out', k, c0)]))
        for t in toks:
            self.P.finish('sp', t)


V_CC = 0
V_BMOD = V_CC + 32
V_NF1 = V_BMOD + 288
V_NMIX = V_NF1 + 32
V_NF2 = V_NMIX + 32
V_PSCALE = V_NF2 + 32
V_FNORM = V_PSCALE + 16
V_WCONV = V_FNORM + 16
V_DEC = V_WCONV + 24
V_SEL = V_DEC + 16
V_PFAC = V_SEL + 4
NV = V_PFAC + 64
C_EPS = 0
C_GNEPS = 1
C_ONE = 2
C_LNK = 3
C_127MP = 4
C_P = 5
C_RDF = 8
C_RDB = C_RDF + 128
C_ROWF = C_RDB + 128
C_ROWB = C_ROWF + 128
C_PSW = C_ROWB + 128
C_ID = C_PSW + 128
NCONST = C_ID + 128
PAIRS = [[0, 1], [2, 3], [4, 5], [6, 7]]


def _fm(a):
    n = a.shape[0]
    return np.ascontiguousarray(a.reshape(n, KC, 128).transpose(2, 1, 0))


def _vec16(v):
    return np.ascontiguousarray(v.reshape(-1, 128).T)


def rope_tables(s):
    t = np.arange(T) + 1024 * s
    rows = (t // 64).astype(np.float32)
    cols = (t % 64).astype(np.float32)
    quarter = 32
    inv = (np.float32(10000.0) ** (-np.arange(quarter, dtype=np.float32) / quarter)).astype(np.float32)
    ang = np.concatenate([rows[:, None] * inv, cols[:, None] * inv], axis=-1).astype(np.float32)
    cos = np.cos(ang).astype(np.float32).T
    sin = np.sin(ang).astype(np.float32).T
    out = np.zeros((128, 2, T), np.float32)
    out[:64, 0] = cos
    out[64:, 0] = cos
    out[:64, 1] = -sin
    out[64:, 1] = sin
    return out


def core_inputs(inp, b, s):
    x = inp['x']
    d = {}
    d['xin'] = _fm(x[b, 1024 * s:1024 * s + 1024])
    xe = np.zeros((NEXT, D), np.float32)
    if s == 1:
        xe[0] = x[b, 1023]
    if s == 0:
        xe[1] = x[b, 1024]
    xe[2:] = inp['ctx'][b, 128 * s:128 * s + 128]
    d['xein'] = _fm(xe)
    v = np.zeros((128, NV), np.float32)
    cc = np.stack([_vec16(inp['c'][b]), _vec16(inp['c_ctx'])], axis=-1)
    v[:, V_CC:V_CC + 32] = cc.reshape(128, 32)
    for li in range(2):
        v[:, V_BMOD + li * 144:V_BMOD + (li + 1) * 144] = _vec16(inp['b_mod'][li])
        v[:, V_NF1 + li * 16:V_NF1 + (li + 1) * 16] = _vec16(inp['norm_ffn1'][li])
        v[:, V_NMIX + li * 16:V_NMIX + (li + 1) * 16] = _vec16(inp['norm_mix'][li])
        v[:, V_NF2 + li * 16:V_NF2 + (li + 1) * 16] = _vec16(inp['norm_ffn2'][li])
    v[:, V_PSCALE:V_PSCALE + 16] = _vec16(inp['pool_scale'][0])
    v[:, V_FNORM:V_FNORM + 16] = _vec16(inp['final_norm'])
    wc = inp['mix_w_conv'][0]
    v[:, V_WCONV:V_WCONV + 24] = np.stack([_vec16(wc[t]) for t in range(3)], axis=1).reshape(128, 24)
    v[:, V_DEC:V_DEC + 8] = inp['ret_decay_fwd'][0][None, :]
    v[:, V_DEC + 8:V_DEC + 16] = inp['ret_decay_bwd'][0][None, :]
    v[:, V_SEL + 0] = 1.0 - s
    v[:, V_SEL + 1] = float(s)
    v[:, V_SEL + 2] = float(s == 1)
    v[:, V_SEL + 3] = float(s == 0)
    for gi, w in enumerate((2, 4, 8, 16)):
        for e in range(16):
            t = (e if e < 8 else T - 16 + e) + 1024 * s
            lo = min(max(t - w // 2, 0), 2048)
            hi = min(max(t + (w - w // 2), 0), 2048)
            v[:, V_PFAC + gi * 16 + e] = 1.0 / float(hi - lo)
    d['vecs'] = v
    d['consts'] = make_consts()
    d['rope'] = rope_tables(s)
    return d


def make_consts():
    c = np.zeros((128, NCONST), np.float32)
    c[:, C_EPS] = EPS
    c[:, C_GNEPS] = GN_EPS
    c[:, C_ONE] = 1.0
    c[:, C_LNK] = np.log(np.float32(K_SCALE))
    p = np.arange(128, dtype=np.float32)
    c[:, C_127MP] = 127.0 - p
    c[:, C_P] = p
    m = p[:, None]
    cc = p[None, :]
    BIG = 3.0e5
    c[:, C_RDF:C_RDF + 128] = np.where(cc >= m, cc - m, BIG)
    c[:, C_RDB:C_RDB + 128] = np.where(m >= cc, m - cc, BIG)
    c[:, C_ROWF:C_ROWF + 128] = cc + 1.0
    c[:, C_ROWB:C_ROWB + 128] = 128.0 - cc
    psw = np.zeros((128, 128), np.float32)
    for d in range(128):
        psw[(d + 64) % 128, d] = 1.0
    c[:, C_PSW:C_PSW + 128] = psw
    c[:, C_ID:C_ID + 128] = np.eye(128, dtype=np.float32)
    return c


def build_wts(wplan, inp):
    out = np.empty((len(wplan), 128, 2048), np.float32)
    for i, d in enumerate(wplan):
        kind = d[0]
        if kind == 'colblk':
            _, name, li, j = d
            W = inp[name][li]
            out[i] = W[:, 128 * j:128 * j + 128].reshape(KC, 128, 128).transpose(1, 0, 2).reshape(128, 2048)
        elif kind == 'rowblk':
            _, name, li, j = d
            out[i] = inp[name][li][128 * j:128 * j + 128, :]
        elif kind == 'pool':
            _, g = d
            out[i] = inp['pool_w'][0][g].reshape(4, 128, 512).transpose(1, 0, 2).reshape(128, 2048)
        else:
            raise ValueError(kind)
    return out


_CACHE = {}


def get_prog(phases, fused):
    key = (tuple(phases), fused)
    if key not in _CACHE:
        b = Builder(list(phases), fused)
        nc = b.build()
        _CACHE[key] = (b, nc)
    return _CACHE[key]


def run_launch(phases, fused, inp, extra=None):
    b, nc = get_prog(phases, fused)
    wts = build_wts(b.wplan, inp)
    in_maps = []
    for core in range(8):
        bb, s = core // 2, core % 2
        d = core_inputs(inp, bb, s)
        d['wts'] = wts
        if extra is not None:
            d.update(extra[core])
        in_maps.append(d)
    res = run_bass_kernel_spmd(nc, in_maps, core_ids=list(range(8)))
    return res.results


def _exchange1(resA):
    ex = []
    for core in range(8):
        p0 = resA[(core // 2) * 2]['pay1']
        p1 = resA[(core // 2) * 2 + 1]['pay1']
        ex.append(np.concatenate([p0, p1], axis=0))
    return ex


def _exchange2(resB):
    ex = []
    for core in range(8):
        p0 = resB[(core // 2) * 2]['pay2']
        p1 = resB[(core // 2) * 2 + 1]['pay2']
        ex.append(np.concatenate([p0, p1], axis=0))
    return ex


FUSED = True


def kernel(**inp):
    inp = {k: np.asarray(v) for k, v in inp.items()}
    if FUSED:
        res = run_launch(['A', 'B', 'C'], True, inp)
    else:
        resA = run_launch(['A'], False, inp)
        ex1 = _exchange1(resA)
        extraB = [{'xin': resA[c]['xout'], 'xein': resA[c]['xeout'], 'pay1g': ex1[c]} for c in range(8)]
        resB = run_launch(['B'], False, inp, extraB)
        ex2 = _exchange2(resB)
        extraC = [{'xin': resB[c]['xout'], 'xein': resB[c]['xeout'], 'pay2g': ex2[c]} for c in range(8)]
        res = run_launch(['C'], False, inp, extraC)
    out = np.empty((4, 2048, D), np.float32)
    for core in range(8):
        b, s = core // 2, core % 2
        o = res[core]['outT']
        out[b, 1024 * s:1024 * s + 1024] = o.transpose(2, 1, 0).reshape(T, D)
    return out
```

```python
import numpy as np
import ml_dtypes
import concourse.bass as bass
import concourse.mybir as mybir
from concourse.bass_utils import run_bass_kernel_spmd

F32 = mybir.dt.float32
BF16 = mybir.dt.bfloat16
AF = mybir.ActivationFunctionType
ALU = mybir.AluOpType

D = 2048
KC = 16
FF = 5632
FC = 44
T = 1024
NEXT = 130
NH = 8
NSLOT = 6
NDMASEM = 12
EPS = 1e-6
GN_EPS = 1e-5
K_SCALE = 128 ** -0.5
SAME_ENGINE_SYNC = True


class Prog:
    ENG = ['pe', 'act', 'dve', 'pool', 'sp']

    def __init__(self):
        self.ops = {e: [] for e in self.ENG}
        self.count = {}
        self.waited = {e: {} for e in self.ENG}
        self.reg = {}
        self.dma_rr = 0
        self.final = []

    def _need(self, eng, tok):
        if tok is None:
            return
        k, v = tok
        if k == 'tl_' + eng and (eng == 'pe' or not SAME_ENGINE_SYNC):
            return
        if self.waited[eng].get(k, 0) < v:
            self.ops[eng].append(('wait', k, v))
            self.waited[eng][k] = v

    def _deps(self, eng, reads, writes):
        for k in reads:
            r = self.reg.get(k)
            if r is not None:
                self._need(eng, r[0])
        for k in writes:
            r = self.reg.get(k)
            if r is not None:
                self._need(eng, r[0])
                for t in r[1]:
                    self._need(eng, t)

    def _update(self, tok, reads, writes):
        for k in reads:
            r = self.reg.setdefault(k, [None, []])
            r[1].append(tok)
            if len(r[1]) > 64:
                best = {}
                for (kk, vv) in r[1]:
                    if best.get(kk, 0) < vv:
                        best[kk] = vv
                r[1] = list(best.items())
        for k in writes:
            self.reg[k] = [tok, []]

    @staticmethod
    def _is_psum(k):
        return k == 'psb' or (isinstance(k, tuple) and k[0] in ('ps', 'psb'))

    def op(self, eng, fn, reads=(), writes=()):
        ex = [k for k in reads if self._is_psum(k)]
        if ex:
            reads = [k for k in reads if not self._is_psum(k)]
            writes = list(writes) + [('psb', 0) if (k == 'psb' or k[0] == 'psb') else k for k in ex]
        writes = [('psb', 0) if (k == 'psb' or (isinstance(k, tuple) and k[0] == 'psb')) else k for k in writes]
        self._deps(eng, reads, writes)
        semk = 'tl_' + eng
        v = self.count.get(semk, 0) + 1
        self.count[semk] = v
        self.ops[eng].append(('op', fn, semk))
        tok = (semk, v)
        self._update(tok, reads, writes)
        return tok

    def dma(self, q, out, in_, reads=(), writes=()):
        self._deps(q, reads, writes)
        semk = 'dma%d' % self.dma_rr
        self.dma_rr = (self.dma_rr + 1) % NDMASEM
        prev = self.count.get(semk, 0)
        if prev:
            self._need(q, (semk, prev))
        v = prev + 16
        self.count[semk] = v
        self.ops[q].append(('dma', out, in_, semk))
        tok = (semk, v)
        self._update(tok, reads, writes)
        return tok

    def cc(self, ins, outs, groups, reads=(), writes=()):
        q = 'pool'
        self._deps(q, reads, writes)
        semk = 'ccsem'
        v = self.count.get(semk, 0) + 1
        self.count[semk] = v
        self.ops[q].append(('cc', ins, outs, groups, semk))
        tok = (semk, v)
        self._update(tok, reads, writes)
        return tok

    def finish(self, q, tok):
        self._need(q, tok)

    def emit(self, nc):
        import contextlib
        semnames = sorted(self.count.keys())
        with contextlib.ExitStack() as st:
            sems = {k: st.enter_context(nc.semaphore(k)) for k in semnames}
            block = st.enter_context(nc.Block())
            ops = self.ops

            def run(eng, e):
                for o in ops[eng]:
                    if o[0] == 'wait':
                        e.wait_ge(sems[o[1]], o[2])
                    elif o[0] == 'op':
                        f = o[1]
                        if isinstance(f, tuple):
                            ins = getattr(e, f[0])(**f[1])
                        else:
                            ins = f(e)
                        ins.then_inc(sems[o[2]], 1)
                    elif o[0] == 'dma':
                        src = o[2]() if callable(o[2]) else o[2]
                        e.dma_start(out=o[1], in_=src).then_inc(sems[o[3]], 16)
                    elif o[0] == 'cc':
                        e.collective_compute("AllGather", ALU.bypass, replica_groups=o[3],
                                             ins=o[1], outs=o[2]).then_inc(sems[o[4]])

            @block.tensor
            def _(e):
                run('pe', e)

            @block.scalar
            def _(e):
                run('act', e)

            @block.vector
            def _(e):
                run('dve', e)

            @block.gpsimd
            def _(e):
                run('pool', e)

            @block.sync
            def _(e):
                run('sp', e)


class Builder:
    def __init__(self, phases, fused):
        self.phases = phases
        self.fused = fused
        self.P = Prog()
        self.wplan = []
        self.nc = bass.Bass("TRN2", target_bir_lowering=False)
        self.ps_rr = 0
        self.rr = {}

    def rot(self, name, n):
        i = self.rr.get(name, 0)
        self.rr[name] = (i + 1) % n
        return i

    def psum(self):
        i = self.ps_rr
        self.ps_rr = (self.ps_rr + 1) % 6
        return i

    def wtile(self, desc):
        i = len(self.wplan)
        self.wplan.append(desc)
        s = i % NSLOT
        self.P.dma('pool', self.ring[:, s, :], (lambda i=i: self.wts[i, :, :]), reads=[], writes=[('ring', s)])
        return s

    def I(self, eng, name, reads=(), writes=(), **kw):
        return self.P.op(eng, (name, kw), reads=reads, writes=writes)

    def mm_group(self, ps_ap, pairs, reads, writes, transpose=False):
        n = len(pairs)

        def fn(e):
            ins = None
            for j, (l, r) in enumerate(pairs):
                ins = e.matmul(ps_ap, l, r, start=(j == 0), stop=(j == n - 1))
            return ins
        return self.P.op('pe', fn, reads=reads, writes=writes)

    def build(self):
        nc = self.nc
        P = self.P
        ph = self.phases
        import contextlib
        st = contextlib.ExitStack()
        with st:
            dt = nc.dram_tensor
            self.xin = dt("xin", [128, KC, T], F32, kind="ExternalInput").ap()
            self.xein = dt("xein", [128, KC, NEXT], F32, kind="ExternalInput").ap()
            self.vecs_d = dt("vecs", [128, NV], F32, kind="ExternalInput").ap()
            self.consts_d = dt("consts", [128, NCONST], F32, kind="ExternalInput").ap()
            self.rope_d = dt("rope", [128, 2, T], F32, kind="ExternalInput").ap()
            last = ph[-1]
            if last == 'C':
                self.out_d = dt("outT", [128, KC, T], F32, kind="ExternalOutput").ap()
            else:
                self.xout = dt("xout", [128, KC, T], F32, kind="ExternalOutput").ap()
                self.xeout = dt("xeout", [128, KC, NEXT], F32, kind="ExternalOutput").ap()
            if self.fused:
                self.pay1 = dt("pay1", [4 * NH * 128, 128], F32)
                self.pay1g = dt("pay1g", [2 * 4 * NH * 128, 128], F32)
                self.pay2 = dt("pay2", [128, KC * 16], F32)
                self.pay2g = dt("pay2g", [2 * 128, KC * 16], F32)
                self.pay1_w = self.pay1.ap() if hasattr(self.pay1, 'ap') else self.pay1
            else:
                if 'A' in ph:
                    self.pay1 = dt("pay1", [4 * NH * 128, 128], F32, kind="ExternalOutput")
                if 'B' in ph:
                    self.pay1g = dt("pay1g", [2 * 4 * NH * 128, 128], F32, kind="ExternalInput")
                    self.pay2 = dt("pay2", [128, KC * 16], F32, kind="ExternalOutput")
                if 'C' in ph:
                    self.pay2g = dt("pay2g", [2 * 128, KC * 16], F32, kind="ExternalInput")

            sb = lambda name, shape, dtype: st.enter_context(nc.sbuf_tensor(name, shape, dtype))
            self.x = sb("x", [128, KC, T], F32)
            self.xe = sb("xe", [128, KC, NEXT], F32)
            self.h = sb("h", [128, KC, T + NEXT], BF16)
            self.ring = sb("ring", [128, NSLOT, 2048], BF16)
            self.abuf = sb("abuf", [128, 2, 4, T + NEXT], BF16)
            self.vecs = sb("vecs_s", [128, NV], F32)
            self.consts = sb("consts_s", [128, NCONST], F32)
            self.wide = sb("wide", [128, 3, T + 16], F32)
            self.modraw = sb("modraw", [128, 2, 144], F32)
            self.tabA = sb("tabA", [128, 2, KC], F32)
            self.tabG = sb("tabG", [128, 2, KC], F32)
            self.sc = sb("sc", [128, KC, 2], BF16)
            self.scf = sb("scf", [128, KC, 2], F32)
            self.ones = sb("ones", [128, 128], BF16)
            self.onesf = sb("onesf", [128, 128], F32)
            self.sq = sb("sq", [128, 2, 512], BF16)
            self.f32t = sb("f32t", [128, 4, 514], F32)
            self.rstd = sb("rstd", [128, 512], F32)
            self.sg = sb("sg", [128, 2, 514], F32)
            ps = lambda name, shape, dtype: st.enter_context(nc.psum_tensor(name, shape, dtype))
            self.ps = [ps("ps%d" % i, [128, 512], F32) for i in range(7)]
            self.psb = ps("psb", [128, 1024], BF16)
            self.mix_alloc(sb)

            P.dma('sp', self.vecs[:, :], self.vecs_d[:, :], writes=['vecs'])
            P.dma('sp', self.consts[:, :], self.consts_d[:, :], writes=['consts'])
            if 'A' in ph or 'B' in ph:
                P.dma('sp', self.wide[:, 0:2, 0:T], self.rope_d[:, :, :], writes=[('wide', 0), ('wide', 1)])
            for k in range(KC):
                P.dma('sp', self.x[:, k, :], self.xin[:, k, :], writes=[('x', k, 0), ('x', k, 1)])
            P.dma('sp', self.xe[:, :, :], self.xein[:, :, :], writes=[('xe', k) for k in range(KC)])
            P.op('dve', lambda e: e.memset(self.ones[:, :], 1.0), writes=['ones'])
            P.op('dve', lambda e: e.memset(self.onesf[:, :], 1.0 / 128.0), writes=['onesf'])
            cc = self.vecs[:, V_CC:V_CC + 32]
            self.I('act', 'activation', reads=['vecs'], writes=['scf'],
                   out=self.scf[:, :, :].rearrange("p k e -> p (k e)"), in_=cc, func=AF.Silu)
            self.I('dve', 'tensor_copy', reads=['scf'], writes=['sc'], out=self.sc[:, :, :], in_=self.scf[:, :, :])

            self.main_tiles = [dict(kind='x', c0=0, n=512, half=0), dict(kind='x', c0=512, n=512, half=1)]
            self.ext_tile = dict(kind='xe', c0=0, n=NEXT)

            fused_all = (ph == ['A', 'B', 'C'])
            if 'A' in ph:
                self.adaln(0, 0, 3)
                self.adaln_begin(0, 3, 9 if fused_all else 6)
                self.ffn(0, 1, self.main_tiles + [self.ext_tile], ext_ctx=True, side=True)
                self.norm_mod(0, 'mix', self.main_tiles + [self.ext_tile], ext_ctx=True)
                self.mixer_states()
            if 'A' in ph and 'B' in ph:
                if self.fused:
                    P.cc([self.pay1[:, :]], [self.pay1g[:, :]], PAIRS, reads=['pay1'], writes=['pay1g'])
            if 'B' in ph:
                if 'A' not in ph:
                    self.adaln(0, 3, 6)
                    self.norm_mod(0, 'mix', self.main_tiles + [self.ext_tile], ext_ctx=True)
                self.mixer_main()
                if not fused_all:
                    self.adaln(0, 6, 9)
                self.adaln_begin(1, 0, 3)
                self.ffn(0, 2, self.main_tiles, side=True)
                self.adaln_begin(1, 3, 9 if fused_all else 3)
                if not fused_all:
                    self.aj = None
                self.ffn(1, 1, self.main_tiles, side=fused_all)
                self.halo_out()
            if 'B' in ph and 'C' in ph:
                if self.fused:
                    P.cc([self.pay2[:, :]], [self.pay2g[:, :]], PAIRS, reads=['pay2'], writes=['pay2g'])
            if 'C' in ph:
                if not fused_all:
                    self.adaln(1, 3, 6)
                self.pool_mixer()
                if not fused_all:
                    self.adaln(1, 6, 9)
                self.ffn(1, 2, self.main_tiles)
                self.final_norm()
            else:
                toks = []
                for k in range(KC):
                    toks.append(P.dma('sp', self.xout[:, k, :], self.x[:, k, :], reads=[('x', k, 0), ('x', k, 1)]))
                toks.append(P.dma('sp', self.xeout[:, :, :], self.xe[:, :, :], reads=[('xe', k) for k in range(KC)]))
                if 'A' in ph and not self.fused:
                    r = self.P.reg.get('pay1')
                    if r is not None and r[0] is not None:
                        toks.append(r[0])
                if 'B' in ph and not self.fused:
                    r = self.P.reg.get('pay2')
                    if r is not None and r[0] is not None:
                        toks.append(r[0])
                for t in toks:
                    P.finish('sp', t)
            self.wts = dt("wts", [len(self.wplan), 128, 2048], F32, kind="ExternalInput").ap()
            P.emit(nc)
        return nc

    def ntiles_placeholder(self):
        return self.ntiles

    def xap(self, tile, k, c0=None, c1=None):
        if c0 is None:
            c0, c1 = 0, tile['n']
        if tile['kind'] == 'x':
            return self.x[:, k, tile['c0'] + c0: tile['c0'] + c1]
        return self.xe[:, k, tile['c0'] + c0: tile['c0'] + c1]

    def xkey(self, tile, k):
        if tile['kind'] == 'x':
            return ('x', k, tile['half'])
        return ('xe', k)

    def hcol(self, tile):
        return tile['c0'] if tile['kind'] == 'x' else T + tile['c0']

    def hkey(self, tile, k):
        if tile['kind'] == 'x':
            return ('h', k, tile['half'])
        return ('he', k)

    def segs(self, tile, ext_ctx):
        if tile['kind'] == 'xe' and ext_ctx:
            return [(0, 2, 0), (2, NEXT, 1)]
        return [(0, tile['n'], 0)]

    def adaln(self, li, q0, q1):
        self.adaln_begin(li, q0, q1)
        self.adaln_work(10 ** 9)

    def adaln_begin(self, li, q0, q1):
        assert getattr(self, 'aj', None) is None
        self.aj = dict(li=li, q0=q0, q1=q1, jj=0, nj=(q1 - q0) * 16)

    def adaln_pending(self):
        aj = getattr(self, 'aj', None)
        return 0 if aj is None else aj['nj'] - aj['jj']

    def adaln_work(self, ntiles):
        aj = getattr(self, 'aj', None)
        if aj is None:
            return
        li, q0, q1 = aj['li'], aj['q0'], aj['q1']
        pb = 6
        psv = self.ps[pb]
        while ntiles > 0 and aj['jj'] < aj['nj']:
            jj = aj['jj']
            j = q0 * 16 + jj
            s = self.wtile(('colblk', 'w_mod', li, j))
            pairs = [(self.ring[:, s, kc * 128:(kc + 1) * 128], self.sc[:, kc, :]) for kc in range(KC)]
            self.mm_group(psv[:, 2 * jj:2 * jj + 2], pairs, reads=[('ring', s), 'sc'], writes=[('ps', pb)])
            aj['jj'] += 1
            ntiles -= 1
        if aj['jj'] >= aj['nj']:
            nj = aj['nj']
            pview = psv[:, 0:2 * nj].rearrange("p (j e) -> p j e", e=2)
            bm = self.vecs[:, V_BMOD + li * 144 + q0 * 16: V_BMOD + li * 144 + q1 * 16]
            for e_ in range(2):
                self.I('dve', 'tensor_tensor', reads=[('ps', pb), 'vecs'], writes=[('modraw', q) for q in range(q0, q1)],
                       out=self.modraw[:, e_, q0 * 16:q1 * 16], in0=pview[:, :, e_], in1=bm, op=ALU.add)
            self.aj = None

    def mod_tables(self, li, sub, gain_col, gate_mul, extra_gate_col=None):
        q0 = 3 * sub
        mk = [('modraw', q0), ('modraw', q0 + 1), ('modraw', q0 + 2)]
        for e_ in range(2):
            self.I('dve', 'scalar_tensor_tensor', reads=mk + ['vecs'], writes=['tabA'],
                   out=self.tabA[:, e_, :], in0=self.modraw[:, e_, (q0 + 1) * 16:(q0 + 2) * 16], scalar=1.0,
                   in1=self.vecs[:, gain_col:gain_col + 16], op0=ALU.add, op1=ALU.mult)
            if extra_gate_col is None:
                self.I('dve', 'tensor_scalar', reads=mk, writes=['tabG'],
                       out=self.tabG[:, e_, :], in0=self.modraw[:, e_, (q0 + 2) * 16:(q0 + 3) * 16],
                       scalar1=float(gate_mul), scalar2=None, op0=ALU.mult)
            else:
                self.I('dve', 'tensor_tensor', reads=mk + ['vecs'], writes=['tabG'],
                       out=self.tabG[:, e_, :], in0=self.modraw[:, e_, (q0 + 2) * 16:(q0 + 3) * 16],
                       in1=self.vecs[:, extra_gate_col:extra_gate_col + 16], op=ALU.mult)

    def norm_mod(self, li, which, tiles, ext_ctx=False):
        sub = {'ffn1': 0, 'mix': 1, 'ffn2': 2}[which]
        gain_col = {'ffn1': V_NF1, 'mix': V_NMIX, 'ffn2': V_NF2}[which] + li * 16
        gate_mul = 1.0 if which == 'mix' else 0.5
        extra = (V_PSCALE if (which == 'mix' and li == 1) else None)
        self.mod_tables(li, sub, gain_col, gate_mul, extra)
        q0 = 3 * sub
        for tile in tiles:
            n = tile['n']
            pb = self.psum()
            psv = self.ps[pb][:, 0:n]
            for k in range(KC):
                b = self.rot('sq', 2)
                self.I('act', 'activation', reads=[self.xkey(tile, k)], writes=[('sq', b)],
                       out=self.sq[:, b, 0:n], in_=self.xap(tile, k), func=AF.Square)
                self.I('pe', 'matmul', reads=[('sq', b), 'ones'], writes=[('ps', pb)],
                       out=psv, lhsT=self.ones[:, :], rhs=self.sq[:, b, 0:n], start=(k == 0), stop=(k == KC - 1))
            self.I('act', 'activation', reads=[('ps', pb), 'consts'], writes=['rstd'],
                   out=self.rstd[:, 0:n], in_=psv, func=AF.Sqrt, bias=self.consts[:, C_EPS:C_EPS + 1], scale=1.0 / D)
            self.I('dve', 'reciprocal', reads=['rstd'], writes=['rstd'], out=self.rstd[:, 0:n], in_=self.rstd[:, 0:n])
            hc0 = self.hcol(tile)
            for k in range(KC):
                b = self.rot('f32t', 4)
                self.I('dve', 'tensor_tensor', reads=[self.xkey(tile, k), 'rstd'], writes=[('f32t', b)],
                       out=self.f32t[:, b, 0:n], in0=self.xap(tile, k), in1=self.rstd[:, 0:n], op=ALU.mult)
                for (c0, c1, e_) in self.segs(tile, ext_ctx):
                    self.I('act', 'activation', reads=[('f32t', b), 'tabA', ('modraw', q0)], writes=[self.hkey(tile, k)],
                           out=self.h[:, k, hc0 + c0:hc0 + c1], in_=self.f32t[:, b, c0:c1], func=AF.Identity,
                           bias=self.modraw[:, e_, q0 * 16 + k:q0 * 16 + k + 1], scale=self.tabA[:, e_, k:k + 1])

    def akey(self, ab, c, tile):
        return ('a', ab, c, tile['kind'], tile.get('half', 0))

    def ffn(self, li, idx, tiles, ext_ctx=False, side=False):
        which = 'ffn1' if idx == 1 else 'ffn2'
        self.norm_mod(li, which, tiles, ext_ctx)
        wg, wu, wd = {1: ('ffn1_w_gate', 'ffn1_w_up', 'ffn1_w_down'), 2: ('ffn2_w_gate', 'ffn2_w_up', 'ffn2_w_down')}[idx]
        NG = FC // 4
        ticks = [2 * FC]

        def tick():
            if side and self.adaln_pending():
                self.adaln_work(-(-self.adaln_pending() // max(1, ticks[0])))
            ticks[0] -= 1

        def gate_up(g):
            ab = g % 2
            for c in range(4):
                if c > 0 or g > 0:
                    tick()
                fcn = g * 4 + c
                sg_ = self.wtile(('colblk', wg, li, fcn))
                su_ = self.wtile(('colblk', wu, li, fcn))
                for tile in tiles:
                    n = tile['n']
                    hc0 = self.hcol(tile)
                    pg = self.psum()
                    pu = self.psum()
                    hk = [self.hkey(tile, k) for k in range(KC)]
                    self.mm_group(self.ps[pg][:, 0:n],
                                  [(self.ring[:, sg_, kc * 128:(kc + 1) * 128], self.h[:, kc, hc0:hc0 + n]) for kc in range(KC)],
                                  reads=[('ring', sg_)] + hk, writes=[('ps', pg)])
                    self.mm_group(self.ps[pu][:, 0:n],
                                  [(self.ring[:, su_, kc * 128:(kc + 1) * 128], self.h[:, kc, hc0:hc0 + n]) for kc in range(KC)],
                                  reads=[('ring', su_)] + hk, writes=[('ps', pu)])
                    b = self.rot('sg', 2)
                    self.I('act', 'activation', reads=[('ps', pg)], writes=[('sg', b)],
                           out=self.sg[:, b, 0:n], in_=self.ps[pg][:, 0:n], func=AF.Silu)
                    self.I('dve', 'tensor_tensor', reads=[('sg', b), ('ps', pu)], writes=[self.akey(ab, c, tile)],
                           out=self.abuf[:, ab, c, hc0:hc0 + n], in0=self.sg[:, b, 0:n], in1=self.ps[pu][:, 0:n], op=ALU.mult)

        def down(g):
            ab = g % 2
            for db in range(4):
                sl = self.wtile(('rowblk4', wd, li, g, db))
                for tile in tiles:
                    n = tile['n']
                    hc0 = self.hcol(tile)
                    akeys = [self.akey(ab, c, tile) for c in range(4)]
                    for dq in range(4):
                        dk = db * 4 + dq
                        pd = self.psum()
                        self.mm_group(self.ps[pd][:, 0:n],
                                      [(self.ring[:, sl, c * 512 + dq * 128:c * 512 + dq * 128 + 128], self.abuf[:, ab, c, hc0:hc0 + n]) for c in range(4)],
                                      reads=[('ring', sl)] + akeys, writes=[('ps', pd)])
                        for (c0, c1, e_) in self.segs(tile, ext_ctx):
                            self.I('dve', 'scalar_tensor_tensor', reads=[('ps', pd), 'tabG', self.xkey(tile, dk)],
                                   writes=[self.xkey(tile, dk)],
                                   out=self.xap(tile, dk, c0, c1), in0=self.ps[pd][:, c0:c1], scalar=self.tabG[:, e_, dk:dk + 1],
                                   in1=self.xap(tile, dk, c0, c1), op0=ALU.mult, op1=ALU.add)
                tick()

        gate_up(0)
        for g in range(NG):
            if g + 1 < NG:
                gate_up(g + 1)
            down(g)
        if side:
            self.adaln_work(10 ** 9)

    def mix_alloc(self, sb):
        self.qk = sb("qk", [128, 2, T], BF16)
        self.qfb = sb("qfb", [128, 2, 2, 128], BF16)
        self.vh = sb("vh", [128, 9, 128], BF16)
        self.kd = sb("kd", [128, 2, 2, 128], BF16)
        self.Sst = sb("Sst", [128, 2, 128], F32)
        self.Sbf = sb("Sbf", [128, 2, 8, 128], BF16)
        self.PT = sb("PT", [128, 2, 128], BF16)
        self.pst = sb("pst", [128, 6, 128], F32)
        self.Dh = sb("Dh", [128, 4, 128], F32)
        self.dec = sb("dec", [128, 6, 16], F32)
        self.identb = sb("identb", [128, 128], BF16)
        self.dec_done = False

    def mt(self, i):
        if i < 4:
            return self.f32t[:, i, :], ('f32t', i)
        return self.sg[:, i - 4, :], ('sg', i - 4)

    def dec_setup(self):
        if self.dec_done:
            return
        self.dec_done = True
        I = self.I
        raw = self.vecs[:, V_DEC:V_DEC + 16]
        LG, KD, CDt, CD8, AL, TMP = [self.dec[:, i, :] for i in range(6)]
        cst = self.consts
        I('act', 'activation', reads=['vecs'], writes=['dec'], out=TMP, in_=raw, func=AF.Exp, scale=-1.0)
        I('act', 'activation', reads=['dec', 'consts'], writes=['dec'], out=TMP, in_=TMP, func=AF.Ln,
          bias=cst[:, C_ONE:C_ONE + 1], scale=1.0)
        I('dve', 'tensor_scalar', reads=['dec'], writes=['dec'], out=LG, in0=TMP, scalar1=-1.0, scalar2=None, op0=ALU.mult)
        I('dve', 'tensor_scalar', reads=['dec', 'consts'], writes=['dec'], out=TMP[:, 0:8], in0=LG[:, 0:8],
          scalar1=cst[:, C_127MP:C_127MP + 1], scalar2=None, op0=ALU.mult)
        I('dve', 'tensor_scalar', reads=['dec', 'consts'], writes=['dec'], out=TMP[:, 8:16], in0=LG[:, 8:16],
          scalar1=cst[:, C_P:C_P + 1], scalar2=None, op0=ALU.mult)
        I('act', 'activation', reads=['dec', 'consts'], writes=['dec'], out=KD, in_=TMP, func=AF.Exp,
          bias=cst[:, C_LNK:C_LNK + 1], scale=1.0)
        I('act', 'activation', reads=['dec'], writes=['dec'], out=CDt, in_=LG, func=AF.Exp, scale=128.0)
        I('act', 'activation', reads=['dec'], writes=['dec'], out=CD8, in_=LG, func=AF.Exp, scale=1024.0)
        sel0 = self.vecs[:, V_SEL:V_SEL + 1]
        sel1 = self.vecs[:, V_SEL + 1:V_SEL + 2]
        I('dve', 'tensor_scalar', reads=['dec', 'vecs'], writes=['dec'], out=AL[:, 0:8], in0=CD8[:, 0:8],
          scalar1=sel1, scalar2=sel0, op0=ALU.mult, op1=ALU.add)
        I('dve', 'tensor_scalar', reads=['dec', 'vecs'], writes=['dec'], out=AL[:, 8:16], in0=CD8[:, 8:16],
          scalar1=sel0, scalar2=sel1, op0=ALU.mult, op1=ALU.add)
        I('dve', 'tensor_copy', reads=['consts'], writes=['identb'], out=self.identb[:, :], in_=cst[:, C_ID:C_ID + 128])

    def LGc(self, d, hd):
        return self.dec[:, 0, 8 * d + hd:8 * d + hd + 1]

    def KDc(self, d, hd):
        return self.dec[:, 1, 8 * d + hd:8 * d + hd + 1]

    def CDc(self, d, hd):
        return self.dec[:, 2, 8 * d + hd:8 * d + hd + 1]

    def ALc(self, d, hd):
        return self.dec[:, 4, 8 * d + hd:8 * d + hd + 1]

    def proj_rope(self, slot, dst):
        I = self.I
        for half in range(2):
            c0 = half * 512
            pq = self.psum()
            self.mm_group(self.ps[pq][:, :],
                          [(self.ring[:, slot, kc * 128:(kc + 1) * 128], self.h[:, kc, c0:c0 + 512]) for kc in range(KC)],
                          reads=[('ring', slot)] + [('h', k, half) for k in range(KC)], writes=[('ps', pq)])
            m0, k0 = self.mt(0)
            m1, k1 = self.mt(1)
            m2, k2 = self.mt(2)
            I('act', 'activation', reads=[('ps', pq)], writes=[k0], out=m0[:, 0:512], in_=self.ps[pq][:, :], func=AF.Copy)
            pr = self.psum()
            I('pe', 'matmul', reads=[k0, 'consts'], writes=[('ps', pr)], out=self.ps[pr][:, :],
              lhsT=self.consts[:, C_PSW:C_PSW + 128], rhs=m0[:, 0:512], start=True, stop=True)
            I('dve', 'tensor_tensor', reads=[k0, ('wide', 0)], writes=[k1], out=m1[:, 0:512], in0=m0[:, 0:512],
              in1=self.wide[:, 0, c0:c0 + 512], op=ALU.mult)
            I('dve', 'tensor_tensor', reads=[('ps', pr), ('wide', 1)], writes=[k2], out=m2[:, 0:512], in0=self.ps[pr][:, :],
              in1=self.wide[:, 1, c0:c0 + 512], op=ALU.mult)
            I('dve', 'tensor_tensor', reads=[k1, k2], writes=[('qk', dst, half)], out=self.qk[:, dst, c0:c0 + 512],
              in0=m1[:, 0:512], in1=m2[:, 0:512], op=ALU.add)

    def v_proj(self, slot):
        I = self.I
        for grp in range(3):
            ns = [0, 1, 2, 3] if grp == 0 else ([4, 5, 6, 7] if grp == 1 else [8])
            pv = self.psum()
            for j, n in enumerate(ns):
                if n < 8:
                    cols = (n * 128, n * 128 + 128)
                    hk = [('h', k, n // 4) for k in range(KC)]
                else:
                    cols = (T + 2, T + 130)
                    hk = [('he', k) for k in range(KC)]
                self.mm_group(self.ps[pv][:, j * 128:(j + 1) * 128],
                              [(self.h[:, kc, cols[0]:cols[1]], self.ring[:, slot, kc * 128:(kc + 1) * 128]) for kc in range(KC)],
                              reads=[('ring', slot)] + hk, writes=[('ps', pv)])
            for j, n in enumerate(ns):
                I('act', 'activation', reads=[('ps', pv)], writes=[('vh', n)], out=self.vh[:, n, :],
                  in_=self.ps[pv][:, j * 128:(j + 1) * 128], func=AF.Copy)

    def kv_mm(self, d, r, n):
        pk = self.psum()
        self.I('pe', 'matmul', reads=[('kd', d, r), ('vh', n)], writes=[('ps', pk)], out=self.ps[pk][:, 0:128],
               lhsT=self.kd[:, d, r, :], rhs=self.vh[:, n, :], start=True, stop=True)
        return pk

    def ctx_kd(self, hd, slot):
        I = self.I
        pc = self.psum()
        self.mm_group(self.ps[pc][:, 0:128],
                      [(self.h[:, kc, T + 2:T + 130], self.ring[:, slot, kc * 128:(kc + 1) * 128]) for kc in range(KC)],
                      reads=[('ring', slot)] + [('he', k) for k in range(KC)], writes=[('ps', pc)])
        r = self.rot('kd', 2)
        I('act', 'activation', reads=[('ps', pc), 'dec'], writes=[('kd', 0, r)], out=self.kd[:, 0, r, :],
          in_=self.ps[pc][:, 0:128], func=AF.Identity, scale=self.KDc(0, hd))
        I('dve', 'tensor_scalar', reads=[('ps', pc), 'dec'], writes=[('kd', 1, r)], out=self.kd[:, 1, r, :],
          in0=self.ps[pc][:, 0:128], scalar1=self.KDc(1, hd), scalar2=None, op0=ALU.mult)
        return r

    def mixer_states(self):
        I = self.I
        self.dec_setup()
        pay = self.pay1.ap() if hasattr(self.pay1, 'ap') else self.pay1
        payv = pay.rearrange("(k h p) v -> h p k v", k=4, h=NH, p=128)
        for hd in range(NH):
            sk = self.wtile(('colblk', 'mix_w_in', 0, 8 + hd))
            sv = self.wtile(('colblk', 'mix_w_in', 0, 16 + hd))
            self.proj_rope(sk, 1)
            self.v_proj(sv)
            r = self.ctx_kd(hd, sk)
            for d in range(2):
                pk = self.kv_mm(d, r, 8)
                I('act', 'activation', reads=[('ps', pk)], writes=[('pst', d)], out=self.pst[:, d, :],
                  in_=self.ps[pk][:, 0:128], func=AF.Copy)
            rs = {}
            for n in range(8):
                rs[n] = None
            order_f = list(range(8))
            order_b = list(range(7, -1, -1))
            first = [True, True]
            for step in range(8):
                for d, n in ((0, order_f[step]), (1, order_b[step])):
                    r = self.k_tm_chunk_dir(hd, n, d)
                    pk = self.kv_mm(d, r, n)
                    if first[d]:
                        I('dve', 'tensor_copy', reads=[('ps', pk)], writes=[('pst', 2 + d)], out=self.pst[:, 2 + d, :],
                          in_=self.ps[pk][:, 0:128])
                        first[d] = False
                    else:
                        I('dve', 'scalar_tensor_tensor', reads=[('ps', pk), ('pst', 2 + d), 'dec'], writes=[('pst', 2 + d)],
                          out=self.pst[:, 2 + d, :], in0=self.pst[:, 2 + d, :], scalar=self.CDc(d, hd),
                          in1=self.ps[pk][:, 0:128], op0=ALU.mult, op1=ALU.add)
            self.P.dma('sp', payv[hd], self.pst[:, 0:4, :], reads=[('pst', i) for i in range(4)], writes=['pay1'])

    def k_tm_chunk_dir(self, hd, n, d):
        I = self.I
        r = self.rot('kd%d' % d, 2)
        I('pe', 'transpose', reads=[('qk', 1, n // 4), 'identb'], writes=[('psb', n)],
          out=self.psb[:, n * 128:(n + 1) * 128], in_=self.qk[:, 1, n * 128:(n + 1) * 128], identity=self.identb[:, :])
        if d == 0:
            I('act', 'activation', reads=[('psb', n), 'dec'], writes=[('kd', d, r)], out=self.kd[:, d, r, :],
              in_=self.psb[:, n * 128:(n + 1) * 128], func=AF.Identity, scale=self.KDc(d, hd))
        else:
            I('dve', 'tensor_scalar', reads=[('psb', n), 'dec'], writes=[('kd', d, r)], out=self.kd[:, d, r, :],
              in0=self.psb[:, n * 128:(n + 1) * 128], scalar1=self.KDc(d, hd), scalar2=None, op0=ALU.mult)
        return r

    def head_tables(self, hd):
        I = self.I
        cst = self.consts
        lnk = cst[:, C_LNK:C_LNK + 1]
        I('act', 'activation', reads=['consts', 'dec'], writes=[('Dh', 0)], out=self.Dh[:, 0, :], in_=cst[:, C_RDF:C_RDF + 128],
          func=AF.Exp, bias=lnk, scale=self.LGc(0, hd))
        I('act', 'activation', reads=['consts', 'dec'], writes=[('Dh', 1)], out=self.Dh[:, 1, :], in_=cst[:, C_RDB:C_RDB + 128],
          func=AF.Exp, bias=lnk, scale=self.LGc(1, hd))
        I('dve', 'tensor_tensor', reads=[('Dh', 0), ('Dh', 1)], writes=[('Dh', 0)], out=self.Dh[:, 0, :], in0=self.Dh[:, 0, :],
          in1=self.Dh[:, 1, :], op=ALU.add)
        I('act', 'activation', reads=['consts', 'dec'], writes=[('Dh', 2)], out=self.Dh[:, 2, :], in_=cst[:, C_ROWF:C_ROWF + 128],
          func=AF.Exp, scale=self.LGc(0, hd))
        I('act', 'activation', reads=['consts', 'dec'], writes=[('Dh', 3)], out=self.Dh[:, 3, :], in_=cst[:, C_ROWB:C_ROWB + 128],
          func=AF.Exp, scale=self.LGc(1, hd))

    def mix_down(self, g):
        ab = g % 2
        for db in range(4):
            sl = self.wtile(('rowblk4', 'mix_w_out', 0, g, db))
            for tile in self.main_tiles:
                c0 = tile['c0']
                akeys = [self.akey(ab, c, tile) for c in range(4)]
                for dq in range(4):
                    dk = db * 4 + dq
                    pd = self.psum()
                    self.mm_group(self.ps[pd][:, :],
                                  [(self.ring[:, sl, c * 512 + dq * 128:c * 512 + dq * 128 + 128], self.abuf[:, ab, c, c0:c0 + 512]) for c in range(4)],
                                  reads=[('ring', sl)] + akeys, writes=[('ps', pd)])
                    self.I('dve', 'scalar_tensor_tensor', reads=[('ps', pd), 'tabG', self.xkey(tile, dk)], writes=[self.xkey(tile, dk)],
                           out=self.xap(tile, dk), in0=self.ps[pd][:, :], scalar=self.tabG[:, 0, dk:dk + 1],
                           in1=self.xap(tile, dk), op0=ALU.mult, op1=ALU.add)

    def mixer_main(self):
        I = self.I
        self.dec_setup()
        payg = self.pay1g.ap() if hasattr(self.pay1g, 'ap') else self.pay1g
        pgv = payg.rearrange("(r k h p) v -> r k h p v", r=2, k=4, h=NH, p=128)
        sel0 = self.vecs[:, V_SEL:V_SEL + 1]
        sel1 = self.vecs[:, V_SEL + 1:V_SEL + 2]
        for hd in range(NH):
            g = hd // 4
            c = hd % 4
            ab = g % 2
            sq_ = self.wtile(('colblk', 'mix_w_in', 0, hd))
            sk = self.wtile(('colblk', 'mix_w_in', 0, 8 + hd))
            sv = self.wtile(('colblk', 'mix_w_in', 0, 16 + hd))
            sgt = self.wtile(('colblk', 'mix_w_in', 0, 24 + hd))
            srcs = [(0, 0), (1, 0), (0, 1), (1, 1), (0, 2), (1, 3)]
            for i, (rk, kind) in enumerate(srcs):
                self.P.dma('sp', self.pst[:, i, :], pgv[rk, kind, hd], reads=['pay1g'], writes=[('pst', i)])
            I('dve', 'scalar_tensor_tensor', reads=[('pst', 0), ('pst', 1), 'dec'], writes=[('Sst', 0)], out=self.Sst[:, 0, :],
              in0=self.pst[:, 0, :], scalar=self.CDc(0, hd), in1=self.pst[:, 1, :], op0=ALU.mult, op1=ALU.add)
            I('dve', 'tensor_scalar', reads=[('Sst', 0), 'dec'], writes=[('Sst', 0)], out=self.Sst[:, 0, :], in0=self.Sst[:, 0, :],
              scalar1=self.ALc(0, hd), scalar2=None, op0=ALU.mult)
            I('dve', 'scalar_tensor_tensor', reads=[('Sst', 0), ('pst', 4), 'vecs'], writes=[('Sst', 0)], out=self.Sst[:, 0, :],
              in0=self.pst[:, 4, :], scalar=sel1, in1=self.Sst[:, 0, :], op0=ALU.mult, op1=ALU.add)
            I('dve', 'scalar_tensor_tensor', reads=[('pst', 2), ('pst', 3), 'dec'], writes=[('Sst', 1)], out=self.Sst[:, 1, :],
              in0=self.pst[:, 3, :], scalar=self.CDc(1, hd), in1=self.pst[:, 2, :], op0=ALU.mult, op1=ALU.add)
            I('dve', 'tensor_scalar', reads=[('Sst', 1), 'dec'], writes=[('Sst', 1)], out=self.Sst[:, 1, :], in0=self.Sst[:, 1, :],
              scalar1=self.ALc(1, hd), scalar2=None, op0=ALU.mult)
            I('dve', 'scalar_tensor_tensor', reads=[('Sst', 1), ('pst', 5), 'vecs'], writes=[('Sst', 1)], out=self.Sst[:, 1, :],
              in0=self.pst[:, 5, :], scalar=sel0, in1=self.Sst[:, 1, :], op0=ALU.mult, op1=ALU.add)
            self.head_tables(hd)
            self.proj_rope(sq_, 0)
            self.proj_rope(sk, 1)
            self.v_proj_main(sv)
            I('act', 'activation', reads=[('Sst', 0)], writes=[('Sbf', 0, 0)], out=self.Sbf[:, 0, 0, :], in_=self.Sst[:, 0, :], func=AF.Copy)
            I('act', 'activation', reads=[('Sst', 1)], writes=[('Sbf', 1, 7)], out=self.Sbf[:, 1, 7, :], in_=self.Sst[:, 1, :], func=AF.Copy)
            for step in range(7):
                for d, n in ((0, step), (1, 7 - step)):
                    r = self.k_tm_chunk_dir(hd, n, d)
                    pk = self.kv_mm(d, r, n)
                    I('dve', 'scalar_tensor_tensor', reads=[('ps', pk), ('Sst', d), 'dec'], writes=[('Sst', d)],
                      out=self.Sst[:, d, :], in0=self.Sst[:, d, :], scalar=self.CDc(d, hd), in1=self.ps[pk][:, 0:128],
                      op0=ALU.mult, op1=ALU.add)
                    nn = n + 1 if d == 0 else n - 1
                    I('act', 'activation', reads=[('Sst', d)], writes=[('Sbf', d, nn)], out=self.Sbf[:, d, nn, :],
                      in_=self.Sst[:, d, :], func=AF.Copy)
            for half in range(2):
                po = self.psum()
                for j in range(4):
                    n = half * 4 + j
                    cs = slice(n * 128, (n + 1) * 128)
                    psc = self.psum()
                    I('pe', 'matmul', reads=[('qk', 0, half), ('qk', 1, half)], writes=[('ps', psc)], out=self.ps[psc][:, 0:128],
                      lhsT=self.qk[:, 1, cs], rhs=self.qk[:, 0, cs], start=True, stop=True)
                    r = self.rot('PT', 2)
                    I('dve', 'tensor_tensor', reads=[('ps', psc), ('Dh', 0)], writes=[('PT', r)], out=self.PT[:, r, :],
                      in0=self.ps[psc][:, 0:128], in1=self.Dh[:, 0, :], op=ALU.mult)
                    rq = self.rot('qfb', 2)
                    I('dve', 'tensor_tensor', reads=[('qk', 0, half), ('Dh', 2)], writes=[('qfb', 0, rq)], out=self.qfb[:, 0, rq, :],
                      in0=self.qk[:, 0, cs], in1=self.Dh[:, 2, :], op=ALU.mult)
                    I('dve', 'tensor_tensor', reads=[('qk', 0, half), ('Dh', 3)], writes=[('qfb', 1, rq)], out=self.qfb[:, 1, rq, :],
                      in0=self.qk[:, 0, cs], in1=self.Dh[:, 3, :], op=ALU.mult)
                    self.mm_group(self.ps[po][:, j * 128:(j + 1) * 128],
                                  [(self.vh[:, n, :], self.PT[:, r, :]),
                                   (self.Sbf[:, 0, n, :], self.qfb[:, 0, rq, :]),
                                   (self.Sbf[:, 1, n, :], self.qfb[:, 1, rq, :])],
                                  reads=[('vh', n), ('PT', r), ('Sbf', 0, n), ('Sbf', 1, n), ('qfb', 0, rq), ('qfb', 1, rq)],
                                  writes=[('ps', po)])
                m0, k0 = self.mt(0)
                m1, k1 = self.mt(1)
                m2, k2 = self.mt(2)
                m3, k3 = self.mt(3)
                I('act', 'activation', reads=[('ps', po)], writes=[k0], out=m0[:, 0:512], in_=self.ps[po][:, :], func=AF.Copy)
                pm = self.psum()
                I('pe', 'matmul', reads=[k0, 'onesf'], writes=[('ps', pm)], out=self.ps[pm][:, :], lhsT=self.onesf[:, :],
                  rhs=m0[:, 0:512], start=True, stop=True)
                I('dve', 'tensor_tensor', reads=[k0, ('ps', pm)], writes=[k1], out=m1[:, 0:512], in0=m0[:, 0:512],
                  in1=self.ps[pm][:, :], op=ALU.subtract)
                I('act', 'activation', reads=[k1], writes=[k2], out=m2[:, 0:512], in_=m1[:, 0:512], func=AF.Square)
                pvv = self.psum()
                I('pe', 'matmul', reads=[k2, 'onesf'], writes=[('ps', pvv)], out=self.ps[pvv][:, :], lhsT=self.onesf[:, :],
                  rhs=m2[:, 0:512], start=True, stop=True)
                I('act', 'activation', reads=[('ps', pvv), 'consts'], writes=[k2], out=m2[:, 0:512], in_=self.ps[pvv][:, :],
                  func=AF.Sqrt, bias=self.consts[:, C_GNEPS:C_GNEPS + 1], scale=1.0)
                I('dve', 'reciprocal', reads=[k2], writes=[k2], out=m2[:, 0:512], in_=m2[:, 0:512])
                I('dve', 'tensor_tensor', reads=[k1, k2], writes=[k1], out=m1[:, 0:512], in0=m1[:, 0:512], in1=m2[:, 0:512], op=ALU.mult)
                pg = self.psum()
                c0 = half * 512
                self.mm_group(self.ps[pg][:, :],
                              [(self.ring[:, sgt, kc * 128:(kc + 1) * 128], self.h[:, kc, c0:c0 + 512]) for kc in range(KC)],
                              reads=[('ring', sgt)] + [('h', k, half) for k in range(KC)], writes=[('ps', pg)])
                I('act', 'activation', reads=[('ps', pg)], writes=[k3], out=m3[:, 0:512], in_=self.ps[pg][:, :], func=AF.Silu)
                I('dve', 'tensor_tensor', reads=[k1, k3], writes=[self.akey(ab, c, self.main_tiles[half])],
                  out=self.abuf[:, ab, c, c0:c0 + 512], in0=m1[:, 0:512], in1=m3[:, 0:512], op=ALU.mult)
            if hd == 7:
                self.mix_down(0)
        wcv = self.vecs[:, V_WCONV:V_WCONV + 24].rearrange("p (t j) -> p t j", t=3)
        for j in range(8):
            g = 2 + j // 4
            c = j % 4
            ab = g % 2
            sB = self.wtile(('colblk', 'mix_w_in', 0, 32 + j))
            sC = self.wtile(('colblk', 'mix_w_in', 0, 40 + j))
            sU = self.wtile(('colblk', 'mix_w_in', 0, 48 + j))
            cu = [self.mt(0), self.mt(1)]
            mu, ku = self.mt(2)
            for half in range(2):
                c0 = half * 512
                hk = [('h', k, half) for k in range(KC)]
                pC = self.psum()
                pU = self.psum()
                self.mm_group(self.ps[pC][:, :], [(self.ring[:, sC, kc * 128:(kc + 1) * 128], self.h[:, kc, c0:c0 + 512]) for kc in range(KC)],
                              reads=[('ring', sC)] + hk, writes=[('ps', pC)])
                self.mm_group(self.ps[pU][:, :], [(self.ring[:, sU, kc * 128:(kc + 1) * 128], self.h[:, kc, c0:c0 + 512]) for kc in range(KC)],
                              reads=[('ring', sU)] + hk, writes=[('ps', pU)])
                I('act', 'activation', reads=[('ps', pU)], writes=[ku], out=mu[:, 0:512], in_=self.ps[pU][:, :], func=AF.Copy)
                I('dve', 'tensor_tensor', reads=[('ps', pC), ku], writes=[cu[half][1]], out=cu[half][0][:, 1:513],
                  in0=self.ps[pC][:, :], in1=mu[:, 0:512], op=ALU.mult)
            pH = self.psum()
            hk = [('he', k) for k in range(KC)]
            self.mm_group(self.ps[pH][:, 0:2], [(self.ring[:, sC, kc * 128:(kc + 1) * 128], self.h[:, kc, T:T + 2]) for kc in range(KC)],
                          reads=[('ring', sC)] + hk, writes=[('ps', pH)])
            pH2 = self.psum()
            self.mm_group(self.ps[pH2][:, 0:2], [(self.ring[:, sU, kc * 128:(kc + 1) * 128], self.h[:, kc, T:T + 2]) for kc in range(KC)],
                          reads=[('ring', sU)] + hk, writes=[('ps', pH2)])
            I('act', 'activation', reads=[('ps', pH2)], writes=[ku], out=mu[:, 0:2], in_=self.ps[pH2][:, 0:2], func=AF.Copy)
            I('dve', 'tensor_tensor', reads=[('ps', pH), ku], writes=[ku], out=mu[:, 2:4], in0=self.ps[pH][:, 0:2], in1=mu[:, 0:2], op=ALU.mult)
            I('dve', 'tensor_tensor', reads=[ku, 'vecs'], writes=[ku], out=mu[:, 4:6], in0=mu[:, 2:4],
              in1=self.vecs[:, V_SEL + 2:V_SEL + 4], op=ALU.mult)
            I('dve', 'tensor_copy', reads=[ku], writes=[cu[0][1]], out=cu[0][0][:, 0:1], in_=mu[:, 4:5])
            I('dve', 'tensor_copy', reads=[ku], writes=[cu[1][1]], out=cu[1][0][:, 513:514], in_=mu[:, 5:6])
            I('dve', 'tensor_copy', reads=[cu[1][1]], writes=[cu[0][1]], out=cu[0][0][:, 513:514], in_=cu[1][0][:, 1:2])
            I('dve', 'tensor_copy', reads=[cu[0][1]], writes=[cu[1][1]], out=cu[1][0][:, 0:1], in_=cu[0][0][:, 512:513])
            for half in range(2):
                c0 = half * 512
                mc_, kc_ = self.mt(3)
                src = cu[half][0]
                I('dve', 'tensor_scalar', reads=[cu[half][1], 'vecs'], writes=[kc_], out=mc_[:, 0:512], in0=src[:, 1:513],
                  scalar1=wcv[:, 1, j:j + 1], scalar2=None, op0=ALU.mult)
                I('dve', 'scalar_tensor_tensor', reads=[cu[half][1], kc_, 'vecs'], writes=[kc_], out=mc_[:, 0:512], in0=src[:, 0:512],
                  scalar=wcv[:, 0, j:j + 1], in1=mc_[:, 0:512], op0=ALU.mult, op1=ALU.add)
                I('dve', 'scalar_tensor_tensor', reads=[cu[half][1], kc_, 'vecs'], writes=[kc_], out=mc_[:, 0:512], in0=src[:, 2:514],
                  scalar=wcv[:, 2, j:j + 1], in1=mc_[:, 0:512], op0=ALU.mult, op1=ALU.add)
                pB = self.psum()
                self.mm_group(self.ps[pB][:, :], [(self.ring[:, sB, kc * 128:(kc + 1) * 128], self.h[:, kc, c0:c0 + 512]) for kc in range(KC)],
                              reads=[('ring', sB)] + [('h', k, half) for k in range(KC)], writes=[('ps', pB)])
                I('dve', 'tensor_tensor', reads=[('ps', pB), kc_], writes=[self.akey(ab, c, self.main_tiles[half])],
                  out=self.abuf[:, ab, c, c0:c0 + 512], in0=self.ps[pB][:, :], in1=mc_[:, 0:512], op=ALU.mult)
            if j == 3:
                self.mix_down(1)
            if j == 7:
                self.mix_down(2)
        self.mix_down(3)

    def v_proj_main(self, slot):
        I = self.I
        for grp in range(2):
            ns = [0, 1, 2, 3] if grp == 0 else [4, 5, 6, 7]
            pv = self.psum()
            for j, n in enumerate(ns):
                self.mm_group(self.ps[pv][:, j * 128:(j + 1) * 128],
                              [(self.h[:, kc, n * 128:n * 128 + 128], self.ring[:, slot, kc * 128:(kc + 1) * 128]) for kc in range(KC)],
                              reads=[('ring', slot)] + [('h', k, n // 4) for k in range(KC)], writes=[('ps', pv)])
            for j, n in enumerate(ns):
                I('act', 'activation', reads=[('ps', pv)], writes=[('vh', n)], out=self.vh[:, n, :],
                  in_=self.ps[pv][:, j * 128:(j + 1) * 128], func=AF.Copy)

    def halo_out(self):
        pay = self.pay2.ap() if hasattr(self.pay2, 'ap') else self.pay2
        pv = pay.rearrange("p (k t) -> p k t", k=KC)
        xk = [('x', k, 0) for k in range(KC)] + [('x', k, 1) for k in range(KC)]
        self.P.dma('sp', pv[:, :, 0:8], self.x[:, :, 0:8], reads=xk, writes=['pay2'])
        self.P.dma('sp', pv[:, :, 8:16], self.x[:, :, T - 8:T], reads=xk, writes=['pay2'])

    def pool_mixer(self):
        I = self.I
        payg = self.pay2g.ap() if hasattr(self.pay2g, 'ap') else self.pay2g
        pgv = payg.rearrange("(r p) (k t) -> r p k t", r=2, k=KC)
        xek = [('xe', k) for k in range(KC)]
        self.P.dma('sp', self.xe[:, :, 0:8], pgv[0][:, :, 8:16], reads=['pay2g'], writes=xek)
        self.P.dma('sp', self.xe[:, :, 8:16], pgv[1][:, :, 0:8], reads=['pay2g'], writes=xek)
        halo_tile = dict(kind='xe', c0=0, n=16)
        self.norm_mod(1, 'mix', self.main_tiles + [halo_tile])
        W = 8 + T + 8
        maskL = self.vecs[:, V_SEL + 2:V_SEL + 3]
        maskR = self.vecs[:, V_SEL + 3:V_SEL + 4]
        pf = self.vecs[:, V_PFAC:V_PFAC + 64].rearrange("p (w t) -> p w t", w=4)
        for g in range(4):
            ab = g % 2
            w = 2 << g
            for c in range(4):
                k = 4 * g + c
                hp = self.wide[:, 0, :]
                I('act', 'activation', reads=[('h', k, 0), ('h', k, 1)], writes=[('wide', 0)], out=hp[:, 8:8 + T], in_=self.h[:, k, 0:T], func=AF.Copy)
                I('dve', 'tensor_scalar', reads=[('he', k), 'vecs'], writes=[('wide', 0)], out=hp[:, 0:8], in0=self.h[:, k, T:T + 8],
                  scalar1=maskL, scalar2=None, op0=ALU.mult)
                I('dve', 'tensor_scalar', reads=[('he', k), 'vecs'], writes=[('wide', 0)], out=hp[:, 8 + T:W], in0=self.h[:, k, T + 8:T + 16],
                  scalar1=maskR, scalar2=None, op0=ALU.mult)
                a_, b_ = self.wide[:, 1, :], self.wide[:, 2, :]
                I('dve', 'tensor_tensor', reads=[('wide', 0)], writes=[('wide', 1)], out=a_[:, 1:W], in0=hp[:, 0:W - 1], in1=hp[:, 1:W], op=ALU.add)
                cur, ck, oth, ok = a_, ('wide', 1), b_, ('wide', 2)
                lo, hi, sh = 1, W, 1
                for lvl in range(g):
                    nlo, nhi = lo + sh, hi - sh
                    I('dve', 'tensor_tensor', reads=[ck], writes=[ok], out=oth[:, nlo:nhi], in0=cur[:, nlo - sh:nhi - sh],
                      in1=cur[:, nlo + sh:nhi + sh], op=ALU.add)
                    cur, ck, oth, ok = oth, ok, cur, ck
                    lo, hi, sh = nlo, nhi, sh * 2
                tile0 = self.main_tiles[0]
                I('dve', 'scalar_tensor_tensor', reads=[ck, ('wide', 0)], writes=[ok], out=oth[:, 8:8 + T], in0=cur[:, 8:8 + T],
                  scalar=1.0 / w, in1=hp[:, 8:8 + T], op0=ALU.mult, op1=ALU.subtract)
                for (a0, t0) in ((8, 0), (T, 8)):
                    I('dve', 'tensor_tensor', reads=[ck, 'vecs'], writes=[ck], out=cur[:, a0:a0 + 8], in0=cur[:, a0:a0 + 8],
                      in1=pf[:, g, t0:t0 + 8], op=ALU.mult)
                    I('dve', 'tensor_tensor', reads=[ck, ('wide', 0), ok], writes=[ok], out=oth[:, a0:a0 + 8], in0=cur[:, a0:a0 + 8],
                      in1=hp[:, a0:a0 + 8], op=ALU.subtract)
                I('act', 'activation', reads=[ok], writes=[self.akey(ab, c, self.main_tiles[0]), self.akey(ab, c, self.main_tiles[1])],
                  out=self.abuf[:, ab, c, 0:T], in_=oth[:, 8:8 + T], func=AF.Copy)
            sp_ = self.wtile(('pool', g))
            for tile in self.main_tiles:
                c0 = tile['c0']
                akeys = [self.akey(ab, c, tile) for c in range(4)]
                for do in range(4):
                    dk = 4 * g + do
                    pd = self.psum()
                    self.mm_group(self.ps[pd][:, :],
                                  [(self.ring[:, sp_, i * 512 + do * 128:i * 512 + do * 128 + 128], self.abuf[:, ab, i, c0:c0 + 512]) for i in range(4)],
                                  reads=[('ring', sp_)] + akeys, writes=[('ps', pd)])
                    I('dve', 'scalar_tensor_tensor', reads=[('ps', pd), 'tabG', self.xkey(tile, dk)], writes=[self.xkey(tile, dk)],
                      out=self.xap(tile, dk), in0=self.ps[pd][:, :], scalar=self.tabG[:, 0, dk:dk + 1],
                      in1=self.xap(tile, dk), op0=ALU.mult, op1=ALU.add)

    def final_norm(self):
        I = self.I
        toks = []
        for tile in self.main_tiles:
            n = tile['n']
            c0 = tile['c0']
            pb = self.psum()
            psv = self.ps[pb][:, 0:n]
            for k in range(KC):
                b = self.rot('sq', 2)
                I('act', 'activation', reads=[self.xkey(tile, k)], writes=[('sq', b)], out=self.sq[:, b, 0:n], in_=self.xap(tile, k), func=AF.Square)
                I('pe', 'matmul', reads=[('sq', b), 'ones'], writes=[('ps', pb)], out=psv, lhsT=self.ones[:, :], rhs=self.sq[:, b, 0:n],
                  start=(k == 0), stop=(k == KC - 1))
            I('act', 'activation', reads=[('ps', pb), 'consts'], writes=['rstd'], out=self.rstd[:, 0:n], in_=psv, func=AF.Sqrt,
              bias=self.consts[:, C_EPS:C_EPS + 1], scale=1.0 / D)
            I('dve', 'reciprocal', reads=['rstd'], writes=['rstd'], out=self.rstd[:, 0:n], in_=self.rstd[:, 0:n])
            for k in range(KC):
                b = self.rot('f32t', 4)
                I('dve', 'scalar_tensor_tensor', reads=[self.xkey(tile, k), 'rstd', 'vecs'], writes=[('f32t', b)],
                  out=self.f32t[:, b, 0:n], in0=self.xap(tile, k), scalar=self.vecs[:, V_FNORM + k:V_FNORM + k + 1],
                  in1=self.rstd[:, 0:n], op0=ALU.mult, op1=ALU.mult)
                toks.append(self.P.dma('sp', self.out_d[:, k, c0:c0 + n], self.f32t[:, b, 0:n], reads=[('f32t', b)], writes=[('out', k, c0)]))
        for t in toks:
            self.P.finish('sp', t)


V_CC = 0
V_BMOD = V_CC + 32
V_NF1 = V_BMOD + 288
V_NMIX = V_NF1 + 32
V_NF2 = V_NMIX + 32
V_PSCALE = V_NF2 + 32
V_FNORM = V_PSCALE + 16
V_WCONV = V_FNORM + 16
V_DEC = V_WCONV + 24
V_SEL = V_DEC + 16
V_PFAC = V_SEL + 4
NV = V_PFAC + 64
C_EPS = 0
C_GNEPS = 1
C_ONE = 2
C_LNK = 3
C_127MP = 4
C_P = 5
C_RDF = 8
C_RDB = C_RDF + 128
C_ROWF = C_RDB + 128
C_ROWB = C_ROWF + 128
C_PSW = C_ROWB + 128
C_ID = C_PSW + 128
NCONST = C_ID + 128
PAIRS = [[0, 1], [2, 3], [4, 5], [6, 7]]


def _fm(a):
    n = a.shape[0]
    return np.ascontiguousarray(a.reshape(n, KC, 128).transpose(2, 1, 0))


def _vec16(v):
    return np.ascontiguousarray(v.reshape(-1, 128).T)


def rope_tables(s):
    t = np.arange(T) + 1024 * s
    rows = (t // 64).astype(np.float32)
    cols = (t % 64).astype(np.float32)
    quarter = 32
    inv = (np.float32(10000.0) ** (-np.arange(quarter, dtype=np.float32) / quarter)).astype(np.float32)
    ang = np.concatenate([rows[:, None] * inv, cols[:, None] * inv], axis=-1).astype(np.float32)
    cos = np.cos(ang).astype(np.float32).T
    sin = np.sin(ang).astype(np.float32).T
    out = np.zeros((128, 2, T), np.float32)
    out[:64, 0] = cos
    out[64:, 0] = cos
    out[:64, 1] = -sin
    out[64:, 1] = sin
    return out


def core_inputs(inp, b, s):
    x = inp['x']
    d = {}
    d['xin'] = _fm(x[b, 1024 * s:1024 * s + 1024])
    xe = np.zeros((NEXT, D), np.float32)
    if s == 1:
        xe[0] = x[b, 1023]
    if s == 0:
        xe[1] = x[b, 1024]
    xe[2:] = inp['ctx'][b, 128 * s:128 * s + 128]
    d['xein'] = _fm(xe)
    v = np.zeros((128, NV), np.float32)
    cc = np.stack([_vec16(inp['c'][b]), _vec16(inp['c_ctx'])], axis=-1)
    v[:, V_CC:V_CC + 32] = cc.reshape(128, 32)
    for li in range(2):
        v[:, V_BMOD + li * 144:V_BMOD + (li + 1) * 144] = _vec16(inp['b_mod'][li])
        v[:, V_NF1 + li * 16:V_NF1 + (li + 1) * 16] = _vec16(inp['norm_ffn1'][li])
        v[:, V_NMIX + li * 16:V_NMIX + (li + 1) * 16] = _vec16(inp['norm_mix'][li])
        v[:, V_NF2 + li * 16:V_NF2 + (li + 1) * 16] = _vec16(inp['norm_ffn2'][li])
    v[:, V_PSCALE:V_PSCALE + 16] = _vec16(inp['pool_scale'][0])
    v[:, V_FNORM:V_FNORM + 16] = _vec16(inp['final_norm'])
    wc = inp['mix_w_conv'][0]
    v[:, V_WCONV:V_WCONV + 24] = np.stack([_vec16(wc[t]) for t in range(3)], axis=1).reshape(128, 24)
    v[:, V_DEC:V_DEC + 8] = inp['ret_decay_fwd'][0][None, :]
    v[:, V_DEC + 8:V_DEC + 16] = inp['ret_decay_bwd'][0][None, :]
    v[:, V_SEL + 0] = 1.0 - s
    v[:, V_SEL + 1] = float(s)
    v[:, V_SEL + 2] = float(s == 1)
    v[:, V_SEL + 3] = float(s == 0)
    for gi, w in enumerate((2, 4, 8, 16)):
        for e in range(16):
            t = (e if e < 8 else T - 16 + e) + 1024 * s
            lo = min(max(t - w // 2, 0), 2048)
            hi = min(max(t + (w - w // 2), 0), 2048)
            v[:, V_PFAC + gi * 16 + e] = 1.0 / float(hi - lo)
    d['vecs'] = v
    d['consts'] = make_consts()
    d['rope'] = rope_tables(s)
    return d


def make_consts():
    c = np.zeros((128, NCONST), np.float32)
    c[:, C_EPS] = EPS
    c[:, C_GNEPS] = GN_EPS
    c[:, C_ONE] = 1.0
    c[:, C_LNK] = np.log(np.float32(K_SCALE))
    p = np.arange(128, dtype=np.float32)
    c[:, C_127MP] = 127.0 - p
    c[:, C_P] = p
    m = p[:, None]
    cc = p[None, :]
    BIG = 3.0e5
    c[:, C_RDF:C_RDF + 128] = np.where(cc >= m, cc - m, BIG)
    c[:, C_RDB:C_RDB + 128] = np.where(m >= cc, m - cc, BIG)
    c[:, C_ROWF:C_ROWF + 128] = cc + 1.0
    c[:, C_ROWB:C_ROWB + 128] = 128.0 - cc
    psw = np.zeros((128, 128), np.float32)
    for d in range(128):
        psw[(d + 64) % 128, d] = 1.0
    c[:, C_PSW:C_PSW + 128] = psw
    c[:, C_ID:C_ID + 128] = np.eye(128, dtype=np.float32)
    return c


def build_wts(wplan, inp):
    out = np.empty((len(wplan), 128, 2048), np.float32)
    for i, d in enumerate(wplan):
        kind = d[0]
        if kind == 'colblk':
            _, name, li, j = d
            W = inp[name][li]
            out[i] = W[:, 128 * j:128 * j + 128].reshape(KC, 128, 128).transpose(1, 0, 2).reshape(128, 2048)
        elif kind == 'rowblk':
            _, name, li, j = d
            out[i] = inp[name][li][128 * j:128 * j + 128, :]
        elif kind == 'rowblk4':
            _, name, li, g, db = d
            W = inp[name][li]
            out[i] = W[512 * g:512 * g + 512, 512 * db:512 * db + 512].reshape(4, 128, 512).transpose(1, 0, 2).reshape(128, 2048)
        elif kind == 'pool':
            _, g = d
            out[i] = inp['pool_w'][0][g].reshape(4, 128, 512).transpose(1, 0, 2).reshape(128, 2048)
        else:
            raise ValueError(kind)
    return out


_CACHE = {}


def get_prog(phases, fused):
    key = (tuple(phases), fused)
    if key not in _CACHE:
        b = Builder(list(phases), fused)
        nc = b.build()
        _CACHE[key] = (b, nc)
    return _CACHE[key]


def run_launch(phases, fused, inp, extra=None):
    b, nc = get_prog(phases, fused)
    wts = build_wts(b.wplan, inp)
    in_maps = []
    for core in range(8):
        bb, s = core // 2, core % 2
        d = core_inputs(inp, bb, s)
        d['wts'] = wts
        if extra is not None:
            d.update(extra[core])
        in_maps.append(d)
    res = run_bass_kernel_spmd(nc, in_maps, core_ids=list(range(8)))
    return res.results


def _exchange1(resA):
    ex = []
    for core in range(8):
        p0 = resA[(core // 2) * 2]['pay1']
        p1 = resA[(core // 2) * 2 + 1]['pay1']
        ex.append(np.concatenate([p0, p1], axis=0))
    return ex


def _exchange2(resB):
    ex = []
    for core in range(8):
        p0 = resB[(core // 2) * 2]['pay2']
        p1 = resB[(core // 2) * 2 + 1]['pay2']
        ex.append(np.concatenate([p0, p1], axis=0))
    return ex


FUSED = True


def kernel(**inp):
    inp = {k: np.asarray(v) for k, v in inp.items()}
    if FUSED:
        res = run_launch(['A', 'B', 'C'], True, inp)
    else:
        resA = run_launch(['A'], False, inp)
        ex1 = _exchange1(resA)
        extraB = [{'xin': resA[c]['xout'], 'xein': resA[c]['xeout'], 'pay1g': ex1[c]} for c in range(8)]
        resB = run_launch(['B'], False, inp, extraB)
        ex2 = _exchange2(resB)
        extraC = [{'xin': resB[c]['xout'], 'xein': resB[c]['xeout'], 'pay2g': ex2[c]} for c in range(8)]
        res = run_launch(['C'], False, inp, extraC)
    out = np.empty((4, 2048, D), np.float32)
    for core in range(8):
        b, s = core // 2, core % 2
        o = res[core]['outT']
        out[b, 1024 * s:1024 * s + 1024] = o.transpose(2, 1, 0).reshape(T, D)
    return out
```

```python
import numpy as np
import ml_dtypes
import concourse.bass as bass
import concourse.mybir as mybir
from concourse.bass_utils import run_bass_kernel_spmd

F32 = mybir.dt.float32
BF16 = mybir.dt.bfloat16
AF = mybir.ActivationFunctionType
ALU = mybir.AluOpType

D = 2048
KC = 16
FF = 5632
FC = 44
T = 1024
NEXT = 130
NH = 8
NSLOT = 6
NDMASEM = 12
EPS = 1e-6
GN_EPS = 1e-5
K_SCALE = 128 ** -0.5
SAME_ENGINE_SYNC = True


class Prog:
    ENG = ['pe', 'act', 'dve', 'pool', 'sp']

    def __init__(self):
        self.ops = {e: [] for e in self.ENG}
        self.count = {}
        self.waited = {e: {} for e in self.ENG}
        self.reg = {}
        self.dma_rr = 0
        self.final = []

    def _need(self, eng, tok):
        if tok is None:
            return
        k, v = tok
        if k == 'tl_' + eng and (eng == 'pe' or not SAME_ENGINE_SYNC):
            return
        if self.waited[eng].get(k, 0) < v:
            self.ops[eng].append(('wait', k, v))
            self.waited[eng][k] = v

    def _deps(self, eng, reads, writes):
        for k in reads:
            r = self.reg.get(k)
            if r is not None:
                self._need(eng, r[0])
        for k in writes:
            r = self.reg.get(k)
            if r is not None:
                self._need(eng, r[0])
                for t in r[1]:
                    self._need(eng, t)

    def _update(self, tok, reads, writes):
        for k in reads:
            r = self.reg.setdefault(k, [None, []])
            r[1].append(tok)
            if len(r[1]) > 64:
                best = {}
                for (kk, vv) in r[1]:
                    if best.get(kk, 0) < vv:
                        best[kk] = vv
                r[1] = list(best.items())
        for k in writes:
            self.reg[k] = [tok, []]

    @staticmethod
    def _is_psum(k):
        return k == 'psb' or (isinstance(k, tuple) and k[0] in ('ps', 'psb'))

    def op(self, eng, fn, reads=(), writes=()):
        ex = [k for k in reads if self._is_psum(k)]
        if ex:
            reads = [k for k in reads if not self._is_psum(k)]
            writes = list(writes) + [('psb', 0) if (k == 'psb' or k[0] == 'psb') else k for k in ex]
        writes = [('psb', 0) if (k == 'psb' or (isinstance(k, tuple) and k[0] == 'psb')) else k for k in writes]
        self._deps(eng, reads, writes)
        semk = 'tl_' + eng
        v = self.count.get(semk, 0) + 1
        self.count[semk] = v
        self.ops[eng].append(('op', fn, semk))
        tok = (semk, v)
        self._update(tok, reads, writes)
        return tok

    def dma(self, q, out, in_, reads=(), writes=()):
        self._deps(q, reads, writes)
        semk = 'dma%d' % self.dma_rr
        self.dma_rr = (self.dma_rr + 1) % NDMASEM
        prev = self.count.get(semk, 0)
        if prev:
            self._need(q, (semk, prev))
        v = prev + 16
        self.count[semk] = v
        self.ops[q].append(('dma', out, in_, semk))
        tok = (semk, v)
        self._update(tok, reads, writes)
        return tok

    def cc(self, ins, outs, groups, reads=(), writes=()):
        q = 'pool'
        self._deps(q, reads, writes)
        semk = 'ccsem'
        v = self.count.get(semk, 0) + 1
        self.count[semk] = v
        self.ops[q].append(('cc', ins, outs, groups, semk))
        tok = (semk, v)
        self._update(tok, reads, writes)
        return tok

    def finish(self, q, tok):
        self._need(q, tok)

    def emit(self, nc):
        import contextlib
        semnames = sorted(self.count.keys())
        with contextlib.ExitStack() as st:
            sems = {k: st.enter_context(nc.semaphore(k)) for k in semnames}
            block = st.enter_context(nc.Block())
            ops = self.ops

            def run(eng, e):
                for o in ops[eng]:
                    if o[0] == 'wait':
                        e.wait_ge(sems[o[1]], o[2])
                    elif o[0] == 'op':
                        f = o[1]
                        if isinstance(f, tuple):
                            ins = getattr(e, f[0])(**f[1])
                        else:
                            ins = f(e)
                        ins.then_inc(sems[o[2]], 1)
                    elif o[0] == 'dma':
                        src = o[2]() if callable(o[2]) else o[2]
                        e.dma_start(out=o[1], in_=src).then_inc(sems[o[3]], 16)
                    elif o[0] == 'cc':
                        e.collective_compute("AllGather", ALU.bypass, replica_groups=o[3],
                                             ins=o[1], outs=o[2]).then_inc(sems[o[4]])

            @block.tensor
            def _(e):
                run('pe', e)

            @block.scalar
            def _(e):
                run('act', e)

            @block.vector
            def _(e):
                run('dve', e)

            @block.gpsimd
            def _(e):
                run('pool', e)

            @block.sync
            def _(e):
                run('sp', e)


class Builder:
    def __init__(self, phases, fused):
        self.phases = phases
        self.fused = fused
        self.P = Prog()
        self.wplan = []
        self.nc = bass.Bass("TRN2", target_bir_lowering=False)
        self.ps_rr = 0
        self.rr = {}

    def rot(self, name, n):
        i = self.rr.get(name, 0)
        self.rr[name] = (i + 1) % n
        return i

    def psum(self):
        i = self.ps_rr
        self.ps_rr = (self.ps_rr + 1) % 6
        return i

    def wtile(self, desc):
        i = len(self.wplan)
        self.wplan.append(desc)
        s = i % NSLOT
        self.P.dma('pool', self.ring[:, s, :], (lambda i=i: self.wts[i, :, :]), reads=[], writes=[('ring', s)])
        return s

    def I(self, eng, name, reads=(), writes=(), **kw):
        return self.P.op(eng, (name, kw), reads=reads, writes=writes)

    def mm_group(self, ps_ap, pairs, reads, writes, transpose=False):
        n = len(pairs)

        def fn(e):
            ins = None
            for j, (l, r) in enumerate(pairs):
                ins = e.matmul(ps_ap, l, r, start=(j == 0), stop=(j == n - 1))
            return ins
        return self.P.op('pe', fn, reads=reads, writes=writes)

    def build(self):
        nc = self.nc
        P = self.P
        ph = self.phases
        import contextlib
        st = contextlib.ExitStack()
        with st:
            dt = nc.dram_tensor
            self.xin = dt("xin", [128, KC, T], F32, kind="ExternalInput").ap()
            self.xein = dt("xein", [128, KC, NEXT], F32, kind="ExternalInput").ap()
            self.vecs_d = dt("vecs", [128, NV], F32, kind="ExternalInput").ap()
            self.consts_d = dt("consts", [128, NCONST], F32, kind="ExternalInput").ap()
            self.rope_d = dt("rope", [128, 2, T], F32, kind="ExternalInput").ap()
            last = ph[-1]
            if last == 'C':
                self.out_d = dt("outT", [128, KC, T], F32, kind="ExternalOutput").ap()
            else:
                self.xout = dt("xout", [128, KC, T], F32, kind="ExternalOutput").ap()
                self.xeout = dt("xeout", [128, KC, NEXT], F32, kind="ExternalOutput").ap()
            if self.fused:
                self.pay1 = dt("pay1", [4 * NH * 128, 128], F32)
                self.pay1g = dt("pay1g", [2 * 4 * NH * 128, 128], F32)
                self.pay2 = dt("pay2", [128, KC * 16], F32)
                self.pay2g = dt("pay2g", [2 * 128, KC * 16], F32)
                self.pay1_w = self.pay1.ap() if hasattr(self.pay1, 'ap') else self.pay1
            else:
                if 'A' in ph:
                    self.pay1 = dt("pay1", [4 * NH * 128, 128], F32, kind="ExternalOutput")
                if 'B' in ph:
                    self.pay1g = dt("pay1g", [2 * 4 * NH * 128, 128], F32, kind="ExternalInput")
                    self.pay2 = dt("pay2", [128, KC * 16], F32, kind="ExternalOutput")
                if 'C' in ph:
                    self.pay2g = dt("pay2g", [2 * 128, KC * 16], F32, kind="ExternalInput")

            sb = lambda name, shape, dtype: st.enter_context(nc.sbuf_tensor(name, shape, dtype))
            self.x = sb("x", [128, KC, T], F32)
            self.xe = sb("xe", [128, KC, NEXT], F32)
            self.h = sb("h", [128, KC, T + NEXT], BF16)
            self.ring = sb("ring", [128, NSLOT, 2048], BF16)
            self.abuf = sb("abuf", [128, 2, 4, T + NEXT], BF16)
            self.vecs = sb("vecs_s", [128, NV], F32)
            self.consts = sb("consts_s", [128, NCONST], F32)
            self.wide = sb("wide", [128, 3, T + 16], F32)
            self.modraw = sb("modraw", [128, 2, 144], F32)
            self.tabA = sb("tabA", [128, 2, KC], F32)
            self.tabG = sb("tabG", [128, 2, KC], F32)
            self.sc = sb("sc", [128, KC, 2], BF16)
            self.scf = sb("scf", [128, KC, 2], F32)
            self.ones = sb("ones", [128, 128], BF16)
            self.onesf = sb("onesf", [128, 128], F32)
            self.sq = sb("sq", [128, 2, 512], BF16)
            self.f32t = sb("f32t", [128, 4, 514], F32)
            self.rstd = sb("rstd", [128, 512], F32)
            self.sg = sb("sg", [128, 2, 514], F32)
            ps = lambda name, shape, dtype: st.enter_context(nc.psum_tensor(name, shape, dtype))
            self.ps = [ps("ps%d" % i, [128, 512], F32) for i in range(7)]
            self.psb = ps("psb", [128, 1024], BF16)
            self.mix_alloc(sb)

            P.dma('sp', self.vecs[:, :], self.vecs_d[:, :], writes=['vecs'])
            P.dma('sp', self.consts[:, :], self.consts_d[:, :], writes=['consts'])
            if 'A' in ph or 'B' in ph:
                P.dma('sp', self.wide[:, 0:2, 0:T], self.rope_d[:, :, :], writes=[('wide', 0), ('wide', 1)])
            for k in range(KC):
                P.dma('sp', self.x[:, k, :], self.xin[:, k, :], writes=[('x', k, 0), ('x', k, 1)])
            P.dma('sp', self.xe[:, :, :], self.xein[:, :, :], writes=[('xe', k) for k in range(KC)])
            P.op('dve', lambda e: e.memset(self.ones[:, :], 1.0), writes=['ones'])
            P.op('dve', lambda e: e.memset(self.onesf[:, :], 1.0 / 128.0), writes=['onesf'])
            cc = self.vecs[:, V_CC:V_CC + 32]
            self.I('act', 'activation', reads=['vecs'], writes=['scf'],
                   out=self.scf[:, :, :].rearrange("p k e -> p (k e)"), in_=cc, func=AF.Silu)
            self.I('dve', 'tensor_copy', reads=['scf'], writes=['sc'], out=self.sc[:, :, :], in_=self.scf[:, :, :])

            self.main_tiles = [dict(kind='x', c0=0, n=512, half=0), dict(kind='x', c0=512, n=512, half=1)]
            self.ext_tile = dict(kind='xe', c0=0, n=NEXT)

            fused_all = (ph == ['A', 'B', 'C'])
            if 'A' in ph:
                self.adaln(0, 0, 3)
                self.adaln_begin(0, 3, 9 if fused_all else 6)
                self.ffn(0, 1, self.main_tiles + [self.ext_tile], ext_ctx=True, side=True)
                self.norm_mod(0, 'mix', self.main_tiles + [self.ext_tile], ext_ctx=True)
                self.mixer_states()
            if 'A' in ph and 'B' in ph:
                if self.fused:
                    P.cc([self.pay1[:, :]], [self.pay1g[:, :]], PAIRS, reads=['pay1'], writes=['pay1g'])
            if 'B' in ph:
                if 'A' not in ph:
                    self.adaln(0, 3, 6)
                    self.norm_mod(0, 'mix', self.main_tiles + [self.ext_tile], ext_ctx=True)
                self.mixer_main()
                if not fused_all:
                    self.adaln(0, 6, 9)
                self.adaln_begin(1, 0, 3)
                self.ffn(0, 2, self.main_tiles, side=True)
                self.adaln_begin(1, 3, 9 if fused_all else 3)
                if not fused_all:
                    self.aj = None
                self.ffn(1, 1, self.main_tiles, side=fused_all)
                self.halo_out()
            if 'B' in ph and 'C' in ph:
                if self.fused:
                    P.cc([self.pay2[:, :]], [self.pay2g[:, :]], PAIRS, reads=['pay2'], writes=['pay2g'])
            if 'C' in ph:
                if not fused_all:
                    self.adaln(1, 3, 6)
                self.pool_mixer()
                if not fused_all:
                    self.adaln(1, 6, 9)
                self.ffn(1, 2, self.main_tiles)
                self.final_norm()
            else:
                toks = []
                for k in range(KC):
                    toks.append(P.dma('sp', self.xout[:, k, :], self.x[:, k, :], reads=[('x', k, 0), ('x', k, 1)]))
                toks.append(P.dma('sp', self.xeout[:, :, :], self.xe[:, :, :], reads=[('xe', k) for k in range(KC)]))
                if 'A' in ph and not self.fused:
                    r = self.P.reg.get('pay1')
                    if r is not None and r[0] is not None:
                        toks.append(r[0])
                if 'B' in ph and not self.fused:
                    r = self.P.reg.get('pay2')
                    if r is not None and r[0] is not None:
                        toks.append(r[0])
                for t in toks:
                    P.finish('sp', t)
            self.wts = dt("wts", [len(self.wplan), 128, 2048], F32, kind="ExternalInput").ap()
            P.emit(nc)
        return nc

    def ntiles_placeholder(self):
        return self.ntiles

    def xap(self, tile, k, c0=None, c1=None):
        if c0 is None:
            c0, c1 = 0, tile['n']
        if tile['kind'] == 'x':
            return self.x[:, k, tile['c0'] + c0: tile['c0'] + c1]
        return self.xe[:, k, tile['c0'] + c0: tile['c0'] + c1]

    def xkey(self, tile, k):
        if tile['kind'] == 'x':
            return ('x', k, tile['half'])
        return ('xe', k)

    def hcol(self, tile):
        return tile['c0'] if tile['kind'] == 'x' else T + tile['c0']

    def hkey(self, tile, k):
        if tile['kind'] == 'x':
            return ('h', k, tile['half'])
        return ('he', k)

    def segs(self, tile, ext_ctx):
        if tile['kind'] == 'xe' and ext_ctx:
            return [(0, 2, 0), (2, NEXT, 1)]
        return [(0, tile['n'], 0)]

    def adaln(self, li, q0, q1):
        self.adaln_begin(li, q0, q1)
        self.adaln_work(10 ** 9)

    def adaln_begin(self, li, q0, q1):
        assert getattr(self, 'aj', None) is None
        self.aj = dict(li=li, q0=q0, q1=q1, jj=0, nj=(q1 - q0) * 16)

    def adaln_pending(self):
        aj = getattr(self, 'aj', None)
        return 0 if aj is None else aj['nj'] - aj['jj']

    def adaln_work(self, ntiles):
        aj = getattr(self, 'aj', None)
        if aj is None:
            return
        li, q0, q1 = aj['li'], aj['q0'], aj['q1']
        pb = 6
        psv = self.ps[pb]
        while ntiles > 0 and aj['jj'] < aj['nj']:
            jj = aj['jj']
            j = q0 * 16 + jj
            s = self.wtile(('colblk', 'w_mod', li, j))
            pairs = [(self.ring[:, s, kc * 128:(kc + 1) * 128], self.sc[:, kc, :]) for kc in range(KC)]
            self.mm_group(psv[:, 2 * jj:2 * jj + 2], pairs, reads=[('ring', s), 'sc'], writes=[('ps', pb)])
            aj['jj'] += 1
            ntiles -= 1
        if aj['jj'] >= aj['nj']:
            nj = aj['nj']
            pview = psv[:, 0:2 * nj].rearrange("p (j e) -> p j e", e=2)
            bm = self.vecs[:, V_BMOD + li * 144 + q0 * 16: V_BMOD + li * 144 + q1 * 16]
            for e_ in range(2):
                self.I('dve', 'tensor_tensor', reads=[('ps', pb), 'vecs'], writes=[('modraw', q) for q in range(q0, q1)],
                       out=self.modraw[:, e_, q0 * 16:q1 * 16], in0=pview[:, :, e_], in1=bm, op=ALU.add)
            self.aj = None

    def mod_tables(self, li, sub, gain_col, gate_mul, extra_gate_col=None):
        q0 = 3 * sub
        mk = [('modraw', q0), ('modraw', q0 + 1), ('modraw', q0 + 2)]
        for e_ in range(2):
            self.I('dve', 'scalar_tensor_tensor', reads=mk + ['vecs'], writes=['tabA'],
                   out=self.tabA[:, e_, :], in0=self.modraw[:, e_, (q0 + 1) * 16:(q0 + 2) * 16], scalar=1.0,
                   in1=self.vecs[:, gain_col:gain_col + 16], op0=ALU.add, op1=ALU.mult)
            if extra_gate_col is None:
                self.I('dve', 'tensor_scalar', reads=mk, writes=['tabG'],
                       out=self.tabG[:, e_, :], in0=self.modraw[:, e_, (q0 + 2) * 16:(q0 + 3) * 16],
                       scalar1=float(gate_mul), scalar2=None, op0=ALU.mult)
            else:
                self.I('dve', 'tensor_tensor', reads=mk + ['vecs'], writes=['tabG'],
                       out=self.tabG[:, e_, :], in0=self.modraw[:, e_, (q0 + 2) * 16:(q0 + 3) * 16],
                       in1=self.vecs[:, extra_gate_col:extra_gate_col + 16], op=ALU.mult)

    def norm_mod(self, li, which, tiles, ext_ctx=False):
        sub = {'ffn1': 0, 'mix': 1, 'ffn2': 2}[which]
        gain_col = {'ffn1': V_NF1, 'mix': V_NMIX, 'ffn2': V_NF2}[which] + li * 16
        gate_mul = 1.0 if which == 'mix' else 0.5
        extra = (V_PSCALE if (which == 'mix' and li == 1) else None)
        self.mod_tables(li, sub, gain_col, gate_mul, extra)
        q0 = 3 * sub
        for tile in tiles:
            n = tile['n']
            pb = self.psum()
            psv = self.ps[pb][:, 0:n]
            for k in range(KC):
                b = self.rot('sq', 2)
                self.I('act', 'activation', reads=[self.xkey(tile, k)], writes=[('sq', b)],
                       out=self.sq[:, b, 0:n], in_=self.xap(tile, k), func=AF.Square)
                self.I('pe', 'matmul', reads=[('sq', b), 'ones'], writes=[('ps', pb)],
                       out=psv, lhsT=self.ones[:, :], rhs=self.sq[:, b, 0:n], start=(k == 0), stop=(k == KC - 1))
            self.I('act', 'activation', reads=[('ps', pb), 'consts'], writes=['rstd'],
                   out=self.rstd[:, 0:n], in_=psv, func=AF.Sqrt, bias=self.consts[:, C_EPS:C_EPS + 1], scale=1.0 / D)
            self.I('dve', 'reciprocal', reads=['rstd'], writes=['rstd'], out=self.rstd[:, 0:n], in_=self.rstd[:, 0:n])
            hc0 = self.hcol(tile)
            for k in range(KC):
                b = self.rot('f32t', 4)
                self.I('dve', 'tensor_tensor', reads=[self.xkey(tile, k), 'rstd'], writes=[('f32t', b)],
                       out=self.f32t[:, b, 0:n], in0=self.xap(tile, k), in1=self.rstd[:, 0:n], op=ALU.mult)
                for (c0, c1, e_) in self.segs(tile, ext_ctx):
                    self.I('act', 'activation', reads=[('f32t', b), 'tabA', ('modraw', q0)], writes=[self.hkey(tile, k)],
                           out=self.h[:, k, hc0 + c0:hc0 + c1], in_=self.f32t[:, b, c0:c1], func=AF.Identity,
                           bias=self.modraw[:, e_, q0 * 16 + k:q0 * 16 + k + 1], scale=self.tabA[:, e_, k:k + 1])

    def akey(self, ab, c, tile):
        return ('a', ab, c, tile['kind'], tile.get('half', 0))

    def ffn(self, li, idx, tiles, ext_ctx=False, side=False):
        which = 'ffn1' if idx == 1 else 'ffn2'
        self.norm_mod(li, which, tiles, ext_ctx)
        wg, wu, wd = {1: ('ffn1_w_gate', 'ffn1_w_up', 'ffn1_w_down'), 2: ('ffn2_w_gate', 'ffn2_w_up', 'ffn2_w_down')}[idx]
        NG = FC // 4
        ticks = [2 * FC]

        def tick():
            if side and self.adaln_pending():
                self.adaln_work(-(-self.adaln_pending() // max(1, ticks[0])))
            ticks[0] -= 1

        def gate_up(g):
            ab = g % 2
            for c in range(4):
                if c > 0 or g > 0:
                    tick()
                fcn = g * 4 + c
                sg_ = self.wtile(('colblk', wg, li, fcn))
                su_ = self.wtile(('colblk', wu, li, fcn))
                for tile in tiles:
                    n = tile['n']
                    hc0 = self.hcol(tile)
                    pg = self.psum()
                    pu = self.psum()
                    hk = [self.hkey(tile, k) for k in range(KC)]
                    self.mm_group(self.ps[pg][:, 0:n],
                                  [(self.ring[:, sg_, kc * 128:(kc + 1) * 128], self.h[:, kc, hc0:hc0 + n]) for kc in range(KC)],
                                  reads=[('ring', sg_)] + hk, writes=[('ps', pg)])
                    self.mm_group(self.ps[pu][:, 0:n],
                                  [(self.ring[:, su_, kc * 128:(kc + 1) * 128], self.h[:, kc, hc0:hc0 + n]) for kc in range(KC)],
                                  reads=[('ring', su_)] + hk, writes=[('ps', pu)])
                    b = self.rot('sg', 2)
                    self.I('act', 'activation', reads=[('ps', pg)], writes=[('sg', b)],
                           out=self.sg[:, b, 0:n], in_=self.ps[pg][:, 0:n], func=AF.Silu)
                    self.I('dve', 'tensor_tensor', reads=[('sg', b), ('ps', pu)], writes=[self.akey(ab, c, tile)],
                           out=self.abuf[:, ab, c, hc0:hc0 + n], in0=self.sg[:, b, 0:n], in1=self.ps[pu][:, 0:n], op=ALU.mult)

        def down(g):
            ab = g % 2
            for db in range(4):
                sl = self.wtile(('rowblk4', wd, li, g, db))
                for tile in tiles:
                    n = tile['n']
                    hc0 = self.hcol(tile)
                    akeys = [self.akey(ab, c, tile) for c in range(4)]
                    for dq in range(4):
                        dk = db * 4 + dq
                        pd = self.psum()
                        self.mm_group(self.ps[pd][:, 0:n],
                                      [(self.ring[:, sl, c * 512 + dq * 128:c * 512 + dq * 128 + 128], self.abuf[:, ab, c, hc0:hc0 + n]) for c in range(4)],
                                      reads=[('ring', sl)] + akeys, writes=[('ps', pd)])
                        for (c0, c1, e_) in self.segs(tile, ext_ctx):
                            self.I('dve', 'scalar_tensor_tensor', reads=[('ps', pd), 'tabG', self.xkey(tile, dk)],
                                   writes=[self.xkey(tile, dk)],
                                   out=self.xap(tile, dk, c0, c1), in0=self.ps[pd][:, c0:c1], scalar=self.tabG[:, e_, dk:dk + 1],
                                   in1=self.xap(tile, dk, c0, c1), op0=ALU.mult, op1=ALU.add)
                tick()

        gate_up(0)
        for g in range(NG):
            if g + 1 < NG:
                gate_up(g + 1)
            down(g)
        if side:
            self.adaln_work(10 ** 9)

    def mix_alloc(self, sb):
        self.qk = sb("qk", [128, 2, T], BF16)
        self.qfb = sb("qfb", [128, 2, 2, 128], BF16)
        self.vh = sb("vh", [128, 9, 128], BF16)
        self.kd = sb("kd", [128, 2, 4, 128], BF16)
        self.Sst = sb("Sst", [128, 2, 128], F32)
        self.Sbf = sb("Sbf", [128, 2, 8, 128], BF16)
        self.PT = sb("PT", [128, 2, 128], BF16)
        self.pst = sb("pst", [128, 6, 128], F32)
        self.Dh = sb("Dh", [128, 4, 128], F32)
        self.dec = sb("dec", [128, 6, 16], F32)
        self.identb = sb("identb", [128, 128], BF16)
        self.dec_done = False

    def mt(self, i):
        if i < 4:
            return self.f32t[:, i, :], ('f32t', i)
        return self.sg[:, i - 4, :], ('sg', i - 4)

    def dec_setup(self):
        if self.dec_done:
            return
        self.dec_done = True
        I = self.I
        raw = self.vecs[:, V_DEC:V_DEC + 16]
        LG, KD, CDt, CD8, AL, TMP = [self.dec[:, i, :] for i in range(6)]
        cst = self.consts
        I('act', 'activation', reads=['vecs'], writes=['dec'], out=TMP, in_=raw, func=AF.Exp, scale=-1.0)
        I('act', 'activation', reads=['dec', 'consts'], writes=['dec'], out=TMP, in_=TMP, func=AF.Ln,
          bias=cst[:, C_ONE:C_ONE + 1], scale=1.0)
        I('dve', 'tensor_scalar', reads=['dec'], writes=['dec'], out=LG, in0=TMP, scalar1=-1.0, scalar2=None, op0=ALU.mult)
        I('dve', 'tensor_scalar', reads=['dec', 'consts'], writes=['dec'], out=TMP[:, 0:8], in0=LG[:, 0:8],
          scalar1=cst[:, C_127MP:C_127MP + 1], scalar2=None, op0=ALU.mult)
        I('dve', 'tensor_scalar', reads=['dec', 'consts'], writes=['dec'], out=TMP[:, 8:16], in0=LG[:, 8:16],
          scalar1=cst[:, C_P:C_P + 1], scalar2=None, op0=ALU.mult)
        I('act', 'activation', reads=['dec', 'consts'], writes=['dec'], out=KD, in_=TMP, func=AF.Exp,
          bias=cst[:, C_LNK:C_LNK + 1], scale=1.0)
        I('act', 'activation', reads=['dec'], writes=['dec'], out=CDt, in_=LG, func=AF.Exp, scale=128.0)
        I('act', 'activation', reads=['dec'], writes=['dec'], out=CD8, in_=LG, func=AF.Exp, scale=1024.0)
        sel0 = self.vecs[:, V_SEL:V_SEL + 1]
        sel1 = self.vecs[:, V_SEL + 1:V_SEL + 2]
        I('dve', 'tensor_scalar', reads=['dec', 'vecs'], writes=['dec'], out=AL[:, 0:8], in0=CD8[:, 0:8],
          scalar1=sel1, scalar2=sel0, op0=ALU.mult, op1=ALU.add)
        I('dve', 'tensor_scalar', reads=['dec', 'vecs'], writes=['dec'], out=AL[:, 8:16], in0=CD8[:, 8:16],
          scalar1=sel0, scalar2=sel1, op0=ALU.mult, op1=ALU.add)
        I('dve', 'tensor_copy', reads=['consts'], writes=['identb'], out=self.identb[:, :], in_=cst[:, C_ID:C_ID + 128])

    def LGc(self, d, hd):
        return self.dec[:, 0, 8 * d + hd:8 * d + hd + 1]

    def KDc(self, d, hd):
        return self.dec[:, 1, 8 * d + hd:8 * d + hd + 1]

    def CDc(self, d, hd):
        return self.dec[:, 2, 8 * d + hd:8 * d + hd + 1]

    def ALc(self, d, hd):
        return self.dec[:, 4, 8 * d + hd:8 * d + hd + 1]

    def proj_rope(self, slot, dst):
        I = self.I
        for half in range(2):
            c0 = half * 512
            pq = self.psum()
            self.mm_group(self.ps[pq][:, :],
                          [(self.ring[:, slot, kc * 128:(kc + 1) * 128], self.h[:, kc, c0:c0 + 512]) for kc in range(KC)],
                          reads=[('ring', slot)] + [('h', k, half) for k in range(KC)], writes=[('ps', pq)])
            m0, k0 = self.mt(0)
            m1, k1 = self.mt(1)
            m2, k2 = self.mt(2)
            I('act', 'activation', reads=[('ps', pq)], writes=[k0], out=m0[:, 0:512], in_=self.ps[pq][:, :], func=AF.Copy)
            pr = self.psum()
            I('pe', 'matmul', reads=[k0, 'consts'], writes=[('ps', pr)], out=self.ps[pr][:, :],
              lhsT=self.consts[:, C_PSW:C_PSW + 128], rhs=m0[:, 0:512], start=True, stop=True)
            I('dve', 'tensor_tensor', reads=[k0, ('wide', 0)], writes=[k1], out=m1[:, 0:512], in0=m0[:, 0:512],
              in1=self.wide[:, 0, c0:c0 + 512], op=ALU.mult)
            I('dve', 'tensor_tensor', reads=[('ps', pr), ('wide', 1)], writes=[k2], out=m2[:, 0:512], in0=self.ps[pr][:, :],
              in1=self.wide[:, 1, c0:c0 + 512], op=ALU.mult)
            I('dve', 'tensor_tensor', reads=[k1, k2], writes=[('qk', dst, half)], out=self.qk[:, dst, c0:c0 + 512],
              in0=m1[:, 0:512], in1=m2[:, 0:512], op=ALU.add)

    def v_proj(self, slot):
        I = self.I
        for grp in range(3):
            ns = [0, 1, 2, 3] if grp == 0 else ([4, 5, 6, 7] if grp == 1 else [8])
            pv = self.psum()
            for j, n in enumerate(ns):
                if n < 8:
                    cols = (n * 128, n * 128 + 128)
                    hk = [('h', k, n // 4) for k in range(KC)]
                else:
                    cols = (T + 2, T + 130)
                    hk = [('he', k) for k in range(KC)]
                self.mm_group(self.ps[pv][:, j * 128:(j + 1) * 128],
                              [(self.h[:, kc, cols[0]:cols[1]], self.ring[:, slot, kc * 128:(kc + 1) * 128]) for kc in range(KC)],
                              reads=[('ring', slot)] + hk, writes=[('ps', pv)])
            for j, n in enumerate(ns):
                I('act', 'activation', reads=[('ps', pv)], writes=[('vh', n)], out=self.vh[:, n, :],
                  in_=self.ps[pv][:, j * 128:(j + 1) * 128], func=AF.Copy)

    def kv_mm(self, d, r, n):
        pk = self.psum()
        self.I('pe', 'matmul', reads=[('kd', d, r), ('vh', n)], writes=[('ps', pk)], out=self.ps[pk][:, 0:128],
               lhsT=self.kd[:, d, r, :], rhs=self.vh[:, n, :], start=True, stop=True)
        return pk

    def ctx_kd(self, hd, slot):
        I = self.I
        pc = self.psum()
        self.mm_group(self.ps[pc][:, 0:128],
                      [(self.h[:, kc, T + 2:T + 130], self.ring[:, slot, kc * 128:(kc + 1) * 128]) for kc in range(KC)],
                      reads=[('ring', slot)] + [('he', k) for k in range(KC)], writes=[('ps', pc)])
        r = self.rot('kd', 2)
        I('act', 'activation', reads=[('ps', pc), 'dec'], writes=[('kd', 0, r)], out=self.kd[:, 0, r, :],
          in_=self.ps[pc][:, 0:128], func=AF.Identity, scale=self.KDc(0, hd))
        I('dve', 'tensor_scalar', reads=[('ps', pc), 'dec'], writes=[('kd', 1, r)], out=self.kd[:, 1, r, :],
          in0=self.ps[pc][:, 0:128], scalar1=self.KDc(1, hd), scalar2=None, op0=ALU.mult)
        return r

    def kv_all(self, hd):
        I = self.I
        banks = [[self.psum(), self.psum()], [self.psum(), self.psum()]]
        loc = {}
        for batch in range(2):
            ns = [4 * batch + i for i in range(4)]
            for n in ns:
                I('pe', 'transpose', reads=[('qk', 1, n // 4), 'identb'], writes=['psb'],
                  out=self.psb[:, n * 128:(n + 1) * 128], in_=self.qk[:, 1, n * 128:(n + 1) * 128], identity=self.identb[:, :])
            for n in ns:
                I('act', 'activation', reads=['psb', 'dec'], writes=[('kd', 0, n % 4)], out=self.kd[:, 0, n % 4, :],
                  in_=self.psb[:, n * 128:(n + 1) * 128], func=AF.Identity, scale=self.KDc(0, hd))
                I('dve', 'tensor_scalar', reads=['psb', 'dec'], writes=[('kd', 1, n % 4)], out=self.kd[:, 1, n % 4, :],
                  in0=self.psb[:, n * 128:(n + 1) * 128], scalar1=self.KDc(1, hd), scalar2=None, op0=ALU.mult)
            for d in range(2):
                for n in ns:
                    pk = banks[d][batch]
                    I('pe', 'matmul', reads=[('kd', d, n % 4), ('vh', n)], writes=[('ps', pk)],
                      out=self.ps[pk][:, (n % 4) * 128:(n % 4) * 128 + 128], lhsT=self.kd[:, d, n % 4, :], rhs=self.vh[:, n, :],
                      start=True, stop=True)
                    loc[(d, n)] = (pk, (n % 4) * 128)
        return loc

    def mixer_states(self):
        I = self.I
        self.dec_setup()
        pay = self.pay1.ap() if hasattr(self.pay1, 'ap') else self.pay1
        payv = pay.rearrange("(k h p) v -> h p k v", k=4, h=NH, p=128)
        for hd in range(NH):
            sk = self.wtile(('colblk', 'mix_w_in', 0, 8 + hd))
            sv = self.wtile(('colblk', 'mix_w_in', 0, 16 + hd))
            self.proj_rope(sk, 1)
            self.v_proj(sv)
            r = self.ctx_kd(hd, sk)
            for d in range(2):
                pk = self.kv_mm(d, r, 8)
                I('act', 'activation', reads=[('ps', pk)], writes=[('pst', d)], out=self.pst[:, d, :],
                  in_=self.ps[pk][:, 0:128], func=AF.Copy)
            loc = self.kv_all(hd)
            for d in range(2):
                order = list(range(8)) if d == 0 else list(range(7, -1, -1))
                for i, n in enumerate(order):
                    pk, c0 = loc[(d, n)]
                    if i == 0:
                        I('dve', 'tensor_copy', reads=[('ps', pk)], writes=[('pst', 2 + d)], out=self.pst[:, 2 + d, :],
                          in_=self.ps[pk][:, c0:c0 + 128])
                    else:
                        I('dve', 'scalar_tensor_tensor', reads=[('ps', pk), ('pst', 2 + d), 'dec'], writes=[('pst', 2 + d)],
                          out=self.pst[:, 2 + d, :], in0=self.pst[:, 2 + d, :], scalar=self.CDc(d, hd),
                          in1=self.ps[pk][:, c0:c0 + 128], op0=ALU.mult, op1=ALU.add)
            self.P.dma('sp', payv[hd], self.pst[:, 0:4, :], reads=[('pst', i) for i in range(4)], writes=['pay1'])

    def k_tm_chunk_dir(self, hd, n, d):
        I = self.I
        r = self.rot('kd%d' % d, 2)
        I('pe', 'transpose', reads=[('qk', 1, n // 4), 'identb'], writes=[('psb', n)],
          out=self.psb[:, n * 128:(n + 1) * 128], in_=self.qk[:, 1, n * 128:(n + 1) * 128], identity=self.identb[:, :])
        if d == 0:
            I('act', 'activation', reads=[('psb', n), 'dec'], writes=[('kd', d, r)], out=self.kd[:, d, r, :],
              in_=self.psb[:, n * 128:(n + 1) * 128], func=AF.Identity, scale=self.KDc(d, hd))
        else:
            I('dve', 'tensor_scalar', reads=[('psb', n), 'dec'], writes=[('kd', d, r)], out=self.kd[:, d, r, :],
              in0=self.psb[:, n * 128:(n + 1) * 128], scalar1=self.KDc(d, hd), scalar2=None, op0=ALU.mult)
        return r

    def head_tables(self, hd):
        I = self.I
        cst = self.consts
        lnk = cst[:, C_LNK:C_LNK + 1]
        I('act', 'activation', reads=['consts', 'dec'], writes=[('Dh', 0)], out=self.Dh[:, 0, :], in_=cst[:, C_RDF:C_RDF + 128],
          func=AF.Exp, bias=lnk, scale=self.LGc(0, hd))
        I('act', 'activation', reads=['consts', 'dec'], writes=[('Dh', 1)], out=self.Dh[:, 1, :], in_=cst[:, C_RDB:C_RDB + 128],
          func=AF.Exp, bias=lnk, scale=self.LGc(1, hd))
        I('dve', 'tensor_tensor', reads=[('Dh', 0), ('Dh', 1)], writes=[('Dh', 0)], out=self.Dh[:, 0, :], in0=self.Dh[:, 0, :],
          in1=self.Dh[:, 1, :], op=ALU.add)
        I('act', 'activation', reads=['consts', 'dec'], writes=[('Dh', 2)], out=self.Dh[:, 2, :], in_=cst[:, C_ROWF:C_ROWF + 128],
          func=AF.Exp, scale=self.LGc(0, hd))
        I('act', 'activation', reads=['consts', 'dec'], writes=[('Dh', 3)], out=self.Dh[:, 3, :], in_=cst[:, C_ROWB:C_ROWB + 128],
          func=AF.Exp, scale=self.LGc(1, hd))

    def mix_down(self, g):
        ab = g % 2
        for db in range(4):
            sl = self.wtile(('rowblk4', 'mix_w_out', 0, g, db))
            for tile in self.main_tiles:
                c0 = tile['c0']
                akeys = [self.akey(ab, c, tile) for c in range(4)]
                for dq in range(4):
                    dk = db * 4 + dq
                    pd = self.psum()
                    self.mm_group(self.ps[pd][:, :],
                                  [(self.ring[:, sl, c * 512 + dq * 128:c * 512 + dq * 128 + 128], self.abuf[:, ab, c, c0:c0 + 512]) for c in range(4)],
                                  reads=[('ring', sl)] + akeys, writes=[('ps', pd)])
                    self.I('dve', 'scalar_tensor_tensor', reads=[('ps', pd), 'tabG', self.xkey(tile, dk)], writes=[self.xkey(tile, dk)],
                           out=self.xap(tile, dk), in0=self.ps[pd][:, :], scalar=self.tabG[:, 0, dk:dk + 1],
                           in1=self.xap(tile, dk), op0=ALU.mult, op1=ALU.add)

    def mixer_main(self):
        I = self.I
        self.dec_setup()
        payg = self.pay1g.ap() if hasattr(self.pay1g, 'ap') else self.pay1g
        pgv = payg.rearrange("(r k h p) v -> r k h p v", r=2, k=4, h=NH, p=128)
        sel0 = self.vecs[:, V_SEL:V_SEL + 1]
        sel1 = self.vecs[:, V_SEL + 1:V_SEL + 2]
        for hd in range(NH):
            g = hd // 4
            c = hd % 4
            ab = g % 2
            sq_ = self.wtile(('colblk', 'mix_w_in', 0, hd))
            sk = self.wtile(('colblk', 'mix_w_in', 0, 8 + hd))
            sv = self.wtile(('colblk', 'mix_w_in', 0, 16 + hd))
            sgt = self.wtile(('colblk', 'mix_w_in', 0, 24 + hd))
            srcs = [(0, 0), (1, 0), (0, 1), (1, 1), (0, 2), (1, 3)]
            for i, (rk, kind) in enumerate(srcs):
                self.P.dma('sp', self.pst[:, i, :], pgv[rk, kind, hd], reads=['pay1g'], writes=[('pst', i)])
            I('dve', 'scalar_tensor_tensor', reads=[('pst', 0), ('pst', 1), 'dec'], writes=[('Sst', 0)], out=self.Sst[:, 0, :],
              in0=self.pst[:, 0, :], scalar=self.CDc(0, hd), in1=self.pst[:, 1, :], op0=ALU.mult, op1=ALU.add)
            I('dve', 'tensor_scalar', reads=[('Sst', 0), 'dec'], writes=[('Sst', 0)], out=self.Sst[:, 0, :], in0=self.Sst[:, 0, :],
              scalar1=self.ALc(0, hd), scalar2=None, op0=ALU.mult)
            I('dve', 'scalar_tensor_tensor', reads=[('Sst', 0), ('pst', 4), 'vecs'], writes=[('Sst', 0)], out=self.Sst[:, 0, :],
              in0=self.pst[:, 4, :], scalar=sel1, in1=self.Sst[:, 0, :], op0=ALU.mult, op1=ALU.add)
            I('dve', 'scalar_tensor_tensor', reads=[('pst', 2), ('pst', 3), 'dec'], writes=[('Sst', 1)], out=self.Sst[:, 1, :],
              in0=self.pst[:, 3, :], scalar=self.CDc(1, hd), in1=self.pst[:, 2, :], op0=ALU.mult, op1=ALU.add)
            I('dve', 'tensor_scalar', reads=[('Sst', 1), 'dec'], writes=[('Sst', 1)], out=self.Sst[:, 1, :], in0=self.Sst[:, 1, :],
              scalar1=self.ALc(1, hd), scalar2=None, op0=ALU.mult)
            I('dve', 'scalar_tensor_tensor', reads=[('Sst', 1), ('pst', 5), 'vecs'], writes=[('Sst', 1)], out=self.Sst[:, 1, :],
              in0=self.pst[:, 5, :], scalar=sel0, in1=self.Sst[:, 1, :], op0=ALU.mult, op1=ALU.add)
            self.head_tables(hd)
            self.proj_rope(sq_, 0)
            self.proj_rope(sk, 1)
            self.v_proj_main(sv)
            I('dve', 'tensor_copy', reads=[('Sst', 0)], writes=[('Sbf', 0, 0)], out=self.Sbf[:, 0, 0, :], in_=self.Sst[:, 0, :])
            I('dve', 'tensor_copy', reads=[('Sst', 1)], writes=[('Sbf', 1, 7)], out=self.Sbf[:, 1, 7, :], in_=self.Sst[:, 1, :])
            loc = self.kv_all(hd)
            for step in range(7):
                for d, n in ((0, step), (1, 7 - step)):
                    pk, c0 = loc[(d, n)]
                    nn = n + 1 if d == 0 else n - 1
                    I('dve', 'scalar_tensor_tensor', reads=[('ps', pk), ('Sst', d), 'dec'], writes=[('Sbf', d, nn)],
                      out=self.Sbf[:, d, nn, :], in0=self.Sst[:, d, :], scalar=self.CDc(d, hd), in1=self.ps[pk][:, c0:c0 + 128],
                      op0=ALU.mult, op1=ALU.add)
                    if step < 6:
                        I('dve', 'scalar_tensor_tensor', reads=[('ps', pk), ('Sst', d), 'dec'], writes=[('Sst', d)],
                          out=self.Sst[:, d, :], in0=self.Sst[:, d, :], scalar=self.CDc(d, hd), in1=self.ps[pk][:, c0:c0 + 128],
                          op0=ALU.mult, op1=ALU.add)
            def scores(n):
                cs = slice(n * 128, (n + 1) * 128)
                psc = self.psum()
                I('pe', 'matmul', reads=[('qk', 0, n // 4), ('qk', 1, n // 4)], writes=[('ps', psc)], out=self.ps[psc][:, 0:128],
                  lhsT=self.qk[:, 1, cs], rhs=self.qk[:, 0, cs], start=True, stop=True)
                return psc
            nxt = scores(0)
            for half in range(2):
                po = self.psum()
                for j in range(4):
                    n = half * 4 + j
                    cs = slice(n * 128, (n + 1) * 128)
                    psc = nxt
                    if n + 1 < 8:
                        nxt = scores(n + 1)
                    r = self.rot('PT', 2)
                    I('dve', 'tensor_tensor', reads=[('ps', psc), ('Dh', 0)], writes=[('PT', r)], out=self.PT[:, r, :],
                      in0=self.ps[psc][:, 0:128], in1=self.Dh[:, 0, :], op=ALU.mult)
                    rq = self.rot('qfb', 2)
                    I('dve', 'tensor_tensor', reads=[('qk', 0, half), ('Dh', 2)], writes=[('qfb', 0, rq)], out=self.qfb[:, 0, rq, :],
                      in0=self.qk[:, 0, cs], in1=self.Dh[:, 2, :], op=ALU.mult)
                    I('dve', 'tensor_tensor', reads=[('qk', 0, half), ('Dh', 3)], writes=[('qfb', 1, rq)], out=self.qfb[:, 1, rq, :],
                      in0=self.qk[:, 0, cs], in1=self.Dh[:, 3, :], op=ALU.mult)
                    self.mm_group(self.ps[po][:, j * 128:(j + 1) * 128],
                                  [(self.vh[:, n, :], self.PT[:, r, :]),
                                   (self.Sbf[:, 0, n, :], self.qfb[:, 0, rq, :]),
                                   (self.Sbf[:, 1, n, :], self.qfb[:, 1, rq, :])],
                                  reads=[('vh', n), ('PT', r), ('Sbf', 0, n), ('Sbf', 1, n), ('qfb', 0, rq), ('qfb', 1, rq)],
                                  writes=[('ps', po)])
                m0, k0 = self.mt(0)
                m1, k1 = self.mt(1)
                m2, k2 = self.mt(2)
                m3, k3 = self.mt(3)
                I('act', 'activation', reads=[('ps', po)], writes=[k0], out=m0[:, 0:512], in_=self.ps[po][:, :], func=AF.Copy)
                pm = self.psum()
                I('pe', 'matmul', reads=[k0, 'onesf'], writes=[('ps', pm)], out=self.ps[pm][:, :], lhsT=self.onesf[:, :],
                  rhs=m0[:, 0:512], start=True, stop=True)
                I('dve', 'tensor_tensor', reads=[k0, ('ps', pm)], writes=[k1], out=m1[:, 0:512], in0=m0[:, 0:512],
                  in1=self.ps[pm][:, :], op=ALU.subtract)
                I('act', 'activation', reads=[k1], writes=[k2], out=m2[:, 0:512], in_=m1[:, 0:512], func=AF.Square)
                pvv = self.psum()
                I('pe', 'matmul', reads=[k2, 'onesf'], writes=[('ps', pvv)], out=self.ps[pvv][:, :], lhsT=self.onesf[:, :],
                  rhs=m2[:, 0:512], start=True, stop=True)
                I('act', 'activation', reads=[('ps', pvv), 'consts'], writes=[k2], out=m2[:, 0:512], in_=self.ps[pvv][:, :],
                  func=AF.Sqrt, bias=self.consts[:, C_GNEPS:C_GNEPS + 1], scale=1.0)
                I('dve', 'reciprocal', reads=[k2], writes=[k2], out=m2[:, 0:512], in_=m2[:, 0:512])
                I('dve', 'tensor_tensor', reads=[k1, k2], writes=[k1], out=m1[:, 0:512], in0=m1[:, 0:512], in1=m2[:, 0:512], op=ALU.mult)
                pg = self.psum()
                c0 = half * 512
                self.mm_group(self.ps[pg][:, :],
                              [(self.ring[:, sgt, kc * 128:(kc + 1) * 128], self.h[:, kc, c0:c0 + 512]) for kc in range(KC)],
                              reads=[('ring', sgt)] + [('h', k, half) for k in range(KC)], writes=[('ps', pg)])
                I('act', 'activation', reads=[('ps', pg)], writes=[k3], out=m3[:, 0:512], in_=self.ps[pg][:, :], func=AF.Silu)
                I('dve', 'tensor_tensor', reads=[k1, k3], writes=[self.akey(ab, c, self.main_tiles[half])],
                  out=self.abuf[:, ab, c, c0:c0 + 512], in0=m1[:, 0:512], in1=m3[:, 0:512], op=ALU.mult)
            if hd == 7:
                self.mix_down(0)
        wcv = self.vecs[:, V_WCONV:V_WCONV + 24].rearrange("p (t j) -> p t j", t=3)
        for j in range(8):
            g = 2 + j // 4
            c = j % 4
            ab = g % 2
            sB = self.wtile(('colblk', 'mix_w_in', 0, 32 + j))
            sC = self.wtile(('colblk', 'mix_w_in', 0, 40 + j))
            sU = self.wtile(('colblk', 'mix_w_in', 0, 48 + j))
            cu = [self.mt(0), self.mt(1)]
            mu, ku = self.mt(2)
            for half in range(2):
                c0 = half * 512
                hk = [('h', k, half) for k in range(KC)]
                pC = self.psum()
                pU = self.psum()
                self.mm_group(self.ps[pC][:, :], [(self.ring[:, sC, kc * 128:(kc + 1) * 128], self.h[:, kc, c0:c0 + 512]) for kc in range(KC)],
                              reads=[('ring', sC)] + hk, writes=[('ps', pC)])
                self.mm_group(self.ps[pU][:, :], [(self.ring[:, sU, kc * 128:(kc + 1) * 128], self.h[:, kc, c0:c0 + 512]) for kc in range(KC)],
                              reads=[('ring', sU)] + hk, writes=[('ps', pU)])
                I('act', 'activation', reads=[('ps', pU)], writes=[ku], out=mu[:, 0:512], in_=self.ps[pU][:, :], func=AF.Copy)
                I('dve', 'tensor_tensor', reads=[('ps', pC), ku], writes=[cu[half][1]], out=cu[half][0][:, 1:513],
                  in0=self.ps[pC][:, :], in1=mu[:, 0:512], op=ALU.mult)
            pH = self.psum()
            hk = [('he', k) for k in range(KC)]
            self.mm_group(self.ps[pH][:, 0:2], [(self.ring[:, sC, kc * 128:(kc + 1) * 128], self.h[:, kc, T:T + 2]) for kc in range(KC)],
                          reads=[('ring', sC)] + hk, writes=[('ps', pH)])
            pH2 = self.psum()
            self.mm_group(self.ps[pH2][:, 0:2], [(self.ring[:, sU, kc * 128:(kc + 1) * 128], self.h[:, kc, T:T + 2]) for kc in range(KC)],
                          reads=[('ring', sU)] + hk, writes=[('ps', pH2)])
            I('act', 'activation', reads=[('ps', pH2)], writes=[ku], out=mu[:, 0:2], in_=self.ps[pH2][:, 0:2], func=AF.Copy)
            I('dve', 'tensor_tensor', reads=[('ps', pH), ku], writes=[ku], out=mu[:, 2:4], in0=self.ps[pH][:, 0:2], in1=mu[:, 0:2], op=ALU.mult)
            I('dve', 'tensor_tensor', reads=[ku, 'vecs'], writes=[ku], out=mu[:, 4:6], in0=mu[:, 2:4],
              in1=self.vecs[:, V_SEL + 2:V_SEL + 4], op=ALU.mult)
            I('dve', 'tensor_copy', reads=[ku], writes=[cu[0][1]], out=cu[0][0][:, 0:1], in_=mu[:, 4:5])
            I('dve', 'tensor_copy', reads=[ku], writes=[cu[1][1]], out=cu[1][0][:, 513:514], in_=mu[:, 5:6])
            I('dve', 'tensor_copy', reads=[cu[1][1]], writes=[cu[0][1]], out=cu[0][0][:, 513:514], in_=cu[1][0][:, 1:2])
            I('dve', 'tensor_copy', reads=[cu[0][1]], writes=[cu[1][1]], out=cu[1][0][:, 0:1], in_=cu[0][0][:, 512:513])
            for half in range(2):
                c0 = half * 512
                mc_, kc_ = self.mt(3)
                src = cu[half][0]
                I('dve', 'tensor_scalar', reads=[cu[half][1], 'vecs'], writes=[kc_], out=mc_[:, 0:512], in0=src[:, 1:513],
                  scalar1=wcv[:, 1, j:j + 1], scalar2=None, op0=ALU.mult)
                I('dve', 'scalar_tensor_tensor', reads=[cu[half][1], kc_, 'vecs'], writes=[kc_], out=mc_[:, 0:512], in0=src[:, 0:512],
                  scalar=wcv[:, 0, j:j + 1], in1=mc_[:, 0:512], op0=ALU.mult, op1=ALU.add)
                I('dve', 'scalar_tensor_tensor', reads=[cu[half][1], kc_, 'vecs'], writes=[kc_], out=mc_[:, 0:512], in0=src[:, 2:514],
                  scalar=wcv[:, 2, j:j + 1], in1=mc_[:, 0:512], op0=ALU.mult, op1=ALU.add)
                pB = self.psum()
                self.mm_group(self.ps[pB][:, :], [(self.ring[:, sB, kc * 128:(kc + 1) * 128], self.h[:, kc, c0:c0 + 512]) for kc in range(KC)],
                              reads=[('ring', sB)] + [('h', k, half) for k in range(KC)], writes=[('ps', pB)])
                I('dve', 'tensor_tensor', reads=[('ps', pB), kc_], writes=[self.akey(ab, c, self.main_tiles[half])],
                  out=self.abuf[:, ab, c, c0:c0 + 512], in0=self.ps[pB][:, :], in1=mc_[:, 0:512], op=ALU.mult)
            if j == 3:
                self.mix_down(1)
            if j == 7:
                self.mix_down(2)
        self.mix_down(3)

    def v_proj_main(self, slot):
        I = self.I
        for grp in range(2):
            ns = [0, 1, 2, 3] if grp == 0 else [4, 5, 6, 7]
            pv = self.psum()
            for j, n in enumerate(ns):
                self.mm_group(self.ps[pv][:, j * 128:(j + 1) * 128],
                              [(self.h[:, kc, n * 128:n * 128 + 128], self.ring[:, slot, kc * 128:(kc + 1) * 128]) for kc in range(KC)],
                              reads=[('ring', slot)] + [('h', k, n // 4) for k in range(KC)], writes=[('ps', pv)])
            for j, n in enumerate(ns):
                I('act', 'activation', reads=[('ps', pv)], writes=[('vh', n)], out=self.vh[:, n, :],
                  in_=self.ps[pv][:, j * 128:(j + 1) * 128], func=AF.Copy)

    def halo_out(self):
        pay = self.pay2.ap() if hasattr(self.pay2, 'ap') else self.pay2
        pv = pay.rearrange("p (k t) -> p k t", k=KC)
        xk = [('x', k, 0) for k in range(KC)] + [('x', k, 1) for k in range(KC)]
        self.P.dma('sp', pv[:, :, 0:8], self.x[:, :, 0:8], reads=xk, writes=['pay2'])
        self.P.dma('sp', pv[:, :, 8:16], self.x[:, :, T - 8:T], reads=xk, writes=['pay2'])

    def pool_mixer(self):
        I = self.I
        payg = self.pay2g.ap() if hasattr(self.pay2g, 'ap') else self.pay2g
        pgv = payg.rearrange("(r p) (k t) -> r p k t", r=2, k=KC)
        xek = [('xe', k) for k in range(KC)]
        self.P.dma('sp', self.xe[:, :, 0:8], pgv[0][:, :, 8:16], reads=['pay2g'], writes=xek)
        self.P.dma('sp', self.xe[:, :, 8:16], pgv[1][:, :, 0:8], reads=['pay2g'], writes=xek)
        halo_tile = dict(kind='xe', c0=0, n=16)
        self.norm_mod(1, 'mix', self.main_tiles + [halo_tile])
        W = 8 + T + 8
        maskL = self.vecs[:, V_SEL + 2:V_SEL + 3]
        maskR = self.vecs[:, V_SEL + 3:V_SEL + 4]
        pf = self.vecs[:, V_PFAC:V_PFAC + 64].rearrange("p (w t) -> p w t", w=4)
        for g in range(4):
            ab = g % 2
            w = 2 << g
            for c in range(4):
                k = 4 * g + c
                hp = self.wide[:, 0, :]
                I('act', 'activation', reads=[('h', k, 0), ('h', k, 1)], writes=[('wide', 0)], out=hp[:, 8:8 + T], in_=self.h[:, k, 0:T], func=AF.Copy)
                I('dve', 'tensor_scalar', reads=[('he', k), 'vecs'], writes=[('wide', 0)], out=hp[:, 0:8], in0=self.h[:, k, T:T + 8],
                  scalar1=maskL, scalar2=None, op0=ALU.mult)
                I('dve', 'tensor_scalar', reads=[('he', k), 'vecs'], writes=[('wide', 0)], out=hp[:, 8 + T:W], in0=self.h[:, k, T + 8:T + 16],
                  scalar1=maskR, scalar2=None, op0=ALU.mult)
                a_, b_ = self.wide[:, 1, :], self.wide[:, 2, :]
                I('dve', 'tensor_tensor', reads=[('wide', 0)], writes=[('wide', 1)], out=a_[:, 1:W], in0=hp[:, 0:W - 1], in1=hp[:, 1:W], op=ALU.add)
                cur, ck, oth, ok = a_, ('wide', 1), b_, ('wide', 2)
                lo, hi, sh = 1, W, 1
                for lvl in range(g):
                    nlo, nhi = lo + sh, hi - sh
                    I('dve', 'tensor_tensor', reads=[ck], writes=[ok], out=oth[:, nlo:nhi], in0=cur[:, nlo - sh:nhi - sh],
                      in1=cur[:, nlo + sh:nhi + sh], op=ALU.add)
                    cur, ck, oth, ok = oth, ok, cur, ck
                    lo, hi, sh = nlo, nhi, sh * 2
                tile0 = self.main_tiles[0]
                I('dve', 'scalar_tensor_tensor', reads=[ck, ('wide', 0)], writes=[ok], out=oth[:, 8:8 + T], in0=cur[:, 8:8 + T],
                  scalar=1.0 / w, in1=hp[:, 8:8 + T], op0=ALU.mult, op1=ALU.subtract)
                for (a0, t0) in ((8, 0), (T, 8)):
                    I('dve', 'tensor_tensor', reads=[ck, 'vecs'], writes=[ck], out=cur[:, a0:a0 + 8], in0=cur[:, a0:a0 + 8],
                      in1=pf[:, g, t0:t0 + 8], op=ALU.mult)
                    I('dve', 'tensor_tensor', reads=[ck, ('wide', 0), ok], writes=[ok], out=oth[:, a0:a0 + 8], in0=cur[:, a0:a0 + 8],
                      in1=hp[:, a0:a0 + 8], op=ALU.subtract)
                I('act', 'activation', reads=[ok], writes=[self.akey(ab, c, self.main_tiles[0]), self.akey(ab, c, self.main_tiles[1])],
                  out=self.abuf[:, ab, c, 0:T], in_=oth[:, 8:8 + T], func=AF.Copy)
            sp_ = self.wtile(('pool', g))
            for tile in self.main_tiles:
                c0 = tile['c0']
                akeys = [self.akey(ab, c, tile) for c in range(4)]
                for do in range(4):
                    dk = 4 * g + do
                    pd = self.psum()
                    self.mm_group(self.ps[pd][:, :],
                                  [(self.ring[:, sp_, i * 512 + do * 128:i * 512 + do * 128 + 128], self.abuf[:, ab, i, c0:c0 + 512]) for i in range(4)],
                                  reads=[('ring', sp_)] + akeys, writes=[('ps', pd)])
                    I('dve', 'scalar_tensor_tensor', reads=[('ps', pd), 'tabG', self.xkey(tile, dk)], writes=[self.xkey(tile, dk)],
                      out=self.xap(tile, dk), in0=self.ps[pd][:, :], scalar=self.tabG[:, 0, dk:dk + 1],
                      in1=self.xap(tile, dk), op0=ALU.mult, op1=ALU.add)

    def final_norm(self):
        I = self.I
        toks = []
        for tile in self.main_tiles:
            n = tile['n']
            c0 = tile['c0']
            pb = self.psum()
            psv = self.ps[pb][:, 0:n]
            for k in range(KC):
                b = self.rot('sq', 2)
                I('act', 'activation', reads=[self.xkey(tile, k)], writes=[('sq', b)], out=self.sq[:, b, 0:n], in_=self.xap(tile, k), func=AF.Square)
                I('pe', 'matmul', reads=[('sq', b), 'ones'], writes=[('ps', pb)], out=psv, lhsT=self.ones[:, :], rhs=self.sq[:, b, 0:n],
                  start=(k == 0), stop=(k == KC - 1))
            I('act', 'activation', reads=[('ps', pb), 'consts'], writes=['rstd'], out=self.rstd[:, 0:n], in_=psv, func=AF.Sqrt,
              bias=self.consts[:, C_EPS:C_EPS + 1], scale=1.0 / D)
            I('dve', 'reciprocal', reads=['rstd'], writes=['rstd'], out=self.rstd[:, 0:n], in_=self.rstd[:, 0:n])
            for k in range(KC):
                b = self.rot('f32t', 4)
                I('dve', 'scalar_tensor_tensor', reads=[self.xkey(tile, k), 'rstd', 'vecs'], writes=[('f32t', b)],
                  out=self.f32t[:, b, 0:n], in0=self.xap(tile, k), scalar=self.vecs[:, V_FNORM + k:V_FNORM + k + 1],
                  in1=self.rstd[:, 0:n], op0=ALU.mult, op1=ALU.mult)
                toks.append(self.P.dma('sp', self.out_d[:, k, c0:c0 + n], self.f32t[:, b, 0:n], reads=[('f32t', b)], writes=[('out', k, c0)]))
        for t in toks:
            self.P.finish('sp', t)


V_CC = 0
V_BMOD = V_CC + 32
V_NF1 = V_BMOD + 288
V_NMIX = V_NF1 + 32
V_NF2 = V_NMIX + 32
V_PSCALE = V_NF2 + 32
V_FNORM = V_PSCALE + 16
V_WCONV = V_FNORM + 16
V_DEC = V_WCONV + 24
V_SEL = V_DEC + 16
V_PFAC = V_SEL + 4
NV = V_PFAC + 64
C_EPS = 0
C_GNEPS = 1
C_ONE = 2
C_LNK = 3
C_127MP = 4
C_P = 5
C_RDF = 8
C_RDB = C_RDF + 128
C_ROWF = C_RDB + 128
C_ROWB = C_ROWF + 128
C_PSW = C_ROWB + 128
C_ID = C_PSW + 128
NCONST = C_ID + 128
PAIRS = [[0, 1], [2, 3], [4, 5], [6, 7]]


def _fm(a):
    n = a.shape[0]
    return np.ascontiguousarray(a.reshape(n, KC, 128).transpose(2, 1, 0))


def _vec16(v):
    return np.ascontiguousarray(v.reshape(-1, 128).T)


def rope_tables(s):
    t = np.arange(T) + 1024 * s
    rows = (t // 64).astype(np.float32)
    cols = (t % 64).astype(np.float32)
    quarter = 32
    inv = (np.float32(10000.0) ** (-np.arange(quarter, dtype=np.float32) / quarter)).astype(np.float32)
    ang = np.concatenate([rows[:, None] * inv, cols[:, None] * inv], axis=-1).astype(np.float32)
    cos = np.cos(ang).astype(np.float32).T
    sin = np.sin(ang).astype(np.float32).T
    out = np.zeros((128, 2, T), np.float32)
    out[:64, 0] = cos
    out[64:, 0] = cos
    out[:64, 1] = -sin
    out[64:, 1] = sin
    return out


def core_inputs(inp, b, s):
    x = inp['x']
    d = {}
    d['xin'] = _fm(x[b, 1024 * s:1024 * s + 1024])
    xe = np.zeros((NEXT, D), np.float32)
    if s == 1:
        xe[0] = x[b, 1023]
    if s == 0:
        xe[1] = x[b, 1024]
    xe[2:] = inp['ctx'][b, 128 * s:128 * s + 128]
    d['xein'] = _fm(xe)
    v = np.zeros((128, NV), np.float32)
    cc = np.stack([_vec16(inp['c'][b]), _vec16(inp['c_ctx'])], axis=-1)
    v[:, V_CC:V_CC + 32] = cc.reshape(128, 32)
    for li in range(2):
        v[:, V_BMOD + li * 144:V_BMOD + (li + 1) * 144] = _vec16(inp['b_mod'][li])
        v[:, V_NF1 + li * 16:V_NF1 + (li + 1) * 16] = _vec16(inp['norm_ffn1'][li])
        v[:, V_NMIX + li * 16:V_NMIX + (li + 1) * 16] = _vec16(inp['norm_mix'][li])
        v[:, V_NF2 + li * 16:V_NF2 + (li + 1) * 16] = _vec16(inp['norm_ffn2'][li])
    v[:, V_PSCALE:V_PSCALE + 16] = _vec16(inp['pool_scale'][0])
    v[:, V_FNORM:V_FNORM + 16] = _vec16(inp['final_norm'])
    wc = inp['mix_w_conv'][0]
    v[:, V_WCONV:V_WCONV + 24] = np.stack([_vec16(wc[t]) for t in range(3)], axis=1).reshape(128, 24)
    v[:, V_DEC:V_DEC + 8] = inp['ret_decay_fwd'][0][None, :]
    v[:, V_DEC + 8:V_DEC + 16] = inp['ret_decay_bwd'][0][None, :]
    v[:, V_SEL + 0] = 1.0 - s
    v[:, V_SEL + 1] = float(s)
    v[:, V_SEL + 2] = float(s == 1)
    v[:, V_SEL + 3] = float(s == 0)
    for gi, w in enumerate((2, 4, 8, 16)):
        for e in range(16):
            t = (e if e < 8 else T - 16 + e) + 1024 * s
            lo = min(max(t - w // 2, 0), 2048)
            hi = min(max(t + (w - w // 2), 0), 2048)
            v[:, V_PFAC + gi * 16 + e] = 1.0 / float(hi - lo)
    d['vecs'] = v
    d['consts'] = make_consts()
    d['rope'] = rope_tables(s)
    return d


def make_consts():
    c = np.zeros((128, NCONST), np.float32)
    c[:, C_EPS] = EPS
    c[:, C_GNEPS] = GN_EPS
    c[:, C_ONE] = 1.0
    c[:, C_LNK] = np.log(np.float32(K_SCALE))
    p = np.arange(128, dtype=np.float32)
    c[:, C_127MP] = 127.0 - p
    c[:, C_P] = p
    m = p[:, None]
    cc = p[None, :]
    BIG = 3.0e5
    c[:, C_RDF:C_RDF + 128] = np.where(cc >= m, cc - m, BIG)
    c[:, C_RDB:C_RDB + 128] = np.where(m >= cc, m - cc, BIG)
    c[:, C_ROWF:C_ROWF + 128] = cc + 1.0
    c[:, C_ROWB:C_ROWB + 128] = 128.0 - cc
    psw = np.zeros((128, 128), np.float32)
    for d in range(128):
        psw[(d + 64) % 128, d] = 1.0
    c[:, C_PSW:C_PSW + 128] = psw
    c[:, C_ID:C_ID + 128] = np.eye(128, dtype=np.float32)
    return c


def build_wts(wplan, inp):
    out = np.empty((len(wplan), 128, 2048), np.float32)
    for i, d in enumerate(wplan):
        kind = d[0]
        if kind == 'colblk':
            _, name, li, j = d
            W = inp[name][li]
            out[i] = W[:, 128 * j:128 * j + 128].reshape(KC, 128, 128).transpose(1, 0, 2).reshape(128, 2048)
        elif kind == 'rowblk':
            _, name, li, j = d
            out[i] = inp[name][li][128 * j:128 * j + 128, :]
        elif kind == 'rowblk4':
            _, name, li, g, db = d
            W = inp[name][li]
            out[i] = W[512 * g:512 * g + 512, 512 * db:512 * db + 512].reshape(4, 128, 512).transpose(1, 0, 2).reshape(128, 2048)
        elif kind == 'pool':
            _, g = d
            out[i] = inp['pool_w'][0][g].reshape(4, 128, 512).transpose(1, 0, 2).reshape(128, 2048)
        else:
            raise ValueError(kind)
    return out


_CACHE = {}


def get_prog(phases, fused):
    key = (tuple(phases), fused)
    if key not in _CACHE:
        b = Builder(list(phases), fused)
        nc = b.build()
        _CACHE[key] = (b, nc)
    return _CACHE[key]


def run_launch(phases, fused, inp, extra=None):
    b, nc = get_prog(phases, fused)
    wts = build_wts(b.wplan, inp)
    in_maps = []
    for core in range(8):
        bb, s = core // 2, core % 2
        d = core_inputs(inp, bb, s)
        d['wts'] = wts
        if extra is not None:
            d.update(extra[core])
        in_maps.append(d)
    res = run_bass_kernel_spmd(nc, in_maps, core_ids=list(range(8)))
    return res.results


def _exchange1(resA):
    ex = []
    for core in range(8):
        p0 = resA[(core // 2) * 2]['pay1']
        p1 = resA[(core // 2) * 2 + 1]['pay1']
        ex.append(np.concatenate([p0, p1], axis=0))
    return ex


def _exchange2(resB):
    ex = []
    for core in range(8):
        p0 = resB[(core // 2) * 2]['pay2']
        p1 = resB[(core // 2) * 2 + 1]['pay2']
        ex.append(np.concatenate([p0, p1], axis=0))
    return ex


FUSED = True


def kernel(**inp):
    inp = {k: np.asarray(v) for k, v in inp.items()}
    if FUSED:
        res = run_launch(['A', 'B', 'C'], True, inp)
    else:
        resA = run_launch(['A'], False, inp)
        ex1 = _exchange1(resA)
        extraB = [{'xin': resA[c]['xout'], 'xein': resA[c]['xeout'], 'pay1g': ex1[c]} for c in range(8)]
        resB = run_launch(['B'], False, inp, extraB)
        ex2 = _exchange2(resB)
        extraC = [{'xin': resB[c]['xout'], 'xein': resB[c]['xeout'], 'pay2g': ex2[c]} for c in range(8)]
        res = run_launch(['C'], False, inp, extraC)
    out = np.empty((4, 2048, D), np.float32)
    for core in range(8):
        b, s = core // 2, core % 2
        o = res[core]['outT']
        out[b, 1024 * s:1024 * s + 1024] = o.transpose(2, 1, 0).reshape(T, D)
    return out
```
